# Optimizing a Trainium2 kernel written in Bass

```python
import jax, jax.numpy as jnp
from jax import lax
import numpy as np

D_MODEL = 2048
BATCH = 4
SEQ = 2048
DEPTH = 4
DEC_BATCH = 128
DEC_SEQ = 8
PAST_LEN = 16384
PAGE_SIZE = 128

N_MIXERS = 3
N_A = (DEPTH + 2) // 3
N_B = (DEPTH + 1) // 3
N_C = DEPTH // 3
CONV_W = 31
D_CONV = D_MODEL
POOL_WINDOWS = (2, 4, 8, 16)
N_POOL_GROUPS = len(POOL_WINDOWS)
POOL_GROUP = D_MODEL // N_POOL_GROUPS
POOL_HIST = max(POOL_WINDOWS) - 1
HGRN_EXPAND = 128
HGRN_HEADS = D_MODEL // HGRN_EXPAND
HGRN_DK = HGRN_EXPAND
HGRN_DV = D_MODEL // HGRN_HEADS
HGRN_KD = HGRN_HEADS * HGRN_DK
HGRN_CHUNK = 64
D_FF = 5632
FFN_CONV_W = 3
NORM_EPS = 1e-6

kernel_name = "hybrid_conv_pool_hgrn2_step"


def rmsnorm(x, g):
    xf = x.astype(jnp.float32)
    y = xf * lax.rsqrt(jnp.mean(xf * xf, axis=-1, keepdims=True) + NORM_EPS) * g.astype(jnp.float32)
    return y.astype(x.dtype)


def layernorm(x, g, b):
    xf = x.astype(jnp.float32)
    mu = jnp.mean(xf, axis=-1, keepdims=True)
    var = jnp.mean(jnp.square(xf - mu), axis=-1, keepdims=True)
    y = (xf - mu) * lax.rsqrt(var + NORM_EPS) * g.astype(jnp.float32) + b.astype(jnp.float32)
    return y.astype(x.dtype)


def causal_dwconv(x, buf, w, b):
    width, ch = w.shape
    xp = jnp.concatenate([buf.astype(x.dtype), x], axis=1)
    y = lax.conv_general_dilated(xp, w.astype(x.dtype)[:, None, :], window_strides=(1,), padding='VALID',
                                 dimension_numbers=('NWC', 'WIO', 'NWC'), feature_group_count=ch)
    return y + b.astype(x.dtype), xp[:, xp.shape[1] - (width - 1):]


def conformer_conv_mixer(h, buf, w_pw1, w_dw, b_dw, ln_g, ln_b, w_pw2):
    u = h @ w_pw1
    a, gate = jnp.split(u, 2, axis=-1)
    v = a * jax.nn.sigmoid(gate)
    c, new_buf = causal_dwconv(v, buf, w_dw, b_dw)
    c = jax.nn.silu(layernorm(c, ln_g, ln_b))
    return c @ w_pw2, new_buf


def pool_mixer(h, buf, start, w_grp, scale):
    B, T, D = h.shape
    hf = h.astype(jnp.float32)
    hp = jnp.concatenate([buf.astype(jnp.float32), hf], axis=1)
    cs = jnp.concatenate([jnp.zeros((B, 1, D), jnp.float32), jnp.cumsum(hp, axis=1)], axis=1)
    pos = start + jnp.arange(T)
    outs = []
    for g, w in enumerate(POOL_WINDOWS):
        sl = slice(g * POOL_GROUP, (g + 1) * POOL_GROUP)
        hi = cs[:, POOL_HIST + 1:POOL_HIST + 1 + T, sl]
        lo = cs[:, POOL_HIST + 1 - w:POOL_HIST + 1 - w + T, sl]
        cnt = jnp.minimum(w, pos + 1).astype(jnp.float32)[None, :, None]
        outs.append((hi - lo) / cnt)
    pooled = (jnp.concatenate(outs, axis=-1) - hf).astype(h.dtype).reshape(B, T, N_POOL_GROUPS, POOL_GROUP)
    y = jnp.einsum('btgc,gcd->btgd', pooled, w_grp).reshape(B, T, D) * scale
    return y, hp[:, hp.shape[1] - POOL_HIST:].astype(h.dtype)


def hgrn2_mixer(h, s0, lb, w_in, g_norm, w_o):
    B, T, _ = h.shape
    proj = h @ w_in
    q, fz, inp, g = jnp.split(proj, [HGRN_KD, 2 * HGRN_KD, 2 * HGRN_KD + D_MODEL], axis=-1)
    f = lb + (1.0 - lb) * jax.nn.sigmoid(fz.astype(jnp.float32))
    logf = jnp.log(f)
    k = 1.0 - f
    q = jax.nn.silu(q.astype(jnp.float32))
    inp = inp.astype(jnp.float32)
    C = min(HGRN_CHUNK, T)
    n = -(-T // C)
    pad = n * C - T

    def chunks(t):
        t = jnp.pad(t, ((0, 0), (0, pad), (0, 0)))
        return t.reshape(B, n, C, HGRN_HEADS, -1).transpose(1, 0, 3, 2, 4)

    mask = jnp.tril(jnp.ones((C, C), bool))[:, :, None]

    def step(S, xs):
        qc, kc, lfc, ic = xs
        G = jnp.cumsum(lfc, axis=2)
        inter = jnp.einsum('bhtk,bhkv->bhtv', qc * jnp.exp(G), S)
        decay = jnp.exp(jnp.where(mask, G[:, :, :, None, :] - G[:, :, None, :, :], -jnp.inf))
        scores = jnp.einsum('bhtk,bhtsk->bhts', qc, decay * kc[:, :, None, :, :])
        o = inter + jnp.einsum('bhts,bhsv->bhtv', scores, ic)
        G_end = G[:, :, -1, :]
        S = jnp.exp(G_end)[..., None] * S + jnp.einsum('bhsk,bhsv->bhkv', kc * jnp.exp(G_end[:, :, None, :] - G), ic)
        return S, o

    S_fin, o = lax.scan(step, s0.astype(jnp.float32), (chunks(q), chunks(k), chunks(logf), chunks(inp)))
    o = o.transpose(1, 0, 3, 2, 4).reshape(B, n * C, HGRN_HEADS, HGRN_DV)[:, :T]
    o = o * lax.rsqrt(jnp.mean(o * o, axis=-1, keepdims=True) + NORM_EPS)
    o = o.reshape(B, T, D_MODEL) * g_norm.astype(jnp.float32) * jax.nn.silu(g.astype(jnp.float32))
    return o.astype(h.dtype) @ w_o, S_fin.astype(s0.dtype)


def conv_ffn(h, buf, w_up, w_dw, b_dw, w_down):
    u = h @ w_up
    c, new_buf = causal_dwconv(u, buf, w_dw, b_dw)
    a, b = jnp.split(c, 2, axis=-1)
    return (jax.nn.silu(a) * b) @ w_down, new_buf


def trunk(x, st_a, st_b, st_c, st_f, start, norm_mix, norm_ffn, norm_final,
          a_w_pw1, a_w_dw, a_b_dw, a_ln_g, a_ln_b, a_w_pw2, b_w_grp, b_scale,
          c_lb, c_w_in, c_g_norm, c_w_o, f_w_up, f_w_dw, f_b_dw, f_w_down):
    lb_all = jnp.cumsum(jax.nn.softmax(c_lb.astype(jnp.float32), axis=0), axis=0)
    lb_all = lb_all - lb_all[0:1]
    new_a, new_b, new_c, new_f = [], [], [], []
    for layer in range(DEPTH):
        kind, j = layer % N_MIXERS, layer // N_MIXERS
        h = rmsnorm(x, norm_mix[layer])
        if kind == 0:
            y, nb = conformer_conv_mixer(h, st_a[j], a_w_pw1[j], a_w_dw[j], a_b_dw[j], a_ln_g[j], a_ln_b[j], a_w_pw2[j])
            new_a.append(nb)
        elif kind == 1:
            y, nb = pool_mixer(h, st_b[j], start, b_w_grp[j], b_scale[j])
            new_b.append(nb)
        else:
            y, nb = hgrn2_mixer(h, st_c[j], lb_all[layer], c_w_in[j], c_g_norm[j], c_w_o[j])
            new_c.append(nb)
        x = x + y
        h = rmsnorm(x, norm_ffn[layer])
        y, nb = conv_ffn(h, st_f[layer], f_w_up[layer], f_w_dw[layer], f_b_dw[layer], f_w_down[layer])
        new_f.append(nb)
        x = x + y
    return (rmsnorm(x, norm_final), jnp.stack(new_a), jnp.stack(new_b), jnp.stack(new_c), jnp.stack(new_f))


def setup_inputs(seed: int = 0) -> dict:
    key = jax.random.key(seed)
    ks = jax.random.split(key, 32)
    nrm = lambda k, s, sc: jax.random.normal(k, s, jnp.float32) * sc
    D = D_MODEL
    return {
        "x_prompt": nrm(ks[0], (BATCH, SEQ, D), 1.0),
        "x_sample": nrm(ks[1], (DEC_BATCH, DEC_SEQ, D), 1.0),
        "state_conv_a": nrm(ks[2], (N_A, DEC_BATCH, CONV_W - 1, D_CONV), 0.5),
        "state_pool": nrm(ks[3], (N_B, DEC_BATCH, POOL_HIST, D), 1.0),
        "state_hgrn": nrm(ks[4], (N_C, DEC_BATCH, HGRN_HEADS, HGRN_DK, HGRN_DV), 0.5),
        "state_ffn_conv": nrm(ks[5], (DEPTH, DEC_BATCH, FFN_CONV_W - 1, 2 * D_FF), 1.0),
        "norm_mix": 1.0 + nrm(ks[6], (DEPTH, D), 0.02),
        "norm_ffn": 1.0 + nrm(ks[7], (DEPTH, D), 0.02),
        "norm_final": 1.0 + nrm(ks[8], (D,), 0.02),
        "a_w_pw1": nrm(ks[9], (N_A, D, 2 * D_CONV), D ** -0.5),
        "a_w_dw": nrm(ks[10], (N_A, CONV_W, D_CONV), CONV_W ** -0.5),
        "a_b_dw": nrm(ks[11], (N_A, D_CONV), 0.02),
        "a_ln_g": 1.0 + nrm(ks[12], (N_A, D_CONV), 0.02),
        "a_ln_b": nrm(ks[13], (N_A, D_CONV), 0.02),
        "a_w_pw2": nrm(ks[14], (N_A, D_CONV, D), D_CONV ** -0.5),
        "b_w_grp": nrm(ks[15], (N_B, N_POOL_GROUPS, POOL_GROUP, POOL_GROUP), POOL_GROUP ** -0.5),
        "b_scale": 1.0 + nrm(ks[16], (N_B, D), 0.02),
        "c_lb": nrm(ks[17], (DEPTH, HGRN_KD), 1.0),
        "c_w_in": nrm(ks[18], (N_C, D, 2 * HGRN_KD + 2 * D), D ** -0.5),
        "c_g_norm": 1.0 + nrm(ks[19], (N_C, D), 0.02),
        "c_w_o": nrm(ks[20], (N_C, D, D), D ** -0.5),
        "f_w_up": nrm(ks[21], (DEPTH, D, 2 * D_FF), D ** -0.5),
        "f_w_dw": nrm(ks[22], (DEPTH, FFN_CONV_W, 2 * D_FF), FFN_CONV_W ** -0.5),
        "f_b_dw": nrm(ks[23], (DEPTH, 2 * D_FF), 0.02),
        "f_w_down": nrm(ks[24], (DEPTH, D_FF, D), D_FF ** -0.5),
    }


def reference(x_prompt, x_sample, state_conv_a, state_pool, state_hgrn, state_ffn_conv,
              norm_mix, norm_ffn, norm_final, a_w_pw1, a_w_dw, a_b_dw, a_ln_g, a_ln_b, a_w_pw2,
              b_w_grp, b_scale, c_lb, c_w_in, c_g_norm, c_w_o, f_w_up, f_w_dw, f_b_dw, f_w_down):
    weights = (norm_mix, norm_ffn, norm_final, a_w_pw1, a_w_dw, a_b_dw, a_ln_g, a_ln_b, a_w_pw2,
               b_w_grp, b_scale, c_lb, c_w_in, c_g_norm, c_w_o, f_w_up, f_w_dw, f_b_dw, f_w_down)
    dt = x_prompt.dtype
    z_a = jnp.zeros((N_A, BATCH, CONV_W - 1, D_CONV), dt)
    z_b = jnp.zeros((N_B, BATCH, POOL_HIST, D_MODEL), dt)
    z_c = jnp.zeros((N_C, BATCH, HGRN_HEADS, HGRN_DK, HGRN_DV), dt)
    z_f = jnp.zeros((DEPTH, BATCH, FFN_CONV_W - 1, 2 * D_FF), dt)
    y_prompt, pa, pb, pc, pf = trunk(x_prompt, z_a, z_b, z_c, z_f, 0, *weights)
    y_sample, sa, sb, sc, sf = trunk(x_sample, state_conv_a, state_pool, state_hgrn, state_ffn_conv, PAST_LEN, *weights)
    return (y_prompt, y_sample, pa, sa, pb, sb, pc, sc, pf, sf)
```

```python
import numpy as np
import ml_dtypes
import concourse.bass as bass
import concourse.mybir as mybir
from concourse.bass_utils import run_bass_kernel_spmd

F32 = mybir.dt.float32
BF16 = mybir.dt.bfloat16
AF = mybir.ActivationFunctionType
ALU = mybir.AluOpType

NCORES = 8
EPS = 1e-6
HO = 30
CW = 31
PH = 15
TS = 8
SLOT = 4096
NSLOT = 3
PAIRS = [[0, 1], [2, 3], [4, 5], [6, 7]]


def make_cfg(D=2048, DFF=5632, P=1024, NS=16, GP=4, CH=64):
    c = dict(D=D, DFF=DFF, P=P, NS=NS, GP=GP, CH=CH)
    c["NDC"] = D // 128
    c["NFP"] = DFF // 128
    c["PT"] = NS * TS
    c["W"] = P + NS * TS
    c["WH"] = HO + c["W"]
    assert c["NFP"] % GP == 0 and c["NDC"] % 4 == 0 and P % CH == 0 and c["PT"] <= 128
    assert (GP * 128) % 1 == 0
    return c


def slab_pair(Wm, c0, c1):
    K = Wm.shape[0]
    a = Wm[:, c0:c0 + 128].reshape(K // 128, 128, 128)
    b = Wm[:, c1:c1 + 128].reshape(K // 128, 128, 128)
    s = np.concatenate([a, b], axis=2)
    return np.ascontiguousarray(s.transpose(1, 0, 2)).reshape(128, -1)


def slab_rows(Wm, r0, nrc, c0, ncol):
    s = Wm[r0:r0 + nrc * 128, c0:c0 + ncol].reshape(nrc, 128, ncol)
    return np.ascontiguousarray(s.transpose(1, 0, 2)).reshape(128, -1)


def build_slabs(cfg, inp):
    D, DFF, NDC, NFP, GP = cfg["D"], cfg["DFF"], cfg["NDC"], cfg["NFP"], cfg["GP"]
    slabs = []
    idx = {}

    def add(key, arr):
        pad = np.zeros((128, SLOT), np.float32)
        pad[:, :arr.shape[1]] = arr
        idx[key] = len(slabs)
        slabs.append(pad)

    def ffn(l):
        wu, wd = inp["f_w_up"][l], inp["f_w_down"][l]
        ng = NFP // GP
        dcols = min(D, SLOT // GP)
        ndh = D // dcols
        for g in range(ng + 1):
            if g < ng:
                for p in range(GP):
                    j = g * GP + p
                    add(("up", l, j), slab_pair(wu, j * 128, DFF + j * 128))
            if g >= 1:
                for h in range(ndh):
                    add(("down", l, g - 1, h), slab_rows(wd, (g - 1) * GP * 128, GP, h * dcols, dcols))

    def mixA(j):
        for dc in range(NDC):
            add(("pw1", j, dc), slab_pair(inp["a_w_pw1"][j], dc * 128, D + dc * 128))
        for dc2 in range(NDC // 2):
            add(("pw2", j, dc2), slab_rows(inp["a_w_pw2"][j], 0, NDC, dc2 * 256, 256))

    def mixB():
        ng = NDC // 4
        for g in range(4):
            add(("grp", g), slab_rows(inp["b_w_grp"][0][g], 0, ng, 0, ng * 128))

    def mixC():
        w = inp["c_w_in"][0]
        for hd in range(NDC):
            add(("cin0", hd), slab_pair(w, hd * 128, D + hd * 128))
            add(("cin1", hd), slab_pair(w, 2 * D + hd * 128, 3 * D + hd * 128))
        for dc2 in range(NDC // 2):
            add(("wo", dc2), slab_rows(inp["c_w_o"][0], 0, NDC, dc2 * 256, 256))

    mixA(0); ffn(0); mixB(); ffn(1); mixC(); ffn(2); mixA(1); ffn(3)
    return np.stack(slabs, 0), idx


def vec_layout(cfg):
    NDC, NFP = cfg["NDC"], cfg["NFP"]
    off = {}
    n = 0
    for name, cnt in [("norm_mix", 4 * NDC), ("norm_ffn", 4 * NDC), ("norm_final", NDC), ("a_b_dw", 2 * NDC),
                      ("a_ln_g", 2 * NDC), ("a_ln_b", 2 * NDC), ("a_w_dw", 2 * CW * NDC), ("b_scale", NDC),
                      ("c_lb", 4 * NDC), ("c_g_norm", NDC), ("f_w_dw", 4 * 3 * 2 * NFP), ("f_b_dw", 4 * 2 * NFP),
                      ("isodd", 1), ("rc", 4 * 16)]:
        off[name] = n
        n += cnt
    return off, n


def colize(v):
    sh = v.shape
    C = sh[-1] // 128
    a = v.reshape(sh[:-1] + (C, 128))
    a = np.moveaxis(a, -1, 0)
    return np.ascontiguousarray(a).reshape(128, -1)


def build_vecs(cfg, inp, core):
    off, n = vec_layout(cfg)
    V = np.zeros((128, n), np.float32)

    def put(name, arr):
        V[:, off[name]:off[name] + arr.shape[1]] = arr
    for k in ["norm_mix", "norm_ffn", "a_b_dw", "a_ln_g", "a_ln_b", "a_w_dw", "b_scale", "c_lb", "c_g_norm",
              "f_w_dw", "f_b_dw"]:
        put(k, colize(np.asarray(inp[k])))
    put("norm_final", colize(np.asarray(inp["norm_final"])[None, :]))
    V[:, off["isodd"]] = float(core % 2)
    start = (core % 2) * cfg["P"]
    rc = np.zeros((4, 16), np.float32)
    for g, w in enumerate((2, 4, 8, 16)):
        for t in range(16):
            rc[g, t] = 1.0 / min(w, start + t + 1)
    V[:, off["rc"]:off["rc"] + 64] = rc.reshape(1, 64)
    return V


def build_consts(cfg):
    W, P, CH = cfg["W"], cfg["P"], cfg["CH"]
    ident = np.eye(128, dtype=np.float32)
    tri = np.triu(np.ones((128, 128), np.float32))
    m0 = np.ones((W,), np.float32)
    m0[0:P:CH] = 0.0
    m0[P::TS] = 0.0
    m0 = np.broadcast_to(m0[None, :], (128, W))
    return np.ascontiguousarray(np.concatenate([ident, tri, m0], axis=1))


class Buf:
    __slots__ = ("lw", "rd", "name")

    def __init__(self, name=""):
        self.lw = None
        self.rd = []
        self.name = name


class Op:
    __slots__ = ("eng", "fn", "deps", "signal", "tick", "idx", "dsem", "dval", "kind")

    def __init__(self, eng, fn, kind):
        self.eng, self.fn, self.kind = eng, fn, kind
        self.deps = []
        self.signal = False
        self.tick = 0
        self.idx = 0
        self.dsem = None
        self.dval = 0


ENGS = ["pe", "act", "dve", "pool", "sp"]
NDSEM = 8


class Sched:
    def __init__(self):
        self.streams = {e: [] for e in ENGS}
        self.ndma = {"pool": 0, "sp": 0}
        self.ncc = 0
        self.dma_ops = {"pool": [], "sp": []}

    def _deps(self, op, r, w):
        deps = op.deps
        for b in r:
            if b.lw is not None:
                deps.append((b.lw, "raw"))
        for b in w:
            if b.lw is not None:
                deps.append((b.lw, "waw"))
            for x in b.rd:
                deps.append((x, "war"))
        for b in r:
            if op.kind == "c":
                b.rd = [x for x in b.rd if not (x.kind == "c" and x.eng == op.eng)]
            b.rd.append(op)
        for b in w:
            b.lw = op
            b.rd = []

    def op(self, eng, fn, r=(), w=()):
        o = Op(eng, fn, "c")
        o.idx = len(self.streams[eng])
        self._deps(o, r, w)
        self.streams[eng].append(o)
        return o

    def dma(self, q, fn, r=(), w=()):
        o = Op(q, fn, "d")
        o.idx = len(self.streams[q])
        n = self.ndma[q]
        self.ndma[q] += 1
        o.dsem = (q, n % NDSEM)
        o.dval = 16 * (n // NDSEM + 1)
        if n >= NDSEM:
            o.deps.append((self.dma_ops[q][n - NDSEM], "raw"))
        self.dma_ops[q].append(o)
        self._deps(o, r, w)
        self.streams[q].append(o)
        return o

    def cc(self, fn, r=(), w=()):
        o = Op("pool", fn, "cc")
        o.idx = len(self.streams["pool"])
        o.dsem = ("cc", self.ncc)
        self.ncc += 1
        o.dval = 1
        self._deps(o, r, w)
        self.streams["pool"].append(o)
        return o

    def barrier(self, bufs):
        b = Buf("barrier")
        lasts = []
        for e in ENGS:
            if self.streams[e]:
                lasts.append(self.streams[e][-1])
        outstanding = []
        for q in ("pool", "sp"):
            outstanding += self.dma_ops[q][-NDSEM:]
        for e in ["pe", "act", "dve", "pool", "sp"]:
            o = Op(e, None, "nop")
            o.idx = len(self.streams[e])
            for x in lasts + outstanding:
                if x is not o:
                    o.deps.append((x, "raw"))
            self.streams[e].append(o)

    def finalize(self):
        for e in ENGS:
            for o in self.streams[e]:
                keep = []
                for (d, kind) in o.deps:
                    if d.kind == "c" or d.kind == "nop":
                        if d.eng == o.eng and o.kind in ("c", "nop"):
                            if d.kind == "nop" or e == "pe" or kind != "raw" or o.idx - d.idx > 3:
                                continue
                        if d.kind == "nop":
                            continue
                        d.signal = True
                    keep.append(d)
                o.deps = keep
        for e in ENGS:
            t = 0
            for o in self.streams[e]:
                if o.signal:
                    t += 1
                    o.tick = t

    def emit(self, nc, sems, dsems, ccsems):
        engobj = {"pe": "tensor", "act": "scalar", "dve": "vector", "pool": "gpsimd", "sp": "sync"}
        with nc.Block() as block:
            for e in ENGS:
                def body(eng, e=e):
                    known = {}
                    for o in self.streams[e]:
                        for d in o.deps:
                            if d.kind == "c":
                                key, val, sem = ("c", d.eng), d.tick, sems[d.eng]
                            elif d.kind == "d":
                                key, val, sem = d.dsem, d.dval, dsems[d.dsem]
                            else:
                                key, val, sem = d.dsem, d.dval, ccsems[d.dsem[1]]
                            if known.get(key, 0) >= val:
                                continue
                            known[key] = val
                            eng.wait_ge(sem, val)
                        if o.fn is None:
                            continue
                        ins = o.fn(eng)
                        if o.kind == "d":
                            ins.then_inc(dsems[o.dsem], 16)
                        elif o.kind == "cc":
                            ins.then_inc(ccsems[o.dsem[1]], 1)
                        elif o.signal:
                            ins.then_inc(sems[e], 1)
                getattr(block, engobj[e])(body)


class Builder:
    def __init__(self, cfg, slab_idx, nslab):
        self.cfg = cfg
        self.sidx = slab_idx
        self.nslab = nslab
        self.S = Sched()
        self.nc = bass.Bass("TRN2", target_bir_lowering=False)
        self.voff, self.nvec = vec_layout(cfg)
        self.next_slab = 0
        self.ncc_tensors = 0
        self.use_scr = False

    def alloc(self, n_f32):
        n_f32 = (n_f32 + 7) // 8 * 8
        if self.use_scr and self.xtop + n_f32 <= self.xlimit:
            a = self.xtop
            self.xtop += n_f32
            return self.arena[:, a:a + n_f32]
        a = self.top
        self.top += n_f32
        assert self.top <= self.arena_words, ("SBUF arena overflow", self.top, self.arena_words)
        return self.arena[:, a:a + n_f32]

    def spill_x(self):
        c = self.cfg
        xsp = self.d["xspill"].ap().rearrange("p (a b) -> p a b", b=c["W"])
        self.spb = Buf("xspill")
        self.dma("sp", xsp, self.xT, list(self.xb), [self.spb])
        self.S.barrier([])
        self.xtop = self.x_start
        self.use_scr = True

    def restore_x(self):
        c = self.cfg
        self.use_scr = False
        self.S.barrier([])
        xsp = self.d["xspill"].ap().rearrange("p (a b) -> p a b", b=c["W"])
        self.dma("sp", self.xT, xsp, [self.spb], list(self.xb))

    def f32(self, *shape):
        n = int(np.prod(shape))
        ap = self.alloc(n)[:, 0:n]
        if len(shape) == 2:
            return ap.rearrange("p (a b) -> p a b", b=shape[1])
        if len(shape) == 3:
            return ap.rearrange("p (a b c) -> p a b c", b=shape[1], c=shape[2])
        return ap

    def bf16(self, *shape):
        n = int(np.prod(shape))
        ap = self.alloc((n + 1) // 2).bitcast(BF16)[:, 0:n]
        if len(shape) == 2:
            return ap.rearrange("p (a b) -> p a b", b=shape[1])
        if len(shape) == 3:
            return ap.rearrange("p (a b c) -> p a b c", b=shape[1], c=shape[2])
        return ap

    def vcol(self, name, i):
        o = self.voff[name] + i
        return self.vecs[:, o:o + 1]

    def mm(self, out, lhsT, rhs, start, stop, r, w):
        return self.S.op("pe", lambda e: e.matmul(out, lhsT, rhs, start=start, stop=stop), r, w)

    def tr(self, out, in_, npart, r, w):
        idn = self.ident[0:npart, 0:npart]
        return self.S.op("pe", lambda e: e.transpose(out, in_, idn), list(r) + [self.cbuf], w)

    def act(self, out, in_, func, r, w, bias=0.0, scale=1.0):
        return self.S.op("act", lambda e: e.activation(out, in_, func, bias=bias, scale=scale), r, w)

    def ts(self, eng, out, in0, s1, s2, op0, op1, r, w):
        if s2 is None:
            return self.S.op(eng, lambda e: e.tensor_scalar(out, in0, s1, None, op0), r, w)
        return self.S.op(eng, lambda e: e.tensor_scalar(out, in0, s1, s2, op0, op1), r, w)

    def tt(self, eng, out, in0, in1, op, r, w):
        return self.S.op(eng, lambda e: e.tensor_tensor(out, in0, in1, op), r, w)

    def stt(self, eng, out, in0, sc, in1, op0, op1, r, w):
        return self.S.op(eng, lambda e: e.scalar_tensor_tensor(out, in0, sc, in1, op0, op1), r, w)

    def copy(self, eng, out, in_, r, w):
        if eng == "act":
            return self.S.op("act", lambda e: e.copy(out, in_), r, w)
        return self.S.op(eng, lambda e: e.tensor_copy(out, in_), r, w)

    def dma(self, q, out, in_, r, w):
        return self.S.dma(q, lambda e: e.dma_start(out=out, in_=in_), r, w)

    def ps(self):
        i = self.ps_next
        self.ps_next = (i + 1) % 8
        return self.psum[i], self.psb[i]

    def wget(self, key):
        sid = self.sidx[key]
        assert sid == self.next_slab, (key, sid, self.next_slab)
        self.next_slab += 1
        k = sid % NSLOT
        slot, sb = self.wslots[k], self.wsb[k]
        src = self.wslab[sid]
        self.dma("pool", slot, src, [], [sb])
        return slot, sb

    def tiles(self, halo):
        c = self.cfg
        out = []
        c0 = HO - halo
        while c0 < c["WH"]:
            w = min(512, c["WH"] - c0)
            out.append((c0, w))
            c0 += w
        return out

    def split(self, c0, w):
        c = self.cfg
        pe = HO + c["P"]
        pp = (c0, min(c0 + w, pe) - c0) if c0 < pe else None
        sp = None
        if c0 + w > pe:
            assert c0 <= pe and c0 + w == c["WH"]
            sp = (pe, c["PT"])
        return pp, sp

    def build(self):
        nc, c = self.nc, self.cfg
        D, NDC, P, NS, PT, W, WH = c["D"], c["NDC"], c["P"], c["NS"], c["PT"], c["W"], c["WH"]
        NFP = c["NFP"]
        dt = lambda n, s, t=F32, k="ExternalInput": nc.dram_tensor(n, s, t, kind=k)
        self.d = d = {}
        d["xp"] = dt("xp", [P, D]); d["xs"] = dt("xs", [PT, D])
        d["st_a"] = dt("st_a", [2, NS, HO, D]); d["st_b"] = dt("st_b", [1, NS, PH, D])
        d["st_c"] = dt("st_c", [1, NS, NDC, 128, 128]); d["st_f"] = dt("st_f", [4, NS, 2, 2 * c["DFF"]])
        d["wslab"] = dt("wslab", [self.nslab, 128, SLOT]); d["vecs"] = dt("vecs", [128, self.nvec])
        d["consts"] = dt("consts", [128, 256 + W])
        o = "ExternalOutput"
        d["yp"] = dt("yp", [P, D], F32, o); d["ys"] = dt("ys", [PT, D], F32, o)
        d["na_p"] = dt("na_p", [2, HO, D], F32, o); d["na_s"] = dt("na_s", [2, NS, HO, D], F32, o)
        d["nb_p"] = dt("nb_p", [1, PH, D], F32, o); d["nb_s"] = dt("nb_s", [1, NS, PH, D], F32, o)
        d["nc_p"] = dt("nc_p", [1, NDC, 128, 128], F32, o); d["nc_s"] = dt("nc_s", [1, NS, NDC, 128, 128], F32, o)
        d["nf_p"] = dt("nf_p", [4, 2, 2 * c["DFF"]], F32, o); d["nf_s"] = dt("nf_s", [4, NS, 2, 2 * c["DFF"]], F32, o)
        d["xspill"] = nc.dram_tensor("xspill", [128, NDC * W], F32)
        self.wslab = d["wslab"].ap()
        self.arena_words = (nc.sbuf_top - nc.sbuf_base) // 4 - 64
        nsem = {}
        import contextlib
        with contextlib.ExitStack() as es:
            self.arena = es.enter_context(nc.sbuf_tensor("arena", [128, self.arena_words], F32))
            self.psum = [es.enter_context(nc.psum_tensor("ps%d" % i, [128, 512], F32)) for i in range(8)]
            self.psb = [Buf("ps%d" % i) for i in range(8)]
            self.ps_next = 0
            sems = {e: es.enter_context(nc.semaphore("s_" + e)) for e in ENGS}
            dsems = {(q, i): es.enter_context(nc.semaphore("d_%s%d" % (q, i))) for q in ("pool", "sp") for i in range(NDSEM)}
            ccsems = [es.enter_context(nc.semaphore("cc%d" % i)) for i in range(32)]
            self.top = 0
            self.vecs = self.alloc(self.nvec)
            cst = self.alloc(256 + W)
            self.ident = cst[:, 0:128]
            self.tri = cst[:, 128:256]
            self.m0 = cst[:, 256:256 + W]
            self.cbuf = Buf("consts")
            self.ones_bf = self.bf16(128)
            self.zero = self.alloc(64)
            self.wslots = [self.bf16(SLOT) for _ in range(NSLOT)]
            self.wsb = [Buf("ws%d" % i) for i in range(NSLOT)]
            self.dma("sp", self.vecs[:, 0:self.nvec], d["vecs"].ap(), [], [self.cbuf])
            self.dma("sp", cst[:, 0:256 + W], d["consts"].ap(), [], [self.cbuf])
            self.S.op("dve", lambda e: e.memset(self.ones_bf, 1.0), [], [self.cbuf])
            self.S.op("dve", lambda e: e.memset(self.zero, 0.0), [], [self.cbuf])
            self.base_top = self.top
            self.program()
            assert self.next_slab == self.nslab, (self.next_slab, self.nslab)
            self.S.barrier([])
            self.S.finalize()
            assert self.S.ncc <= 32
            self.S.emit(nc, sems, dsems, ccsems)
        return nc

    def program(self):
        c = self.cfg
        NDC, P, PT, W, WH, D = c["NDC"], c["P"], c["PT"], c["W"], c["WH"], c["D"]
        self.x_start = self.top
        self.xT = self.f32(NDC, W)
        self.xlimit = self.top
        self.xb = [Buf("x%d" % i) for i in range(NDC)]
        self.hT = self.bf16(NDC, WH)
        self.hb = [Buf("h%d" % i) for i in range(NDC)]
        self.rstd = self.f32(W)
        self.rstd_b = Buf("rstd")
        self.stage_top = self.top
        self.load_x()
        for layer in range(4):
            kind = layer % 3
            self.top = self.stage_top
            self.rmsnorm("norm_mix", layer, halo=(HO if kind == 0 else 0))
            if kind == 0:
                self.mixer_a(layer // 3)
            elif kind == 1:
                self.mixer_b()
            else:
                self.mixer_c()
            self.S.barrier([])
            self.top = self.stage_top
            self.rmsnorm("norm_ffn", layer, halo=2)
            self.ffn(layer)
            self.S.barrier([])
        self.top = self.stage_top
        self.final_out()

    def load_x(self):
        c = self.cfg
        NDC, P, PT, D = c["NDC"], c["P"], c["PT"], c["D"]
        stg = [self.f32(D) for _ in range(2)]
        sb = [Buf("xstg0"), Buf("xstg1")]
        srcs = [(self.d["xp"].ap()[t * 128:(t + 1) * 128, :], 128, t * 128) for t in range(P // 128)]
        srcs.append((self.d["xs"].ap()[:, :], PT, P))
        for i, (src, n, col) in enumerate(srcs):
            s, b = stg[i % 2], sb[i % 2]
            self.dma("sp", s[0:n, :], src, [], [b])
            for q in range(NDC // 4):
                ps, pb = self.ps()
                for k in range(4):
                    dc = q * 4 + k
                    self.tr(ps[:, k * 128:k * 128 + n], s[0:n, dc * 128:(dc + 1) * 128], n, [b], [pb])
                eng = "act" if q % 2 == 0 else "dve"
                for k in range(4):
                    dc = q * 4 + k
                    self.copy(eng, self.xT[:, dc, col:col + n], ps[:, k * 128:k * 128 + n], [pb], [self.xb[dc]])

    def sumsq_to_rstd(self, srcs, width, dim, dst, dstb):
        sq = [self.bf16(512) for _ in range(2)]
        sqb = [Buf("sq0"), Buf("sq1")]
        n = len(srcs)
        c0 = 0
        k = 0
        while c0 < width:
            w = min(512, width - c0)
            ps, pb = self.ps()
            for i, (ap, b) in enumerate(srcs):
                self.act(sq[k % 2][:, 0:w], ap[:, c0:c0 + w], AF.Square, [b], [sqb[k % 2]])
                self.mm(ps[:, 0:w], self.ones_bf, sq[k % 2][:, 0:w], i == 0, i == n - 1, [sqb[k % 2], self.cbuf], [pb])
                k += 1
            self.act(dst[:, c0:c0 + w], ps[:, 0:w], AF.Sqrt, [pb], [dstb], bias=EPS, scale=1.0 / dim)
            c0 += w
        self.S.op("dve", lambda e: e.reciprocal(dst[:, 0:width], dst[:, 0:width]), [dstb], [dstb])

    def rmsnorm(self, gname, layer, halo=0):
        c = self.cfg
        NDC, W, D, P = c["NDC"], c["W"], c["D"], c["P"]
        self.sumsq_to_rstd([(self.xT[:, dc, :], self.xb[dc]) for dc in range(NDC)], W, D, self.rstd, self.rstd_b)
        dst, dstb = self.hT, self.hb
        st = None
        if halo:
            tail = self.bf16(NDC, halo)
            tlb = [Buf("rt%d" % i) for i in range(NDC)]
            for dc in range(NDC):
                self.stt("dve", tail[:, dc, :], self.xT[:, dc, P - halo:P], self.vcol(gname, layer * NDC + dc), self.rstd[:, P - halo:P],
                         ALU.mult, ALU.mult, [self.xb[dc], self.rstd_b, self.cbuf], [tlb[dc]])
            st = self.handoff_start(tail, tlb, halo, halo, BF16)
        for dc in range(NDC):
            g = self.vcol(gname, layer * NDC + dc)
            self.stt("dve", dst[:, dc, HO:HO + W], self.xT[:, dc, :], g, self.rstd[:, 0:W], ALU.mult, ALU.mult,
                     [self.xb[dc], self.rstd_b, self.cbuf], [dstb[dc]])
        if st is not None:
            self.handoff_finish(st, self.hT, self.hb, HO - halo)

    def handoff_start(self, src, srcb, col_end, h, dtype):
        c = self.cfg
        NDC = c["NDC"]
        nc = self.nc
        k = self.ncc_tensors
        self.ncc_tensors += 1
        snd = nc.dram_tensor("hs%d" % k, [128, NDC * h], dtype)
        rcv = nc.dram_tensor("hr%d" % k, [256, NDC * h], dtype)
        sb, rb = Buf("snd"), Buf("rcv")
        self.dma("pool", snd.ap().rearrange("p (a b) -> p a b", b=h), src[:, :, col_end - h:col_end], list(srcb), [sb])
        self.S.cc(lambda e: e.collective_compute("AllGather", ALU.bypass, replica_groups=PAIRS, ins=[snd.ap()], outs=[rcv.ap()]),
                  [sb], [rb])
        tmp = self.f32(NDC, h) if dtype == F32 else self.bf16(NDC, h)
        tb = Buf("hrtmp")
        self.dma("sp", tmp, rcv.ap()[0:128, :].rearrange("p (a b) -> p a b", b=h), [rb], [tb])
        return (tmp, tb, h)

    def handoff_finish(self, st, dst, dstb, dst_col):
        tmp, tb, h = st
        self.ts("dve", dst[:, :, dst_col:dst_col + h], tmp, self.vcol("isodd", 0), None, ALU.mult, None,
                [tb, self.cbuf], list(dstb))

    def handoff(self, src, srcb, col_end, h, dst, dstb, dst_col, dtype):
        st = self.handoff_start(src, srcb, col_end, h, dtype)
        self.handoff_finish(st, dst, dstb, dst_col)

    def ffn(self, l):
        c = self.cfg
        NDC, P, PT, W, WH, NS, NFP, GP, DFF, D = (c[k] for k in ["NDC", "P", "PT", "W", "WH", "NS", "NFP", "GP", "DFF", "D"])
        NG = NFP // GP
        UW = 2 + P + NS * 10
        tiles = self.tiles(2)
        own_tiles = self.tiles(0)
        ub1 = self.f32(2, UW); ubb1 = Buf("ub0")
        ubuf = [ub1, ub1]
        ubb = [ubb1, ubb1]
        cb1 = self.f32(2, W); cbb1 = Buf("cb0")
        cbuf_ = [cb1, cb1]
        cbb = [cbb1, cbb1]
        gT = [self.bf16(GP, W) for _ in range(2)]
        gTb = [[Buf("g%d_%d" % (i, p)) for p in range(GP)] for i in range(2)]
        ust = [self.f32(2 * GP, 34) for _ in range(2)]
        ustb = [Buf("ust0"), Buf("ust1")]
        os1 = self.f32(2, GP * 128); osb1 = Buf("os0")
        ostg = [os1, os1]
        ostb = [osb1, osb1]
        hsb_sb = [self.f32(GP * 4 * NS) for _ in range(2)]
        hsb_b = [Buf("hsb0"), Buf("hsb1")]
        ss1 = self.f32(2, GP * 128); ssb1 = Buf("ss0")
        sstg = [ss1, ss1]
        sstb = [ssb1, ssb1]
        stf = self.d["st_f"].ap()
        nfs = self.d["nf_s"].ap()
        nfp = self.d["nf_p"].ap()
        dcols = min(D, SLOT // GP)
        ndh = D // dcols
        pend_down = None
        pi = 0
        for g in range(NG + 1):
            if g < NG:
                gi = g % 2
                ss, ssb = sstg[gi], sstb[gi]
                for k in range(2):
                    src = stf[l][:, :, k * DFF + g * GP * 128:k * DFF + (g + 1) * GP * 128].rearrange("s r c -> (s r) c")
                    self.dma("sp", ss[0:2 * NS, k, :], src, [], [ssb])
                hps, hpb = hsb_sb[gi], hsb_b[gi]

                def hist_block():
                    hps_, hpb_ = self.ps()
                    for p_ in range(GP):
                        for k_ in range(2):
                            q_ = p_ * 2 + k_
                            self.tr(hps_[:, q_ * 2 * NS:(q_ + 1) * 2 * NS], ss[0:2 * NS, k_, p_ * 128:(p_ + 1) * 128], 2 * NS, [ssb], [hpb_])
                    self.copy("act", hps[:, 0:GP * 4 * NS], hps_[:, 0:GP * 4 * NS], [hpb_], [hpb])
                for p in range(GP):
                    j = g * GP + p
                    ui = pi % 2
                    pi += 1
                    ub, ubf = ubuf[ui], ubb[ui]
                    slab, sb = self.wget(("up", l, j))
                    sv = slab.rearrange("p (a b) -> p a b", b=256)
                    for k in range(2):
                        for (c0, w) in tiles:
                            ps, pb = self.ps()
                            for dc in range(NDC):
                                self.mm(ps[:, 0:w], sv[:, dc, k * 128:(k + 1) * 128], self.hT[:, dc, c0:c0 + w],
                                        dc == 0, dc == NDC - 1, [sb, self.hb[dc]], [pb])
                            pp, sp = self.split(c0, w)
                            if pp:
                                self.copy("act", ub[:, k, pp[0] - (HO - 2):pp[0] - (HO - 2) + pp[1]], ps[:, 0:pp[1]], [pb], [ubf])
                            if sp:
                                o0 = sp[0] - c0
                                self.copy("act", ub[:, k, 2 + P:2 + P + NS * 10].rearrange("p (s t) -> p s t", t=10)[:, :, 2:10],
                                          ps[:, o0:o0 + PT].rearrange("p (s t) -> p s t", t=TS), [pb], [ubf])
                    if p == 0:
                        hist_block()
                    for k in range(2):
                        q = p * 2 + k
                        self.copy("act", ub[:, k, 2 + P:2 + P + NS * 10].rearrange("p (s t) -> p s t", t=10)[:, :, 0:2],
                                  hps[:, q * 2 * NS:(q + 1) * 2 * NS].rearrange("p (s r) -> p s r", r=2), [hpb], [ubf])
                    cb, cbf = cbuf_[ui], cbb[ui]
                    for k in range(2):
                        ch = k * NFP + j
                        wv = lambda t: self.vcol("f_w_dw", (l * 3 + t) * 2 * NFP + ch)
                        bv = self.vcol("f_b_dw", l * 2 * NFP + ch)
                        for seg in range(2):
                            if seg == 0:
                                src = lambda t: ub[:, k, t:t + P]
                                dst = cb[:, k, 0:P]
                            else:
                                sview = ub[:, k, 2 + P:2 + P + NS * 10].rearrange("p (s t) -> p s t", t=10)
                                src = lambda t: sview[:, :, t:t + TS]
                                dst = cb[:, k, P:W].rearrange("p (s t) -> p s t", t=TS)
                            self.ts("dve", dst, src(0), wv(0), bv, ALU.mult, ALU.add, [ubf, self.cbuf], [cbf])
                            self.stt("dve", dst, src(1), wv(1), dst, ALU.mult, ALU.add, [ubf, cbf, self.cbuf], [cbf])
                            self.stt("dve", dst, src(2), wv(2), dst, ALU.mult, ALU.add, [ubf, cbf, self.cbuf], [cbf])
                    self.act(cb[:, 0, :], cb[:, 0, :], AF.Silu, [cbf], [cbf])
                    self.tt("dve", gT[gi][:, p, :], cb[:, 0, :], cb[:, 1, :], ALU.mult, [cbf], [gTb[gi][p]])
                    us, usb = ust[gi], ustb[gi]
                    for k in range(2):
                        self.copy("act", us[:, p * 2 + k, 0:2], ub[:, k, P:P + 2], [ubf], [usb])
                        self.copy("act", us[:, p * 2 + k, 2:2 + 2 * NS].rearrange("p (s r) -> p s r", r=2),
                                  ub[:, k, 2 + P:2 + P + NS * 10].rearrange("p (s t) -> p s t", t=10)[:, :, 8:10], [ubf], [usb])
                us, usb = ust[gi], ustb[gi]
                og, ogb = ostg[gi], ostb[gi]
                n34 = 2 + 2 * NS
                for k in range(2):
                    ps, pb = self.ps()
                    for p in range(GP):
                        self.tr(ps[0:n34, p * 128:(p + 1) * 128], us[:, p * 2 + k, 0:n34], 128, [usb], [pb])
                    self.copy("act", og[0:n34, k, :], ps[0:n34, 0:GP * 128], [pb], [ogb])
                    col = k * DFF + g * GP * 128
                    self.dma("sp", nfp[l][:, col:col + GP * 128], og[0:2, k, :], [ogb], [])
                    self.dma("sp", nfs[l][:, :, col:col + GP * 128].rearrange("s r c -> (s r) c"), og[2:n34, k, :], [ogb], [])
            if pend_down is not None:
                pg = pend_down
                pgi = pg % 2
                for h in range(ndh):
                    slab, sb = self.wget(("down", l, pg, h))
                    sv = slab.rearrange("p (a b) -> p a b", b=dcols)
                    for dl in range(dcols // 128):
                        dc = h * (dcols // 128) + dl
                        for (c0, w) in own_tiles:
                            ps, pb = self.ps()
                            for p in range(GP):
                                self.mm(ps[:, 0:w], sv[:, p, dl * 128:(dl + 1) * 128], gT[pgi][:, p, c0 - HO:c0 - HO + w],
                                        p == 0, p == GP - 1, [sb, gTb[pgi][p]], [pb])
                            self.tt("dve", self.xT[:, dc, c0 - HO:c0 - HO + w], self.xT[:, dc, c0 - HO:c0 - HO + w], ps[:, 0:w],
                                    ALU.add, [pb, self.xb[dc]], [self.xb[dc]])
            pend_down = g if g < NG else None

    def stage_staging(self):
        self.rstg = self.f32(self.cfg["D"])
        self.rstgb = Buf("rowstage")

    def rows_out(self, srcf, n, dst_rows_fn):
        c = self.cfg
        NDC, D = c["NDC"], c["D"]
        st, stb = self.rstg, self.rstgb
        for q in range(NDC // 4):
            ps, pb = self.ps()
            for k in range(4):
                dc = q * 4 + k
                ap, bufs = srcf(dc)
                self.tr(ps[0:n, k * 128:(k + 1) * 128], ap, 128, bufs, [pb])
            self.copy("act", st[0:n, q * 512:(q + 1) * 512], ps[0:n, 0:512], [pb], [stb])
        dst_rows_fn(st, stb)

    def rows_in(self, src_rows, n, dst_fn):
        c = self.cfg
        NDC, D = c["NDC"], c["D"]
        st, stb = self.rstg, self.rstgb
        self.dma("sp", st[0:n, :], src_rows, [], [stb])
        for q in range(NDC // 4):
            ps, pb = self.ps()
            for k in range(4):
                dc = q * 4 + k
                self.tr(ps[:, k * 128:k * 128 + n], st[0:n, dc * 128:(dc + 1) * 128], n, [stb], [pb])
            for k in range(4):
                dst_fn(q * 4 + k, ps[:, k * 128:k * 128 + n], pb)

    def mixer_a(self, ja):
        c = self.cfg
        NDC, P, PT, W, WH, NS, D = (c[k] for k in ["NDC", "P", "PT", "W", "WH", "NS", "D"])
        SW = HO + TS
        VW = HO + P + NS * SW
        self.stage_staging()
        self.spill_x()
        vbuf = self.bf16(NDC, VW)
        vb = [Buf("v%d" % i) for i in range(NDC)]
        vf = self.f32(NDC, HO + PT)
        vfb = [Buf("vf%d" % i) for i in range(NDC)]
        sig = [self.f32(512) for _ in range(2)]
        sgb = [Buf("sig0"), Buf("sig1")]
        self.use_scr = False
        sta, nas, nap = self.d["st_a"].ap(), self.d["na_s"].ap(), self.d["na_p"].ap()
        for s0 in range(0, NS, 4):
            ns = min(4, NS - s0)

            def dst_fn(dc, psap, pb, s0=s0, ns=ns):
                self.copy("act", vbuf[:, dc, HO + P + s0 * SW:HO + P + (s0 + ns) * SW].rearrange("p (s t) -> p s t", t=SW)[:, :, 0:HO],
                          psap.rearrange("p (s t) -> p s t", t=HO), [pb], [vb[dc]])
            self.rows_in(sta[ja][s0:s0 + ns].rearrange("s r c -> (s r) c"), ns * HO, dst_fn)
        for s in range(NS):
            self.dma("sp", nas[ja][s, 0:HO - TS, :], sta[ja][s, TS:HO, :], [], [])
        tiles = self.tiles(HO)
        si = 0
        for dc in range(NDC):
            slab, sb = self.wget(("pw1", ja, dc))
            sv = slab.rearrange("p (a b) -> p a b", b=256)
            for (c0, w) in tiles:
                pa, pab = self.ps()
                pg, pgb = self.ps()
                for kdc in range(NDC):
                    self.mm(pa[:, 0:w], sv[:, kdc, 0:128], self.hT[:, kdc, c0:c0 + w], kdc == 0, kdc == NDC - 1, [sb, self.hb[kdc]], [pab])
                for kdc in range(NDC):
                    self.mm(pg[:, 0:w], sv[:, kdc, 128:256], self.hT[:, kdc, c0:c0 + w], kdc == 0, kdc == NDC - 1, [sb, self.hb[kdc]], [pgb])
                sg, sgbuf = sig[si % 2], sgb[si % 2]
                si += 1
                self.act(sg[:, 0:w], pg[:, 0:w], AF.Sigmoid, [pgb], [sgbuf])
                pp, sp = self.split(c0, w)
                if pp:
                    self.tt("dve", vbuf[:, dc, pp[0]:pp[0] + pp[1]], pa[:, 0:pp[1]], sg[:, 0:pp[1]], ALU.mult, [pab, sgbuf], [vb[dc]])
                    t0 = max(pp[0], P)
                    t1 = pp[0] + pp[1]
                    if t1 > t0:
                        self.tt("dve", vf[:, dc, t0 - P:t1 - P], pa[:, t0 - c0:t1 - c0], sg[:, t0 - c0:t1 - c0], ALU.mult, [pab, sgbuf], [vfb[dc]])
                if sp:
                    o0 = sp[0] - c0
                    self.tt("dve", vbuf[:, dc, HO + P:VW].rearrange("p (s t) -> p s t", t=SW)[:, :, HO:SW],
                            pa[:, o0:o0 + PT].rearrange("p (s t) -> p s t", t=TS), sg[:, o0:o0 + PT].rearrange("p (s t) -> p s t", t=TS),
                            ALU.mult, [pab, sgbuf], [vb[dc]])
                    self.tt("dve", vf[:, dc, HO:HO + PT], pa[:, o0:o0 + PT], sg[:, o0:o0 + PT], ALU.mult, [pab, sgbuf], [vfb[dc]])
        self.rows_out(lambda dc: (vf[:, dc, 0:HO], [vfb[dc]]), HO,
                      lambda st, stb: self.dma("sp", nap[ja], st[0:HO, :], [stb], []))
        def samp_out(st, stb):
            for s in range(NS):
                self.dma("sp", nas[ja][s, HO - TS:HO, :], st[s * TS:(s + 1) * TS, :], [stb], [])
        self.rows_out(lambda dc: (vf[:, dc, HO:HO + PT], [vfb[dc]]), PT, samp_out)
        top_conv = self.top
        diag = [self.bf16(CW, 128) for _ in range(2)]
        dgb = [Buf("dg0"), Buf("dg1")]
        acc = [self.f32(W) for _ in range(2)]
        accb = [Buf("acc0"), Buf("acc1")]
        own = self.tiles(0)
        cT, cb = self.hT, self.hb
        for dc in range(NDC):
            dg, dgbuf = diag[dc % 2], dgb[dc % 2]
            for j in range(CW):
                if j % 3 != 0:
                    self.act(dg[:, j, :], self.ident, AF.Copy, [self.cbuf], [dgbuf], scale=self.vcol("a_w_dw", (ja * CW + j) * NDC + dc))
            ac, acb_ = acc[dc % 2], accb[dc % 2]
            wcol = lambda j: self.vcol("a_w_dw", (ja * CW + j) * NDC + dc)
            sview = vbuf[:, dc, HO + P:VW].rearrange("p (s t) -> p s t", t=SW)
            acs = ac[:, P:W].rearrange("p (s t) -> p s t", t=TS)
            odd = list(range(0, CW, 3))
            for n_, j in enumerate(odd):
                if n_ == 0:
                    self.ts("dve", ac[:, 0:P], vbuf[:, dc, j:j + P], wcol(j), self.vcol("a_b_dw", ja * NDC + dc), ALU.mult, ALU.add,
                            [vb[dc], self.cbuf], [acb_])
                    self.ts("dve", acs, sview[:, :, j:j + TS], wcol(j), self.vcol("a_b_dw", ja * NDC + dc), ALU.mult, ALU.add,
                            [vb[dc], self.cbuf], [acb_])
                else:
                    self.stt("dve", ac[:, 0:P], vbuf[:, dc, j:j + P], wcol(j), ac[:, 0:P], ALU.mult, ALU.add, [vb[dc], acb_, self.cbuf], [acb_])
                    self.stt("dve", acs, sview[:, :, j:j + TS], wcol(j), acs, ALU.mult, ALU.add, [vb[dc], acb_, self.cbuf], [acb_])
            even = [j for j in range(CW) if j % 3 != 0]
            for (c0, w) in own:
                ps, pb = self.ps()
                pp, sp = self.split(c0, w)
                if pp:
                    for n_, j in enumerate(even):
                        a0 = pp[0] - HO + j
                        self.mm(ps[:, 0:pp[1]], dg[:, j, :], vbuf[:, dc, a0:a0 + pp[1]], n_ == 0, n_ == len(even) - 1, [dgbuf, vb[dc]], [pb])
                if sp:
                    o0 = sp[0] - c0
                    for n_, j in enumerate(even):
                        self.mm(ps[:, o0:o0 + PT].rearrange("p (s t) -> p s t", t=TS), dg[:, j, :], sview[:, :, j:j + TS],
                                n_ == 0, n_ == len(even) - 1, [dgbuf, vb[dc]], [pb])
                self.tt("dve", cT[:, dc, c0:c0 + w], ps[:, 0:w], ac[:, c0 - HO:c0 - HO + w], ALU.add, [pb, acb_], [cb[dc]])
        self.restore_x()
        self.top = top_conv
        mu = self.f32(W); mub = Buf("mu")
        rs = self.f32(W); rsb = Buf("rs")
        sq = [self.bf16(W) for _ in range(2)]; sqb = [Buf("lsq0"), Buf("lsq1")]
        tl = [(c0, w) + self.ps() + self.ps() for (c0, w) in own]
        for dc in range(NDC):
            self.act(sq[dc % 2], cT[:, dc, HO:HO + W], AF.Square, [cb[dc]], [sqb[dc % 2]])
            for (c0, w, p1, p1b, p2, p2b) in tl:
                self.mm(p1[:, 0:w], self.ones_bf, cT[:, dc, c0:c0 + w], dc == 0, dc == NDC - 1, [cb[dc], self.cbuf], [p1b])
                self.mm(p2[:, 0:w], self.ones_bf, sq[dc % 2][:, c0 - HO:c0 - HO + w], dc == 0, dc == NDC - 1, [sqb[dc % 2], self.cbuf], [p2b])
        for (c0, w, p1, p1b, p2, p2b) in tl:
            a, b_ = c0 - HO, c0 - HO + w
            self.act(mu[:, a:b_], p1[:, 0:w], AF.Copy, [p1b], [mub], scale=1.0 / D)
            self.tt("dve", rs[:, a:b_], mu[:, a:b_], mu[:, a:b_], ALU.mult, [mub], [rsb])
            self.stt("dve", rs[:, a:b_], p2[:, 0:w], 1.0 / D, rs[:, a:b_], ALU.mult, ALU.subtract, [p2b, rsb], [rsb])
        self.act(rs[:, 0:W], rs[:, 0:W], AF.Sqrt, [rsb], [rsb], bias=EPS)
        self.S.op("dve", lambda e: e.reciprocal(rs[:, 0:W], rs[:, 0:W]), [rsb], [rsb])
        tmp = [self.f32(W) for _ in range(2)]; tmb = [Buf("lt0"), Buf("lt1")]
        for dc in range(NDC):
            t, tb = tmp[dc % 2], tmb[dc % 2]
            self.tt("dve", t[:, 0:W], cT[:, dc, HO:HO + W], mu[:, 0:W], ALU.subtract, [cb[dc], mub], [tb])
            self.tt("dve", t[:, 0:W], t[:, 0:W], rs[:, 0:W], ALU.mult, [tb, rsb], [tb])
            self.act(cT[:, dc, HO:HO + W], t[:, 0:W], AF.Silu, [tb, self.cbuf], [cb[dc]],
                     bias=self.vcol("a_ln_b", ja * NDC + dc), scale=self.vcol("a_ln_g", ja * NDC + dc))
        self.proj_residual("pw2", ja, cT, cb)

    def proj_residual(self, key, j, src, srcb, scale_name=None):
        c = self.cfg
        NDC = c["NDC"]
        own = self.tiles(0)
        for dc2 in range(NDC // 2):
            slab, sb = self.wget((key, j, dc2) if j is not None else (key, dc2))
            sv = slab.rearrange("p (a b) -> p a b", b=256)
            for k in range(2):
                dco = dc2 * 2 + k
                for (c0, w) in own:
                    ps, pb = self.ps()
                    for kdc in range(NDC):
                        self.mm(ps[:, 0:w], sv[:, kdc, k * 128:(k + 1) * 128], src[:, kdc, c0:c0 + w], kdc == 0, kdc == NDC - 1,
                                [sb, srcb[kdc]], [pb])
                    xs = self.xT[:, dco, c0 - HO:c0 - HO + w]
                    self.tt("dve", xs, xs, ps[:, 0:w], ALU.add, [pb, self.xb[dco]], [self.xb[dco]])

    def mixer_b(self):
        c = self.cfg
        NDC, P, PT, W, WH, NS, D = (c[k] for k in ["NDC", "P", "PT", "W", "WH", "NS", "D"])
        NG = NDC // 4
        SW = PH + TS
        LP = PH + P
        LB = LP + NS * SW
        stb_, nbs, nbp = self.d["st_b"].ap(), self.d["nb_s"].ap(), self.d["nb_p"].ap()
        tail = self.f32(NDC, PH); tlb = [Buf("tl%d" % i) for i in range(NDC)]
        for dc in range(NDC):
            self.stt("dve", tail[:, dc, :], self.xT[:, dc, P - PH:P], self.vcol("norm_mix", 1 * NDC + dc), self.rstd[:, P - PH:P],
                     ALU.mult, ALU.mult, [self.xb[dc], self.rstd_b, self.cbuf], [tlb[dc]])
        halo = self.f32(NDC, PH); hlb = [Buf("hl%d" % i) for i in range(NDC)]
        self.handoff(tail, tlb, PH, PH, halo, hlb, 0, F32)
        self.stage_staging()
        self.rows_out(lambda dc: (tail[:, dc, :], [tlb[dc]]), PH, lambda st, sb: self.dma("sp", nbp[0], st[0:PH, :], [sb], []))
        for s in range(NS):
            self.dma("sp", nbs[0][s, 0:PH - TS, :], stb_[0][s, TS:PH, :], [], [])
        hist = self.f32(NDC, NS * PH); hib = [Buf("hi%d" % i) for i in range(NDC)]
        for s0 in range(0, NS, 8):
            ns = min(8, NS - s0)

            def dst_fn(dc, psap, pb, s0=s0, ns=ns):
                self.copy("act", hist[:, dc, s0 * PH:(s0 + ns) * PH], psap, [pb], [hib[dc]])
            self.rows_in(stb_[0][s0:s0 + ns].rearrange("s r c -> (s r) c"), ns * PH, dst_fn)
        hs = self.f32(NDC, PT); hsb = [Buf("hs%d" % i) for i in range(NDC)]
        for dc in range(NDC):
            self.stt("dve", hs[:, dc, :], self.xT[:, dc, P:W], self.vcol("norm_mix", 1 * NDC + dc), self.rstd[:, P:W],
                     ALU.mult, ALU.mult, [self.xb[dc], self.rstd_b, self.cbuf], [hsb[dc]])
        def samp_out(st, sb):
            for s in range(NS):
                self.dma("sp", nbs[0][s, PH - TS:PH, :], st[s * TS:(s + 1) * TS, :], [sb], [])
        self.rows_out(lambda dc: (hs[:, dc, :], [hsb[dc]]), PT, samp_out)
        pooled, plb = self.hT, self.hb
        hf = self.f32(LB); hfb = Buf("hf")
        sA = self.f32(LB); sAb = Buf("sA")
        sB = self.f32(LB); sBb = Buf("sB")
        own = self.tiles(0)
        for g in range(4):
            wdw = (2, 4, 8, 16)[g]
            for dl in range(NG):
                dc = g * NG + dl
                gcol = self.vcol("norm_mix", 1 * NDC + dc)
                self.copy("act", hf[:, 0:PH], halo[:, dc, :], [hlb[dc]], [hfb])
                self.stt("dve", hf[:, PH:LP], self.xT[:, dc, 0:P], gcol, self.rstd[:, 0:P], ALU.mult, ALU.mult,
                         [self.xb[dc], self.rstd_b, self.cbuf], [hfb])
                sv = hf[:, LP:LB].rearrange("p (s t) -> p s t", t=SW)
                self.copy("act", sv[:, :, 0:PH], hist[:, dc, :].rearrange("p (s t) -> p s t", t=PH), [hib[dc]], [hfb])
                self.copy("act", sv[:, :, PH:SW], hs[:, dc, :].rearrange("p (s t) -> p s t", t=TS), [hsb[dc]], [hfb])
                cur, curb = hf, hfb
                sh = 1
                bufs = [(sA, sAb), (sB, sBb)]
                lvl = 0
                while sh < wdw:
                    nxt, nxtb = bufs[lvl % 2]
                    self.tt("dve", nxt[:, sh:LP], cur[:, sh:LP], cur[:, 0:LP - sh], ALU.add, [curb], [nxtb])
                    cs = cur[:, LP:LB].rearrange("p (s t) -> p s t", t=SW)
                    ns_ = nxt[:, LP:LB].rearrange("p (s t) -> p s t", t=SW)
                    self.tt("dve", ns_[:, :, sh:SW], cs[:, :, sh:SW], cs[:, :, 0:SW - sh], ALU.add, [curb], [nxtb])
                    cur, curb = nxt, nxtb
                    sh *= 2
                    lvl += 1
                self.stt("dve", pooled[:, dc, HO + 16:HO + P], cur[:, PH + 16:LP], 1.0 / wdw, hf[:, PH + 16:LP], ALU.mult, ALU.subtract,
                         [curb, hfb], [plb[dc]])
                rc = self.vecs[:, self.voff["rc"] + g * 16:self.voff["rc"] + (g + 1) * 16]
                self.tt("dve", sB[:, 0:16] if cur is not sB else sA[:, 0:16], cur[:, PH:PH + 16], rc, ALU.mult, [curb, self.cbuf],
                        [sBb if cur is not sB else sAb])
                o16 = sB if cur is not sB else sA
                o16b = sBb if cur is not sB else sAb
                self.tt("dve", pooled[:, dc, HO:HO + 16], o16[:, 0:16], hf[:, PH:PH + 16], ALU.subtract, [o16b, hfb], [plb[dc]])
                cs = cur[:, LP:LB].rearrange("p (s t) -> p s t", t=SW)
                hv = hf[:, LP:LB].rearrange("p (s t) -> p s t", t=SW)
                self.stt("dve", pooled[:, dc, HO + P:HO + W].rearrange("p (s t) -> p s t", t=TS), cs[:, :, PH:SW], 1.0 / wdw, hv[:, :, PH:SW],
                         ALU.mult, ALU.subtract, [curb, hfb], [plb[dc]])
            slab, sb = self.wget(("grp", g))
            sv = slab[:, 0:NG * NG * 128].rearrange("p (a b) -> p a b", b=NG * 128)
            for dl in range(NG):
                dco = g * NG + dl
                for (c0, w) in own:
                    ps, pb = self.ps()
                    for k in range(NG):
                        self.mm(ps[:, 0:w], sv[:, k, dl * 128:(dl + 1) * 128], pooled[:, g * NG + k, c0:c0 + w], k == 0, k == NG - 1,
                                [sb, plb[g * NG + k]], [pb])
                    xs = self.xT[:, dco, c0 - HO:c0 - HO + w]
                    self.stt("dve", xs, ps[:, 0:w], self.vcol("b_scale", dco), xs, ALU.mult, ALU.add, [pb, self.xb[dco], self.cbuf], [self.xb[dco]])

    def mixer_c(self):
        c = self.cfg
        NDC, P, PT, W, WH, NS, D, CH = (c[k] for k in ["NDC", "P", "PT", "W", "WH", "NS", "D", "CH"])
        nc = self.nc
        H = NDC
        stc, ncs, ncp = self.d["st_c"].ap(), self.d["nc_s"].ap(), self.d["nc_p"].ap()
        oN = self.bf16(NDC, WH); onb = [Buf("on%d" % i) for i in range(NDC)]
        self.spill_x()
        nf = lambda nm: (self.f32(W), Buf(nm))
        A2 = [nf("A0"), nf("A1")]; B, Bb = nf("B"); X1, X1b = nf("X1"); X2, X2b = nf("X2"); Fb, Fbb = nf("F")
        G3 = [(self.bf16(W), Buf("G%d" % i)) for i in range(3)]
        E2 = [nf("E0"), nf("E1")]
        qt = self.bf16(W); qtb = Buf("qt")
        kt = self.bf16(W); ktb = Buf("kt")
        qh2 = [(self.bf16(P), Buf("qh0")), (self.bf16(P), Buf("qh1"))]
        osq = self.bf16(W); osqb = Buf("osq")
        rs = self.f32(W); rsb = Buf("rs")
        lb = self.f32(NDC); lbb = Buf("lb")
        oml = self.f32(NDC)
        ex = self.f32(4 * NDC)
        Sin = self.f32(NS, 128); Sinb = Buf("Sin")
        Sout = self.f32(NS, 128); Soutb = Buf("Sout")
        NR = 6
        Sf = [self.f32(128) for _ in range(NR)]; Sfb = [Buf("Sf%d" % i) for i in range(NR)]
        Sb = [self.bf16(128) for _ in range(NR)]; Sbb = [Buf("Sb%d" % i) for i in range(NR)]
        k2c = [self.f32(CH) for _ in range(4)]; k2b = [Buf("k2c%d" % i) for i in range(4)]
        kiT = [self.bf16(256) for _ in range(4)]; kib = [Buf("kiT%d" % i) for i in range(4)]
        PTm = [self.bf16(CH) for _ in range(4)]; ptb = [Buf("PT%d" % i) for i in range(4)]
        Send2 = [(self.f32(128), Buf("Se0")), (self.f32(128), Buf("Se1"))]
        Srv2 = [(self.f32(128), Buf("Srv0")), (self.f32(128), Buf("Srv1"))]
        Srb = self.bf16(128); Srbb = Buf("Srb")
        Pc2 = [(self.f32(P // CH + 1), Buf("Pc0")), (self.f32(P // CH + 1), Buf("Pc1"))]
        Sfin = self.f32(128); Sfinb = Buf("Sfin")
        self.use_scr = False
        cl = self.vecs[:, self.voff["c_lb"]:self.voff["c_lb"] + 4 * NDC]
        self.act(ex[:, 0:4 * NDC], cl, AF.Exp, [self.cbuf], [lbb])
        exv = ex[:, 0:4 * NDC].rearrange("p (l c) -> p l c", c=NDC)
        self.tt("dve", lb[:, 0:NDC], exv[:, 1, :], exv[:, 2, :], ALU.add, [lbb], [lbb])
        self.tt("dve", oml[:, 0:NDC], exv[:, 0, :], exv[:, 3, :], ALU.add, [lbb], [lbb])
        self.tt("dve", ex[:, 0:NDC], lb[:, 0:NDC], oml[:, 0:NDC], ALU.add, [lbb], [lbb])
        self.S.op("dve", lambda e: e.reciprocal(ex[:, 0:NDC], ex[:, 0:NDC]), [lbb], [lbb])
        self.tt("dve", lb[:, 0:NDC], lb[:, 0:NDC], ex[:, 0:NDC], ALU.mult, [lbb], [lbb])
        self.tt("dve", oml[:, 0:NDC], oml[:, 0:NDC], ex[:, 0:NDC], ALU.mult, [lbb], [lbb])
        own = self.tiles(0)
        NCH = P // CH

        def finalize(hd):
            par = hd % 2
            E, Eb = E2[par]; G, Gb = G3[hd % 3]; qh, qhb = qh2[par]
            Srv, Srvb = Srv2[par]; Send, Sendb = Send2[par]; Pc, Pcb = Pc2[par]
            self.ts("dve", Srv, Srv, self.vcol("isodd", 0), None, ALU.mult, None, [Srvb, self.cbuf], [Srvb])
            self.copy("act", Srb, Srv, [Srvb], [Srbb])
            self.stt("dve", Sfin, Srv, Pc[:, NCH:NCH + 1], Send, ALU.mult, ALU.add, [Srvb, Pcb, Sendb], [Sfinb])
            self.dma("sp", ncp[0][hd], Sfin, [Sfinb], [])
            for (c0, w) in own:
                pp, sp = self.split(c0, w)
                if pp:
                    a = pp[0] - HO
                    ps, pb = self.ps()
                    self.mm(ps[:, 0:pp[1]], Srb, qh[:, a:a + pp[1]], True, True, [Srbb, qhb], [pb])
                    self.tt("dve", E[:, a:a + pp[1]], E[:, a:a + pp[1]], ps[:, 0:pp[1]], ALU.add, [pb, Eb], [Eb])
            self.act(osq[:, 0:W], E[:, 0:W], AF.Square, [Eb], [osqb])
            for (c0, w) in own:
                ps, pb = self.ps()
                self.mm(ps[:, 0:w], self.ones_bf, osq[:, c0 - HO:c0 - HO + w], True, True, [osqb, self.cbuf], [pb])
                self.act(rs[:, c0 - HO:c0 - HO + w], ps[:, 0:w], AF.Sqrt, [pb], [rsb], bias=EPS, scale=1.0 / 128)
            self.S.op("dve", lambda e: e.reciprocal(rs[:, 0:W], rs[:, 0:W]), [rsb], [rsb])
            self.tt("dve", E[:, 0:W], E[:, 0:W], rs[:, 0:W], ALU.mult, [Eb, rsb], [Eb])
            self.stt("dve", oN[:, hd, HO:HO + W], E[:, 0:W], self.vcol("c_g_norm", hd), G[:, 0:W], ALU.mult, ALU.mult,
                     [Eb, Gb, self.cbuf], [onb[hd]])

        ri = 0

        def make_proj_items(hd):
            par = hd % 2
            Ai, Aib = A2[par]; G, Gb = G3[hd % 3]
            s0, s0b = self.wget(("cin0", hd))
            s1, s1b = self.wget(("cin1", hd))
            v0 = s0.rearrange("p (a b) -> p a b", b=256)
            v1 = s1.rearrange("p (a b) -> p a b", b=256)
            items = []

            def item(sv, sbuf, k, fn, c0, w):
                def run():
                    ps, pb = self.ps()
                    for dc in range(NDC):
                        self.mm(ps[:, 0:w], sv[:, dc, k * 128:(k + 1) * 128], self.hT[:, dc, c0:c0 + w], dc == 0, dc == NDC - 1,
                                [sbuf, self.hb[dc]], [pb])
                    fn(ps[:, 0:w], pb, c0 - HO, w)
                return run
            specs = [
                (v0, s0b, 1, lambda ps, pb, a, w: self.act(X1[:, a:a + w], ps, AF.Sigmoid, [pb], [X1b])),
                (v0, s0b, 0, lambda ps, pb, a, w: self.act(X2[:, a:a + w], ps, AF.Silu, [pb], [X2b])),
                (v1, s1b, 0, lambda ps, pb, a, w: self.copy("act", Ai[:, a:a + w], ps, [pb], [Aib])),
                (v1, s1b, 1, lambda ps, pb, a, w: self.act(G[:, a:a + w], ps, AF.Silu, [pb], [Gb])),
            ]
            for (sv, sbuf, k, fn) in specs:
                for (c0, w) in own:
                    items.append(item(sv, sbuf, k, fn, c0, w))
            return items

        def prep(hd):
            par = hd % 2
            T, Tb = E2[par]
            Pc, Pcb = Pc2[par]
            self.ts("dve", X1[:, 0:W], X1[:, 0:W], oml[:, hd:hd + 1], lb[:, hd:hd + 1], ALU.mult, ALU.add, [X1b, lbb], [X1b])
            self.ts("dve", B[:, 0:W], X1[:, 0:W], -1.0, 1.0, ALU.mult, ALU.add, [X1b], [Bb])
            self.tt("dve", T[:, 0:W], X1[:, 0:W], self.m0, ALU.mult, [X1b, self.cbuf], [Tb])
            self.tt("dve", X1[:, 0:W], X1[:, 0:W], T[:, 0:W], ALU.subtract, [X1b, Tb], [X1b])
            self.S.op("dve", lambda e: e.tensor_tensor_scan(Fb[:, 0:W], T[:, 0:W], X1[:, 0:W], 0.0, ALU.mult, ALU.add), [X1b, Tb], [Fbb])
            self.ts("dve", T[:, 0:W], Fb[:, 0:W], 1e-36, None, ALU.max, None, [Fbb], [Tb])
            self.S.op("dve", lambda e: e.reciprocal(T[:, 0:W], T[:, 0:W]), [Tb], [Tb])
            self.tt("dve", B[:, 0:W], B[:, 0:W], T[:, 0:W], ALU.mult, [Tb, Bb], [Bb])
            self.copy("act", kt[:, 0:W], B[:, 0:W], [Bb], [ktb])
            self.tt("dve", qt[:, 0:W], X2[:, 0:W], Fb[:, 0:W], ALU.mult, [X2b, Fbb], [qtb])
            Fe = Fb[:, 0:P].rearrange("p (c t) -> p c t", t=CH)[:, :, CH - 1]
            self.S.op("dve", lambda e, Pc=Pc: e.memset(Pc[:, 0:1], 1.0), [], [Pcb])
            self.S.op("dve", lambda e, Fe=Fe, Pc=Pc: e.tensor_tensor_scan(Pc[:, 1:NCH + 1], Fe, self.zero[:, 0:NCH], 1.0, ALU.mult, ALU.add),
                      [Fbb, self.cbuf], [Pcb])

        for it in make_proj_items(0):
            it()
        prep(0)
        for hd in range(H):
            par = hd % 2
            E, Eb = E2[par]; G, Gb = G3[hd % 3]; qh, qhb = qh2[par]; A, Ab = A2[par]
            Srv, Srvb = Srv2[par]; Send, Sendb = Send2[par]; Pc, Pcb = Pc2[par]
            pending = make_proj_items(hd + 1) if hd + 1 < H else []
            self.dma("sp", Sin[:, 0:NS, :], stc[0][:, hd, :, :].rearrange("s k v -> k s v"), [], [Sinb])
            psteps = [("p", ci, ci * CH, CH) for ci in range(NCH)]
            ssteps = [("s", s, P + s * TS, TS) for s in range(NS)]
            steps = []
            for i_ in range(max(len(psteps), len(ssteps))):
                if i_ < len(psteps):
                    steps.append(psteps[i_])
                if i_ < len(ssteps):
                    steps.append(ssteps[i_])
            st = {"cur": None}
            ctx = []

            def S1(t):
                kind, ci, a, C = steps[t]
                d_ = {"i4": t % 4}
                i4 = d_["i4"]
                fend = Fb[:, a + C - 1:a + C]
                self.ts("dve", k2c[i4][:, 0:C], B[:, a:a + C], fend, None, ALU.mult, None, [Bb, Fbb], [k2b[i4]])
                pT, pTb = self.ps()
                self.tr(pT[0:C, 0:128], k2c[i4][:, 0:C], 128, [k2b[i4]], [pTb])
                self.tr(pT[0:C, 128:256], A[:, a:a + C], 128, [Ab], [pTb])
                pS, pSb = self.ps()
                self.mm(pS[0:C, 0:C], kt[:, a:a + C], qt[:, a:a + C], True, True, [ktb, qtb], [pSb])
                d_.update(pT=pT, pTb=pTb, pS=pS, pSb=pSb, fend=fend)
                ctx.append(d_)

            def S2(t):
                kind, ci, a, C = steps[t]
                d_ = ctx[t]
                i4 = d_["i4"]
                self.copy("act", kiT[i4][0:C, 0:256], d_["pT"][0:C, 0:256], [d_["pTb"]], [kib[i4]])
                self.tt("dve", PTm[i4][0:C, 0:C], d_["pS"][0:C, 0:C], self.tri[0:C, 0:C], ALU.mult, [d_["pSb"], self.cbuf], [ptb[i4]])
                if kind == "p":
                    self.ts("dve", qh[:, a:a + C], qt[:, a:a + C], Pc[:, ci:ci + 1], None, ALU.mult, None, [qtb, Pcb], [qhb])
                pD, pDb = self.ps()
                self.mm(pD[:, 0:128], kiT[i4][0:C, 0:128], kiT[i4][0:C, 128:256], True, True, [kib[i4]], [pDb])
                d_.update(pD=pD, pDb=pDb)

            def S3(t):
                nonlocal ri
                kind, ci, a, C = steps[t]
                d_ = ctx[t]
                i4 = d_["i4"]
                fend, pD, pDb = d_["fend"], d_["pD"], d_["pDb"]
                cur = st["cur"]
                if kind == "p":
                    have_S = cur is not None
                    if have_S:
                        sf_in, sfb_in, sb_in, sbb_in = cur
                    r = ri % NR
                    ri += 1
                    nf_, nfb_, nb_, nbb_ = Sf[r], Sfb[r], Sb[r], Sbb[r]
                    if have_S:
                        self.stt("dve", nf_, sf_in, fend, pD[:, 0:128], ALU.mult, ALU.add, [sfb_in, Fbb, pDb], [nfb_])
                    else:
                        self.copy("dve", nf_, pD[:, 0:128], [pDb], [nfb_])
                    self.copy("act", nb_, nf_, [nfb_], [nbb_])
                    st["cur"] = (nf_, nfb_, nb_, nbb_)
                else:
                    have_S = True
                    sf_in, sfb_in = Sin[:, ci, :], Sinb
                    r = ri % NR
                    ri += 1
                    sb_in, sbb_in = Sb[r], Sbb[r]
                    self.copy("act", sb_in, sf_in, [sfb_in], [sbb_in])
                    self.stt("dve", Sout[:, ci, :], sf_in, fend, pD[:, 0:128], ALU.mult, ALU.add, [sfb_in, Fbb, pDb], [Soutb])
                pO, pOb = self.ps()
                self.mm(pO[:, 0:C], kiT[i4][0:C, 128:256], PTm[i4][0:C, 0:C], True, not have_S, [kib[i4], ptb[i4]], [pOb])
                if have_S:
                    self.mm(pO[:, 0:C], sb_in, qt[:, a:a + C], False, True, [sbb_in, qtb], [pOb])
                self.copy("act", E[:, a:a + C], pO[:, 0:C], [pOb], [Eb])

            nst = len(steps)
            every = 10 ** 9
            for t in range(nst + 2):
                if t < nst:
                    S1(t)
                if 1 <= t <= nst:
                    S2(t - 1)
                if t >= 2:
                    S3(t - 2)
            cur = st["cur"]
            self.dma("sp", ncs[0][:, hd, :, :].rearrange("s k v -> k s v"), Sout[:, 0:NS, :], [Soutb], [])
            if hd >= 1:
                finalize(hd - 1)
            self.copy("dve", Send, cur[0], [cur[1]], [Sendb])
            k = self.ncc_tensors
            self.ncc_tensors += 1
            snd = nc.dram_tensor("ss%d" % k, [128, 128], F32)
            rcv = nc.dram_tensor("sr%d" % k, [256, 128], F32)
            sdb, rvb = Buf("ssnd"), Buf("srcv")
            self.dma("pool", snd.ap(), Send, [Sendb], [sdb])
            self.S.cc(lambda e, snd=snd, rcv=rcv: e.collective_compute("AllGather", ALU.bypass, replica_groups=PAIRS, ins=[snd.ap()], outs=[rcv.ap()]),
                      [sdb], [rvb])
            self.dma("sp", Srv, rcv.ap()[0:128, :], [rvb], [Srvb])
            if hd + 1 < H:
                half = len(pending) // 2
                for it in pending[:half]:
                    it()
                prep(hd + 1)
                for it in pending[half:]:
                    it()
        finalize(H - 1)
        self.restore_x()
        self.proj_residual("wo", None, oN, onb)

    def final_out(self):
        c = self.cfg
        NDC, P, PT, W, D = c["NDC"], c["P"], c["PT"], c["W"], c["D"]
        self.sumsq_to_rstd([(self.xT[:, dc, :], self.xb[dc]) for dc in range(NDC)], W, D, self.rstd, self.rstd_b)
        dsts = [(self.d["yp"].ap()[t * 128:(t + 1) * 128, :], 128, t * 128) for t in range(P // 128)]
        dsts.append((self.d["ys"].ap()[:, :], PT, P))
        stg = [self.f32(D) for _ in range(2)]
        sb = [Buf("ystg0"), Buf("ystg1")]
        tmp = [self.f32(128) for _ in range(8)]
        tmb = [Buf("yt%d" % i) for i in range(8)]
        ti = 0
        for i, (dst, n, col) in enumerate(dsts):
            s, b = stg[i % 2], sb[i % 2]
            for q in range(NDC // 4):
                ps, pb = self.ps()
                for k in range(4):
                    dc = q * 4 + k
                    t, tb = tmp[ti % 8], tmb[ti % 8]
                    ti += 1
                    self.stt("dve", t[:, 0:n], self.xT[:, dc, col:col + n], self.vcol("norm_final", dc), self.rstd[:, col:col + n],
                             ALU.mult, ALU.mult, [self.xb[dc], self.rstd_b, self.cbuf], [tb])
                    self.tr(ps[0:n, k * 128:(k + 1) * 128], t[:, 0:n], 128, [tb], [pb])
                self.copy("act", s[0:n, q * 512:(q + 1) * 512], ps[0:n, 0:512], [pb], [b])
            self.dma("sp", dst, s[0:n, :], [b], [])


_CACHE = {}


def run(cfg, inp):
    D, P, NS, NDC, DFF = cfg["D"], cfg["P"], cfg["NS"], cfg["NDC"], cfg["DFF"]
    inp = {k: np.asarray(v) for k, v in inp.items()}
    wslab, sidx = build_slabs(cfg, inp)
    consts = build_consts(cfg)
    key = tuple(sorted(cfg.items()))
    if key not in _CACHE:
        _CACHE[key] = Builder(cfg, sidx, wslab.shape[0]).build()
    nc = _CACHE[key]
    B = inp["x_prompt"].shape[0]
    assert 2 * B == NCORES and inp["x_prompt"].shape[1] == 2 * P and inp["x_sample"].shape[0] == NCORES * NS
    in_maps = []
    for c in range(NCORES):
        sq, hf = c // 2, c % 2
        sl = slice(c * NS, (c + 1) * NS)
        in_maps.append({
            "xp": np.ascontiguousarray(inp["x_prompt"][sq, hf * P:(hf + 1) * P]),
            "xs": np.ascontiguousarray(inp["x_sample"][sl]).reshape(NS * TS, D),
            "st_a": np.ascontiguousarray(inp["state_conv_a"][:, sl]),
            "st_b": np.ascontiguousarray(inp["state_pool"][:, sl]),
            "st_c": np.ascontiguousarray(inp["state_hgrn"][:, sl]),
            "st_f": np.ascontiguousarray(inp["state_ffn_conv"][:, sl]),
            "wslab": wslab, "vecs": build_vecs(cfg, inp, c), "consts": consts,
        })
    res = run_bass_kernel_spmd(nc, in_maps, core_ids=list(range(NCORES)))
    R = res.results
    f = np.float32
    y_p = np.zeros((B, 2 * P, D), f); y_s = np.zeros((NCORES * NS, TS, D), f)
    na_p = np.zeros((2, B, HO, D), f); na_s = np.zeros((2, NCORES * NS, HO, D), f)
    nb_p = np.zeros((1, B, PH, D), f); nb_s = np.zeros((1, NCORES * NS, PH, D), f)
    nc_p = np.zeros((1, B, NDC, 128, 128), f); nc_s = np.zeros((1, NCORES * NS, NDC, 128, 128), f)
    nf_p = np.zeros((4, B, 2, 2 * DFF), f); nf_s = np.zeros((4, NCORES * NS, 2, 2 * DFF), f)
    for c in range(NCORES):
        sq, hf = c // 2, c % 2
        sl = slice(c * NS, (c + 1) * NS)
        r = R[c]
        y_p[sq, hf * P:(hf + 1) * P] = r["yp"]
        y_s[sl] = r["ys"].reshape(NS, TS, D)
        na_s[:, sl] = r["na_s"]; nb_s[:, sl] = r["nb_s"]; nc_s[:, sl] = r["nc_s"]; nf_s[:, sl] = r["nf_s"]
        if hf == 1:
            na_p[:, sq] = r["na_p"]; nb_p[:, sq] = r["nb_p"]; nc_p[:, sq] = r["nc_p"]; nf_p[:, sq] = r["nf_p"]
    return (y_p, y_s, na_p, na_s, nb_p, nb_s, nc_p, nc_s, nf_p, nf_s)


def kernel(**inputs):
    return run(make_cfg(), inputs)
```

```python
import numpy as np
import ml_dtypes
import concourse.bass as bass
import concourse.mybir as mybir
from concourse.bass_utils import run_bass_kernel_spmd

F32 = mybir.dt.float32
BF16 = mybir.dt.bfloat16
AF = mybir.ActivationFunctionType
ALU = mybir.AluOpType

NCORES = 8
EPS = 1e-6
HO = 30
CW = 31
PH = 15
TS = 8
SLOT = 4096
NSLOT = 3
PAIRS = [[0, 1], [2, 3], [4, 5], [6, 7]]


def make_cfg(D=2048, DFF=5632, P=1024, NS=16, GP=4, CH=64):
    c = dict(D=D, DFF=DFF, P=P, NS=NS, GP=GP, CH=CH)
    c["NDC"] = D // 128
    c["NFP"] = DFF // 128
    c["PT"] = NS * TS
    c["W"] = P + NS * TS
    c["WH"] = HO + c["W"]
    assert c["NFP"] % GP == 0 and c["NDC"] % 4 == 0 and P % CH == 0 and c["PT"] <= 128
    assert (GP * 128) % 1 == 0
    return c


def slab_pair(Wm, c0, c1):
    K = Wm.shape[0]
    a = Wm[:, c0:c0 + 128].reshape(K // 128, 128, 128)
    b = Wm[:, c1:c1 + 128].reshape(K // 128, 128, 128)
    s = np.concatenate([a, b], axis=2)
    return np.ascontiguousarray(s.transpose(1, 0, 2)).reshape(128, -1)


def slab_rows(Wm, r0, nrc, c0, ncol):
    s = Wm[r0:r0 + nrc * 128, c0:c0 + ncol].reshape(nrc, 128, ncol)
    return np.ascontiguousarray(s.transpose(1, 0, 2)).reshape(128, -1)


def build_slabs(cfg, inp):
    D, DFF, NDC, NFP, GP = cfg["D"], cfg["DFF"], cfg["NDC"], cfg["NFP"], cfg["GP"]
    slabs = []
    idx = {}

    def add(key, arr):
        pad = np.zeros((128, SLOT), np.float32)
        pad[:, :arr.shape[1]] = arr
        idx[key] = len(slabs)
        slabs.append(pad)

    def ffn(l):
        wu, wd = inp["f_w_up"][l], inp["f_w_down"][l]
        ng = NFP // GP
        dcols = min(D, SLOT // GP)
        ndh = D // dcols
        for g in range(ng + 1):
            if g < ng:
                for p in range(GP):
                    j = g * GP + p
                    add(("up", l, j), slab_pair(wu, j * 128, DFF + j * 128))
            if g >= 1:
                for h in range(ndh):
                    add(("down", l, g - 1, h), slab_rows(wd, (g - 1) * GP * 128, GP, h * dcols, dcols))

    def mixA(j):
        for dc in range(NDC):
            add(("pw1", j, dc), slab_pair(inp["a_w_pw1"][j], dc * 128, D + dc * 128))
        for dc2 in range(NDC // 2):
            add(("pw2", j, dc2), slab_rows(inp["a_w_pw2"][j], 0, NDC, dc2 * 256, 256))

    def mixB():
        ng = NDC // 4
        for g in range(4):
            add(("grp", g), slab_rows(inp["b_w_grp"][0][g], 0, ng, 0, ng * 128))

    def mixC():
        w = inp["c_w_in"][0]
        for hd in range(NDC):
            add(("cin0", hd), slab_pair(w, hd * 128, D + hd * 128))
            add(("cin1", hd), slab_pair(w, 2 * D + hd * 128, 3 * D + hd * 128))
        for dc2 in range(NDC // 2):
            add(("wo", dc2), slab_rows(inp["c_w_o"][0], 0, NDC, dc2 * 256, 256))

    mixA(0); ffn(0); mixB(); ffn(1); mixC(); ffn(2); mixA(1); ffn(3)
    return np.stack(slabs, 0), idx


def vec_layout(cfg):
    NDC, NFP = cfg["NDC"], cfg["NFP"]
    off = {}
    n = 0
    for name, cnt in [("norm_mix", 4 * NDC), ("norm_ffn", 4 * NDC), ("norm_final", NDC), ("a_b_dw", 2 * NDC),
                      ("a_ln_g", 2 * NDC), ("a_ln_b", 2 * NDC), ("a_w_dw", 2 * CW * NDC), ("b_scale", NDC),
                      ("c_lb", 4 * NDC), ("c_g_norm", NDC), ("f_w_dw", 4 * 3 * 2 * NFP), ("f_b_dw", 4 * 2 * NFP),
                      ("isodd", 1), ("rc", 4 * 16)]:
        off[name] = n
        n += cnt
    return off, n


def colize(v):
    sh = v.shape
    C = sh[-1] // 128
    a = v.reshape(sh[:-1] + (C, 128))
    a = np.moveaxis(a, -1, 0)
    return np.ascontiguousarray(a).reshape(128, -1)


def build_vecs(cfg, inp, core):
    off, n = vec_layout(cfg)
    V = np.zeros((128, n), np.float32)

    def put(name, arr):
        V[:, off[name]:off[name] + arr.shape[1]] = arr
    for k in ["norm_mix", "norm_ffn", "a_b_dw", "a_ln_g", "a_ln_b", "a_w_dw", "b_scale", "c_lb", "c_g_norm",
              "f_w_dw", "f_b_dw"]:
        put(k, colize(np.asarray(inp[k])))
    put("norm_final", colize(np.asarray(inp["norm_final"])[None, :]))
    V[:, off["isodd"]] = float(core % 2)
    start = (core % 2) * cfg["P"]
    rc = np.zeros((4, 16), np.float32)
    for g, w in enumerate((2, 4, 8, 16)):
        for t in range(16):
            rc[g, t] = 1.0 / min(w, start + t + 1)
    V[:, off["rc"]:off["rc"] + 64] = rc.reshape(1, 64)
    return V


def build_consts(cfg):
    W, P, CH = cfg["W"], cfg["P"], cfg["CH"]
    ident = np.eye(128, dtype=np.float32)
    tri = np.triu(np.ones((128, 128), np.float32))
    m0 = np.ones((W,), np.float32)
    m0[0:P:CH] = 0.0
    m0[P::TS] = 0.0
    m0 = np.broadcast_to(m0[None, :], (128, W))
    return np.ascontiguousarray(np.concatenate([ident, tri, m0], axis=1))


class Buf:
    __slots__ = ("lw", "rd", "name")

    def __init__(self, name=""):
        self.lw = None
        self.rd = []
        self.name = name


class Op:
    __slots__ = ("eng", "fn", "deps", "signal", "tick", "idx", "dsem", "dval", "kind")

    def __init__(self, eng, fn, kind):
        self.eng, self.fn, self.kind = eng, fn, kind
        self.deps = []
        self.signal = False
        self.tick = 0
        self.idx = 0
        self.dsem = None
        self.dval = 0


ENGS = ["pe", "act", "dve", "pool", "sp"]
NDSEM = 8


class Sched:
    def __init__(self):
        self.streams = {e: [] for e in ENGS}
        self.ndma = {"pool": 0, "sp": 0}
        self.ncc = 0
        self.dma_ops = {"pool": [], "sp": []}

    def _deps(self, op, r, w):
        deps = op.deps
        for b in r:
            if b.lw is not None:
                deps.append((b.lw, "raw"))
        for b in w:
            if b.lw is not None:
                deps.append((b.lw, "waw"))
            for x in b.rd:
                deps.append((x, "war"))
        for b in r:
            if op.kind == "c":
                b.rd = [x for x in b.rd if not (x.kind == "c" and x.eng == op.eng)]
            b.rd.append(op)
        for b in w:
            b.lw = op
            b.rd = []

    def op(self, eng, fn, r=(), w=()):
        o = Op(eng, fn, "c")
        o.idx = len(self.streams[eng])
        self._deps(o, r, w)
        self.streams[eng].append(o)
        return o

    def dma(self, q, fn, r=(), w=()):
        o = Op(q, fn, "d")
        o.idx = len(self.streams[q])
        n = self.ndma[q]
        self.ndma[q] += 1
        o.dsem = (q, n % NDSEM)
        o.dval = 16 * (n // NDSEM + 1)
        if n >= NDSEM:
            o.deps.append((self.dma_ops[q][n - NDSEM], "raw"))
        self.dma_ops[q].append(o)
        self._deps(o, r, w)
        self.streams[q].append(o)
        return o

    def cc(self, fn, r=(), w=()):
        o = Op("pool", fn, "cc")
        o.idx = len(self.streams["pool"])
        o.dsem = ("cc", self.ncc)
        self.ncc += 1
        o.dval = 1
        self._deps(o, r, w)
        self.streams["pool"].append(o)
        return o

    def barrier(self, bufs):
        b = Buf("barrier")
        lasts = []
        for e in ENGS:
            if self.streams[e]:
                lasts.append(self.streams[e][-1])
        outstanding = []
        for q in ("pool", "sp"):
            outstanding += self.dma_ops[q][-NDSEM:]
        for e in ["pe", "act", "dve", "pool", "sp"]:
            o = Op(e, None, "nop")
            o.idx = len(self.streams[e])
            for x in lasts + outstanding:
                if x is not o:
                    o.deps.append((x, "raw"))
            self.streams[e].append(o)

    def finalize(self):
        for e in ENGS:
            for o in self.streams[e]:
                keep = []
                for (d, kind) in o.deps:
                    if d.kind == "c" or d.kind == "nop":
                        if d.eng == o.eng and o.kind in ("c", "nop"):
                            if d.kind == "nop" or e == "pe" or kind != "raw" or o.idx - d.idx > 3:
                                continue
                        if d.kind == "nop":
                            continue
                        d.signal = True
                    keep.append(d)
                o.deps = keep
        for e in ENGS:
            t = 0
            for o in self.streams[e]:
                if o.signal:
                    t += 1
                    o.tick = t

    def emit(self, nc, sems, dsems, ccsems):
        engobj = {"pe": "tensor", "act": "scalar", "dve": "vector", "pool": "gpsimd", "sp": "sync"}
        with nc.Block() as block:
            for e in ENGS:
                def body(eng, e=e):
                    known = {}
                    for o in self.streams[e]:
                        for d in o.deps:
                            if d.kind == "c":
                                key, val, sem = ("c", d.eng), d.tick, sems[d.eng]
                            elif d.kind == "d":
                                key, val, sem = d.dsem, d.dval, dsems[d.dsem]
                            else:
                                key, val, sem = d.dsem, d.dval, ccsems[d.dsem[1]]
                            if known.get(key, 0) >= val:
                                continue
                            known[key] = val
                            eng.wait_ge(sem, val)
                        if o.fn is None:
                            continue
                        ins = o.fn(eng)
                        if o.kind == "d":
                            ins.then_inc(dsems[o.dsem], 16)
                        elif o.kind == "cc":
                            ins.then_inc(ccsems[o.dsem[1]], 1)
                        elif o.signal:
                            ins.then_inc(sems[e], 1)
                getattr(block, engobj[e])(body)


class Builder:
    def __init__(self, cfg, slab_idx, nslab):
        self.cfg = cfg
        self.sidx = slab_idx
        self.nslab = nslab
        self.S = Sched()
        self.nc = bass.Bass("TRN2", target_bir_lowering=False)
        self.voff, self.nvec = vec_layout(cfg)
        self.next_slab = 0
        self.ncc_tensors = 0
        self.use_scr = False

    def alloc(self, n_f32):
        n_f32 = (n_f32 + 7) // 8 * 8
        if self.use_scr and self.xtop + n_f32 <= self.xlimit:
            a = self.xtop
            self.xtop += n_f32
            return self.arena[:, a:a + n_f32]
        a = self.top
        self.top += n_f32
        assert self.top <= self.arena_words, ("SBUF arena overflow", self.top, self.arena_words)
        return self.arena[:, a:a + n_f32]

    def spill_x(self):
        c = self.cfg
        xsp = self.d["xspill"].ap().rearrange("p (a b) -> p a b", b=c["W"])
        self.spb = Buf("xspill")
        self.dma("sp", xsp, self.xT, list(self.xb), [self.spb])
        self.S.barrier([])
        self.xtop = self.x_start
        self.use_scr = True

    def restore_x(self):
        c = self.cfg
        self.use_scr = False
        self.S.barrier([])
        xsp = self.d["xspill"].ap().rearrange("p (a b) -> p a b", b=c["W"])
        self.dma("sp", self.xT, xsp, [self.spb], list(self.xb))

    def f32(self, *shape):
        n = int(np.prod(shape))
        ap = self.alloc(n)[:, 0:n]
        if len(shape) == 2:
            return ap.rearrange("p (a b) -> p a b", b=shape[1])
        if len(shape) == 3:
            return ap.rearrange("p (a b c) -> p a b c", b=shape[1], c=shape[2])
        return ap

    def bf16(self, *shape):
        n = int(np.prod(shape))
        ap = self.alloc((n + 1) // 2).bitcast(BF16)[:, 0:n]
        if len(shape) == 2:
            return ap.rearrange("p (a b) -> p a b", b=shape[1])
        if len(shape) == 3:
            return ap.rearrange("p (a b c) -> p a b c", b=shape[1], c=shape[2])
        return ap

    def vcol(self, name, i):
        o = self.voff[name] + i
        return self.vecs[:, o:o + 1]

    def mm(self, out, lhsT, rhs, start, stop, r, w):
        return self.S.op("pe", lambda e: e.matmul(out, lhsT, rhs, start=start, stop=stop), r, w)

    def tr(self, out, in_, npart, r, w):
        idn = self.ident[0:npart, 0:npart]
        return self.S.op("pe", lambda e: e.transpose(out, in_, idn), list(r) + [self.cbuf], w)

    def act(self, out, in_, func, r, w, bias=0.0, scale=1.0):
        return self.S.op("act", lambda e: e.activation(out, in_, func, bias=bias, scale=scale), r, w)

    def ts(self, eng, out, in0, s1, s2, op0, op1, r, w):
        if s2 is None:
            return self.S.op(eng, lambda e: e.tensor_scalar(out, in0, s1, None, op0), r, w)
        return self.S.op(eng, lambda e: e.tensor_scalar(out, in0, s1, s2, op0, op1), r, w)

    def tt(self, eng, out, in0, in1, op, r, w):
        return self.S.op(eng, lambda e: e.tensor_tensor(out, in0, in1, op), r, w)

    def stt(self, eng, out, in0, sc, in1, op0, op1, r, w):
        return self.S.op(eng, lambda e: e.scalar_tensor_tensor(out, in0, sc, in1, op0, op1), r, w)

    def copy(self, eng, out, in_, r, w):
        if eng == "act":
            return self.S.op("act", lambda e: e.copy(out, in_), r, w)
        return self.S.op(eng, lambda e: e.tensor_copy(out, in_), r, w)

    def dma(self, q, out, in_, r, w):
        return self.S.dma(q, lambda e: e.dma_start(out=out, in_=in_), r, w)

    def ps(self):
        i = self.ps_next
        self.ps_next = (i + 1) % 8
        return self.psum[i], self.psb[i]

    def wget(self, key):
        sid = self.sidx[key]
        assert sid == self.next_slab, (key, sid, self.next_slab)
        self.next_slab += 1
        k = sid % NSLOT
        slot, sb = self.wslots[k], self.wsb[k]
        src = self.wslab[sid]
        self.dma("pool", slot, src, [], [sb])
        return slot, sb

    def tiles(self, halo):
        c = self.cfg
        out = []
        c0 = HO - halo
        while c0 < c["WH"]:
            w = min(512, c["WH"] - c0)
            out.append((c0, w))
            c0 += w
        return out

    def split(self, c0, w):
        c = self.cfg
        pe = HO + c["P"]
        pp = (c0, min(c0 + w, pe) - c0) if c0 < pe else None
        sp = None
        if c0 + w > pe:
            assert c0 <= pe and c0 + w == c["WH"]
            sp = (pe, c["PT"])
        return pp, sp

    def build(self):
        nc, c = self.nc, self.cfg
        D, NDC, P, NS, PT, W, WH = c["D"], c["NDC"], c["P"], c["NS"], c["PT"], c["W"], c["WH"]
        NFP = c["NFP"]
        dt = lambda n, s, t=F32, k="ExternalInput": nc.dram_tensor(n, s, t, kind=k)
        self.d = d = {}
        d["xp"] = dt("xp", [P, D]); d["xs"] = dt("xs", [PT, D])
        d["st_a"] = dt("st_a", [2, NS, HO, D]); d["st_b"] = dt("st_b", [1, NS, PH, D])
        d["st_c"] = dt("st_c", [1, NS, NDC, 128, 128]); d["st_f"] = dt("st_f", [4, NS, 2, 2 * c["DFF"]])
        d["wslab"] = dt("wslab", [self.nslab, 128, SLOT]); d["vecs"] = dt("vecs", [128, self.nvec])
        d["consts"] = dt("consts", [128, 256 + W])
        o = "ExternalOutput"
        d["yp"] = dt("yp", [P, D], F32, o); d["ys"] = dt("ys", [PT, D], F32, o)
        d["na_p"] = dt("na_p", [2, HO, D], F32, o); d["na_s"] = dt("na_s", [2, NS, HO, D], F32, o)
        d["nb_p"] = dt("nb_p", [1, PH, D], F32, o); d["nb_s"] = dt("nb_s", [1, NS, PH, D], F32, o)
        d["nc_p"] = dt("nc_p", [1, NDC, 128, 128], F32, o); d["nc_s"] = dt("nc_s", [1, NS, NDC, 128, 128], F32, o)
        d["nf_p"] = dt("nf_p", [4, 2, 2 * c["DFF"]], F32, o); d["nf_s"] = dt("nf_s", [4, NS, 2, 2 * c["DFF"]], F32, o)
        d["xspill"] = nc.dram_tensor("xspill", [128, NDC * W], F32)
        self.wslab = d["wslab"].ap()
        self.arena_words = (nc.sbuf_top - nc.sbuf_base) // 4 - 64
        nsem = {}
        import contextlib
        with contextlib.ExitStack() as es:
            self.arena = es.enter_context(nc.sbuf_tensor("arena", [128, self.arena_words], F32))
            self.psum = [es.enter_context(nc.psum_tensor("ps%d" % i, [128, 512], F32)) for i in range(8)]
            self.psb = [Buf("ps%d" % i) for i in range(8)]
            self.ps_next = 0
            sems = {e: es.enter_context(nc.semaphore("s_" + e)) for e in ENGS}
            dsems = {(q, i): es.enter_context(nc.semaphore("d_%s%d" % (q, i))) for q in ("pool", "sp") for i in range(NDSEM)}
            ccsems = [es.enter_context(nc.semaphore("cc%d" % i)) for i in range(32)]
            self.top = 0
            self.vecs = self.alloc(self.nvec)
            cst = self.alloc(256 + W)
            self.ident = cst[:, 0:128]
            self.tri = cst[:, 128:256]
            self.m0 = cst[:, 256:256 + W]
            self.cbuf = Buf("consts")
            self.ones_bf = self.bf16(128)
            self.zero = self.alloc(64)
            self.wslots = [self.bf16(SLOT) for _ in range(NSLOT)]
            self.wsb = [Buf("ws%d" % i) for i in range(NSLOT)]
            self.dma("sp", self.vecs[:, 0:self.nvec], d["vecs"].ap(), [], [self.cbuf])
            self.dma("sp", cst[:, 0:256 + W], d["consts"].ap(), [], [self.cbuf])
            self.S.op("dve", lambda e: e.memset(self.ones_bf, 1.0), [], [self.cbuf])
            self.S.op("dve", lambda e: e.memset(self.zero, 0.0), [], [self.cbuf])
            self.base_top = self.top
            self.program()
            assert self.next_slab == self.nslab, (self.next_slab, self.nslab)
            self.S.barrier([])
            self.S.finalize()
            assert self.S.ncc <= 32
            self.S.emit(nc, sems, dsems, ccsems)
        return nc

    def program(self):
        c = self.cfg
        NDC, P, PT, W, WH, D = c["NDC"], c["P"], c["PT"], c["W"], c["WH"], c["D"]
        self.x_start = self.top
        self.xT = self.f32(NDC, W)
        self.xlimit = self.top
        self.xb = [Buf("x%d" % i) for i in range(NDC)]
        self.hT = self.bf16(NDC, WH)
        self.hb = [Buf("h%d" % i) for i in range(NDC)]
        self.rstd = self.f32(W)
        self.rstd_b = Buf("rstd")
        self.stage_top = self.top
        self.load_x()
        for layer in range(4):
            kind = layer % 3
            self.top = self.stage_top
            self.rmsnorm("norm_mix", layer, halo=(HO if kind == 0 else 0))
            if kind == 0:
                self.mixer_a(layer // 3)
            elif kind == 1:
                self.mixer_b()
            else:
                self.mixer_c()
            self.S.barrier([])
            self.top = self.stage_top
            self.rmsnorm("norm_ffn", layer, halo=2)
            self.ffn(layer)
            self.S.barrier([])
        self.top = self.stage_top
        self.final_out()

    def load_x(self):
        c = self.cfg
        NDC, P, PT, D = c["NDC"], c["P"], c["PT"], c["D"]
        stg = [self.f32(D) for _ in range(2)]
        sb = [Buf("xstg0"), Buf("xstg1")]
        srcs = [(self.d["xp"].ap()[t * 128:(t + 1) * 128, :], 128, t * 128) for t in range(P // 128)]
        srcs.append((self.d["xs"].ap()[:, :], PT, P))
        for i, (src, n, col) in enumerate(srcs):
            s, b = stg[i % 2], sb[i % 2]
            self.dma("sp", s[0:n, :], src, [], [b])
            for q in range(NDC // 4):
                ps, pb = self.ps()
                for k in range(4):
                    dc = q * 4 + k
                    self.tr(ps[:, k * 128:k * 128 + n], s[0:n, dc * 128:(dc + 1) * 128], n, [b], [pb])
                eng = "act" if q % 2 == 0 else "dve"
                for k in range(4):
                    dc = q * 4 + k
                    self.copy(eng, self.xT[:, dc, col:col + n], ps[:, k * 128:k * 128 + n], [pb], [self.xb[dc]])

    def sumsq_to_rstd(self, srcs, width, dim, dst, dstb):
        sq = [self.bf16(512) for _ in range(2)]
        sqb = [Buf("sq0"), Buf("sq1")]
        n = len(srcs)
        c0 = 0
        k = 0
        while c0 < width:
            w = min(512, width - c0)
            ps, pb = self.ps()
            for i, (ap, b) in enumerate(srcs):
                self.act(sq[k % 2][:, 0:w], ap[:, c0:c0 + w], AF.Square, [b], [sqb[k % 2]])
                self.mm(ps[:, 0:w], self.ones_bf, sq[k % 2][:, 0:w], i == 0, i == n - 1, [sqb[k % 2], self.cbuf], [pb])
                k += 1
            self.act(dst[:, c0:c0 + w], ps[:, 0:w], AF.Sqrt, [pb], [dstb], bias=EPS, scale=1.0 / dim)
            c0 += w
        self.S.op("dve", lambda e: e.reciprocal(dst[:, 0:width], dst[:, 0:width]), [dstb], [dstb])

    def rmsnorm(self, gname, layer, halo=0):
        c = self.cfg
        NDC, W, D, P = c["NDC"], c["W"], c["D"], c["P"]
        self.sumsq_to_rstd([(self.xT[:, dc, :], self.xb[dc]) for dc in range(NDC)], W, D, self.rstd, self.rstd_b)
        dst, dstb = self.hT, self.hb
        st = None
        if halo:
            tail = self.bf16(NDC, halo)
            tlb = [Buf("rt%d" % i) for i in range(NDC)]
            for dc in range(NDC):
                self.stt("dve", tail[:, dc, :], self.xT[:, dc, P - halo:P], self.vcol(gname, layer * NDC + dc), self.rstd[:, P - halo:P],
                         ALU.mult, ALU.mult, [self.xb[dc], self.rstd_b, self.cbuf], [tlb[dc]])
            st = self.handoff_start(tail, tlb, halo, halo, BF16)
        for dc in range(NDC):
            g = self.vcol(gname, layer * NDC + dc)
            self.stt("dve", dst[:, dc, HO:HO + W], self.xT[:, dc, :], g, self.rstd[:, 0:W], ALU.mult, ALU.mult,
                     [self.xb[dc], self.rstd_b, self.cbuf], [dstb[dc]])
        if st is not None:
            self.handoff_finish(st, self.hT, self.hb, HO - halo)

    def handoff_start(self, src, srcb, col_end, h, dtype):
        c = self.cfg
        NDC = c["NDC"]
        nc = self.nc
        k = self.ncc_tensors
        self.ncc_tensors += 1
        snd = nc.dram_tensor("hs%d" % k, [128, NDC * h], dtype)
        rcv = nc.dram_tensor("hr%d" % k, [256, NDC * h], dtype)
        sb, rb = Buf("snd"), Buf("rcv")
        self.dma("pool", snd.ap().rearrange("p (a b) -> p a b", b=h), src[:, :, col_end - h:col_end], list(srcb), [sb])
        self.S.cc(lambda e: e.collective_compute("AllGather", ALU.bypass, replica_groups=PAIRS, ins=[snd.ap()], outs=[rcv.ap()]),
                  [sb], [rb])
        tmp = self.f32(NDC, h) if dtype == F32 else self.bf16(NDC, h)
        tb = Buf("hrtmp")
        self.dma("sp", tmp, rcv.ap()[0:128, :].rearrange("p (a b) -> p a b", b=h), [rb], [tb])
        return (tmp, tb, h)

    def handoff_finish(self, st, dst, dstb, dst_col):
        tmp, tb, h = st
        self.ts("dve", dst[:, :, dst_col:dst_col + h], tmp, self.vcol("isodd", 0), None, ALU.mult, None,
                [tb, self.cbuf], list(dstb))

    def handoff(self, src, srcb, col_end, h, dst, dstb, dst_col, dtype):
        st = self.handoff_start(src, srcb, col_end, h, dtype)
        self.handoff_finish(st, dst, dstb, dst_col)

    def ffn(self, l):
        c = self.cfg
        NDC, P, PT, W, WH, NS, NFP, GP, DFF, D = (c[k] for k in ["NDC", "P", "PT", "W", "WH", "NS", "NFP", "GP", "DFF", "D"])
        NG = NFP // GP
        UW = 2 + P + NS * 10
        tiles = self.tiles(2)
        own_tiles = self.tiles(0)
        ub1 = self.f32(2, UW); ubb1 = Buf("ub0")
        ubuf = [ub1, ub1]
        ubb = [ubb1, ubb1]
        cb1 = self.f32(2, W); cbb1 = Buf("cb0")
        cbuf_ = [cb1, cb1]
        cbb = [cbb1, cbb1]
        gT = [self.bf16(GP, W) for _ in range(2)]
        gTb = [[Buf("g%d_%d" % (i, p)) for p in range(GP)] for i in range(2)]
        ust = [self.f32(2 * GP, 34) for _ in range(2)]
        ustb = [Buf("ust0"), Buf("ust1")]
        os1 = self.f32(2, GP * 128); osb1 = Buf("os0")
        ostg = [os1, os1]
        ostb = [osb1, osb1]
        hsb_sb = [self.f32(GP * 4 * NS) for _ in range(2)]
        hsb_b = [Buf("hsb0"), Buf("hsb1")]
        ss1 = self.f32(2, GP * 128); ssb1 = Buf("ss0")
        sstg = [ss1, ss1]
        sstb = [ssb1, ssb1]
        stf = self.d["st_f"].ap()
        nfs = self.d["nf_s"].ap()
        nfp = self.d["nf_p"].ap()
        dcols = min(D, SLOT // GP)
        ndh = D // dcols
        pend_down = None
        pi = 0
        for g in range(NG + 1):
            if g < NG:
                gi = g % 2
                ss, ssb = sstg[gi], sstb[gi]
                for k in range(2):
                    src = stf[l][:, :, k * DFF + g * GP * 128:k * DFF + (g + 1) * GP * 128].rearrange("s r c -> (s r) c")
                    self.dma("sp", ss[0:2 * NS, k, :], src, [], [ssb])
                hps, hpb = hsb_sb[gi], hsb_b[gi]

                def hist_block():
                    hps_, hpb_ = self.ps()
                    for p_ in range(GP):
                        for k_ in range(2):
                            q_ = p_ * 2 + k_
                            self.tr(hps_[:, q_ * 2 * NS:(q_ + 1) * 2 * NS], ss[0:2 * NS, k_, p_ * 128:(p_ + 1) * 128], 2 * NS, [ssb], [hpb_])
                    self.copy("act", hps[:, 0:GP * 4 * NS], hps_[:, 0:GP * 4 * NS], [hpb_], [hpb])
                for p in range(GP):
                    j = g * GP + p
                    ui = pi % 2
                    pi += 1
                    ub, ubf = ubuf[ui], ubb[ui]
                    slab, sb = self.wget(("up", l, j))
                    sv = slab.rearrange("p (a b) -> p a b", b=256)
                    for k in range(2):
                        for (c0, w) in tiles:
                            ps, pb = self.ps()
                            for dc in range(NDC):
                                self.mm(ps[:, 0:w], sv[:, dc, k * 128:(k + 1) * 128], self.hT[:, dc, c0:c0 + w],
                                        dc == 0, dc == NDC - 1, [sb, self.hb[dc]], [pb])
                            pp, sp = self.split(c0, w)
                            if pp:
                                self.copy("act", ub[:, k, pp[0] - (HO - 2):pp[0] - (HO - 2) + pp[1]], ps[:, 0:pp[1]], [pb], [ubf])
                            if sp:
                                o0 = sp[0] - c0
                                self.copy("act", ub[:, k, 2 + P:2 + P + NS * 10].rearrange("p (s t) -> p s t", t=10)[:, :, 2:10],
                                          ps[:, o0:o0 + PT].rearrange("p (s t) -> p s t", t=TS), [pb], [ubf])
                    if p == 0:
                        hist_block()
                    for k in range(2):
                        q = p * 2 + k
                        self.copy("act", ub[:, k, 2 + P:2 + P + NS * 10].rearrange("p (s t) -> p s t", t=10)[:, :, 0:2],
                                  hps[:, q * 2 * NS:(q + 1) * 2 * NS].rearrange("p (s r) -> p s r", r=2), [hpb], [ubf])
                    cb, cbf = cbuf_[ui], cbb[ui]
                    for k in range(2):
                        ch = k * NFP + j
                        wv = lambda t: self.vcol("f_w_dw", (l * 3 + t) * 2 * NFP + ch)
                        bv = self.vcol("f_b_dw", l * 2 * NFP + ch)
                        for seg in range(2):
                            if seg == 0:
                                src = lambda t: ub[:, k, t:t + P]
                                dst = cb[:, k, 0:P]
                            else:
                                sview = ub[:, k, 2 + P:2 + P + NS * 10].rearrange("p (s t) -> p s t", t=10)
                                src = lambda t: sview[:, :, t:t + TS]
                                dst = cb[:, k, P:W].rearrange("p (s t) -> p s t", t=TS)
                            self.ts("dve", dst, src(0), wv(0), bv, ALU.mult, ALU.add, [ubf, self.cbuf], [cbf])
                            self.stt("dve", dst, src(1), wv(1), dst, ALU.mult, ALU.add, [ubf, cbf, self.cbuf], [cbf])
                            self.stt("dve", dst, src(2), wv(2), dst, ALU.mult, ALU.add, [ubf, cbf, self.cbuf], [cbf])
                    self.act(cb[:, 0, :], cb[:, 0, :], AF.Silu, [cbf], [cbf])
                    self.tt("dve", gT[gi][:, p, :], cb[:, 0, :], cb[:, 1, :], ALU.mult, [cbf], [gTb[gi][p]])
                    us, usb = ust[gi], ustb[gi]
                    for k in range(2):
                        self.copy("act", us[:, p * 2 + k, 0:2], ub[:, k, P:P + 2], [ubf], [usb])
                        self.copy("act", us[:, p * 2 + k, 2:2 + 2 * NS].rearrange("p (s r) -> p s r", r=2),
                                  ub[:, k, 2 + P:2 + P + NS * 10].rearrange("p (s t) -> p s t", t=10)[:, :, 8:10], [ubf], [usb])
            if pend_down is not None:
                pg = pend_down
                pgi = pg % 2
                for h in range(ndh):
                    slab, sb = self.wget(("down", l, pg, h))
                    sv = slab.rearrange("p (a b) -> p a b", b=dcols)
                    for dl in range(dcols // 128):
                        dc = h * (dcols // 128) + dl
                        for (c0, w) in own_tiles:
                            ps, pb = self.ps()
                            for p in range(GP):
                                self.mm(ps[:, 0:w], sv[:, p, dl * 128:(dl + 1) * 128], gT[pgi][:, p, c0 - HO:c0 - HO + w],
                                        p == 0, p == GP - 1, [sb, gTb[pgi][p]], [pb])
                            self.tt("dve", self.xT[:, dc, c0 - HO:c0 - HO + w], self.xT[:, dc, c0 - HO:c0 - HO + w], ps[:, 0:w],
                                    ALU.add, [pb, self.xb[dc]], [self.xb[dc]])
            if g < NG:
                us, usb = ust[gi], ustb[gi]
                og, ogb = ostg[gi], ostb[gi]
                n34 = 2 + 2 * NS
                for k in range(2):
                    ps, pb = self.ps()
                    for p in range(GP):
                        self.tr(ps[0:n34, p * 128:(p + 1) * 128], us[:, p * 2 + k, 0:n34], 128, [usb], [pb])
                    self.copy("act", og[0:n34, k, :], ps[0:n34, 0:GP * 128], [pb], [ogb])
                    col = k * DFF + g * GP * 128
                    self.dma("sp", nfp[l][:, col:col + GP * 128], og[0:2, k, :], [ogb], [])
                    self.dma("sp", nfs[l][:, :, col:col + GP * 128].rearrange("s r c -> (s r) c"), og[2:n34, k, :], [ogb], [])
            pend_down = g if g < NG else None

    def stage_staging(self):
        self.rstg = self.f32(self.cfg["D"])
        self.rstgb = Buf("rowstage")

    def rows_out(self, srcf, n, dst_rows_fn):
        c = self.cfg
        NDC, D = c["NDC"], c["D"]
        st, stb = self.rstg, self.rstgb
        for q in range(NDC // 4):
            ps, pb = self.ps()
            for k in range(4):
                dc = q * 4 + k
                ap, bufs = srcf(dc)
                self.tr(ps[0:n, k * 128:(k + 1) * 128], ap, 128, bufs, [pb])
            self.copy("act", st[0:n, q * 512:(q + 1) * 512], ps[0:n, 0:512], [pb], [stb])
        dst_rows_fn(st, stb)

    def rows_in(self, src_rows, n, dst_fn):
        c = self.cfg
        NDC, D = c["NDC"], c["D"]
        st, stb = self.rstg, self.rstgb
        self.dma("sp", st[0:n, :], src_rows, [], [stb])
        for q in range(NDC // 4):
            ps, pb = self.ps()
            for k in range(4):
                dc = q * 4 + k
                self.tr(ps[:, k * 128:k * 128 + n], st[0:n, dc * 128:(dc + 1) * 128], n, [stb], [pb])
            for k in range(4):
                dst_fn(q * 4 + k, ps[:, k * 128:k * 128 + n], pb)

    def mixer_a(self, ja):
        c = self.cfg
        NDC, P, PT, W, WH, NS, D = (c[k] for k in ["NDC", "P", "PT", "W", "WH", "NS", "D"])
        SW = HO + TS
        VW = HO + P + NS * SW
        self.stage_staging()
        self.spill_x()
        vbuf = self.bf16(NDC, VW)
        vb = [Buf("v%d" % i) for i in range(NDC)]
        vf = self.f32(NDC, HO + PT)
        vfb = [Buf("vf%d" % i) for i in range(NDC)]
        sig = [self.f32(512) for _ in range(2)]
        sgb = [Buf("sig0"), Buf("sig1")]
        self.use_scr = False
        sta, nas, nap = self.d["st_a"].ap(), self.d["na_s"].ap(), self.d["na_p"].ap()
        for s0 in range(0, NS, 4):
            ns = min(4, NS - s0)

            def dst_fn(dc, psap, pb, s0=s0, ns=ns):
                self.copy("act", vbuf[:, dc, HO + P + s0 * SW:HO + P + (s0 + ns) * SW].rearrange("p (s t) -> p s t", t=SW)[:, :, 0:HO],
                          psap.rearrange("p (s t) -> p s t", t=HO), [pb], [vb[dc]])
            self.rows_in(sta[ja][s0:s0 + ns].rearrange("s r c -> (s r) c"), ns * HO, dst_fn)
        for s in range(NS):
            self.dma("sp", nas[ja][s, 0:HO - TS, :], sta[ja][s, TS:HO, :], [], [])
        tiles = self.tiles(HO)
        si = 0
        for dc in range(NDC):
            slab, sb = self.wget(("pw1", ja, dc))
            sv = slab.rearrange("p (a b) -> p a b", b=256)
            for (c0, w) in tiles:
                pa, pab = self.ps()
                pg, pgb = self.ps()
                for kdc in range(NDC):
                    self.mm(pa[:, 0:w], sv[:, kdc, 0:128], self.hT[:, kdc, c0:c0 + w], kdc == 0, kdc == NDC - 1, [sb, self.hb[kdc]], [pab])
                for kdc in range(NDC):
                    self.mm(pg[:, 0:w], sv[:, kdc, 128:256], self.hT[:, kdc, c0:c0 + w], kdc == 0, kdc == NDC - 1, [sb, self.hb[kdc]], [pgb])
                sg, sgbuf = sig[si % 2], sgb[si % 2]
                si += 1
                self.act(sg[:, 0:w], pg[:, 0:w], AF.Sigmoid, [pgb], [sgbuf])
                pp, sp = self.split(c0, w)
                if pp:
                    self.tt("dve", vbuf[:, dc, pp[0]:pp[0] + pp[1]], pa[:, 0:pp[1]], sg[:, 0:pp[1]], ALU.mult, [pab, sgbuf], [vb[dc]])
                    t0 = max(pp[0], P)
                    t1 = pp[0] + pp[1]
                    if t1 > t0:
                        self.tt("dve", vf[:, dc, t0 - P:t1 - P], pa[:, t0 - c0:t1 - c0], sg[:, t0 - c0:t1 - c0], ALU.mult, [pab, sgbuf], [vfb[dc]])
                if sp:
                    o0 = sp[0] - c0
                    self.tt("dve", vbuf[:, dc, HO + P:VW].rearrange("p (s t) -> p s t", t=SW)[:, :, HO:SW],
                            pa[:, o0:o0 + PT].rearrange("p (s t) -> p s t", t=TS), sg[:, o0:o0 + PT].rearrange("p (s t) -> p s t", t=TS),
                            ALU.mult, [pab, sgbuf], [vb[dc]])
                    self.tt("dve", vf[:, dc, HO:HO + PT], pa[:, o0:o0 + PT], sg[:, o0:o0 + PT], ALU.mult, [pab, sgbuf], [vfb[dc]])
        self.rows_out(lambda dc: (vf[:, dc, 0:HO], [vfb[dc]]), HO,
                      lambda st, stb: self.dma("sp", nap[ja], st[0:HO, :], [stb], []))
        def samp_out(st, stb):
            for s in range(NS):
                self.dma("sp", nas[ja][s, HO - TS:HO, :], st[s * TS:(s + 1) * TS, :], [stb], [])
        self.rows_out(lambda dc: (vf[:, dc, HO:HO + PT], [vfb[dc]]), PT, samp_out)
        top_conv = self.top
        diag = [self.bf16(CW, 128) for _ in range(2)]
        dgb = [Buf("dg0"), Buf("dg1")]
        acc = [self.f32(W) for _ in range(2)]
        accb = [Buf("acc0"), Buf("acc1")]
        own = self.tiles(0)
        cT, cb = self.hT, self.hb
        for dc in range(NDC):
            dg, dgbuf = diag[dc % 2], dgb[dc % 2]
            for j in range(CW):
                if j % 4 != 0:
                    self.act(dg[:, j, :], self.ident, AF.Copy, [self.cbuf], [dgbuf], scale=self.vcol("a_w_dw", (ja * CW + j) * NDC + dc))
            ac, acb_ = acc[dc % 2], accb[dc % 2]
            wcol = lambda j: self.vcol("a_w_dw", (ja * CW + j) * NDC + dc)
            sview = vbuf[:, dc, HO + P:VW].rearrange("p (s t) -> p s t", t=SW)
            acs = ac[:, P:W].rearrange("p (s t) -> p s t", t=TS)
            odd = list(range(0, CW, 4))
            for n_, j in enumerate(odd):
                if n_ == 0:
                    self.ts("dve", ac[:, 0:P], vbuf[:, dc, j:j + P], wcol(j), self.vcol("a_b_dw", ja * NDC + dc), ALU.mult, ALU.add,
                            [vb[dc], self.cbuf], [acb_])
                    self.ts("dve", acs, sview[:, :, j:j + TS], wcol(j), self.vcol("a_b_dw", ja * NDC + dc), ALU.mult, ALU.add,
                            [vb[dc], self.cbuf], [acb_])
                else:
                    self.stt("dve", ac[:, 0:P], vbuf[:, dc, j:j + P], wcol(j), ac[:, 0:P], ALU.mult, ALU.add, [vb[dc], acb_, self.cbuf], [acb_])
                    self.stt("dve", acs, sview[:, :, j:j + TS], wcol(j), acs, ALU.mult, ALU.add, [vb[dc], acb_, self.cbuf], [acb_])
            even = [j for j in range(CW) if j % 4 != 0]
            for (c0, w) in own:
                ps, pb = self.ps()
                pp, sp = self.split(c0, w)
                if pp:
                    for n_, j in enumerate(even):
                        a0 = pp[0] - HO + j
                        self.mm(ps[:, 0:pp[1]], dg[:, j, :], vbuf[:, dc, a0:a0 + pp[1]], n_ == 0, n_ == len(even) - 1, [dgbuf, vb[dc]], [pb])
                if sp:
                    o0 = sp[0] - c0
                    for n_, j in enumerate(even):
                        self.mm(ps[:, o0:o0 + PT].rearrange("p (s t) -> p s t", t=TS), dg[:, j, :], sview[:, :, j:j + TS],
                                n_ == 0, n_ == len(even) - 1, [dgbuf, vb[dc]], [pb])
                self.tt("dve", cT[:, dc, c0:c0 + w], ps[:, 0:w], ac[:, c0 - HO:c0 - HO + w], ALU.add, [pb, acb_], [cb[dc]])
        self.restore_x()
        self.top = top_conv
        mu = self.f32(W); mub = Buf("mu")
        rs = self.f32(W); rsb = Buf("rs")
        sq = [self.bf16(W) for _ in range(2)]; sqb = [Buf("lsq0"), Buf("lsq1")]
        tl = [(c0, w) + self.ps() + self.ps() for (c0, w) in own]
        for dc in range(NDC):
            self.act(sq[dc % 2], cT[:, dc, HO:HO + W], AF.Square, [cb[dc]], [sqb[dc % 2]])
            for (c0, w, p1, p1b, p2, p2b) in tl:
                self.mm(p1[:, 0:w], self.ones_bf, cT[:, dc, c0:c0 + w], dc == 0, dc == NDC - 1, [cb[dc], self.cbuf], [p1b])
                self.mm(p2[:, 0:w], self.ones_bf, sq[dc % 2][:, c0 - HO:c0 - HO + w], dc == 0, dc == NDC - 1, [sqb[dc % 2], self.cbuf], [p2b])
        for (c0, w, p1, p1b, p2, p2b) in tl:
            a, b_ = c0 - HO, c0 - HO + w
            self.act(mu[:, a:b_], p1[:, 0:w], AF.Copy, [p1b], [mub], scale=1.0 / D)
            self.tt("dve", rs[:, a:b_], mu[:, a:b_], mu[:, a:b_], ALU.mult, [mub], [rsb])
            self.stt("dve", rs[:, a:b_], p2[:, 0:w], 1.0 / D, rs[:, a:b_], ALU.mult, ALU.subtract, [p2b, rsb], [rsb])
        self.act(rs[:, 0:W], rs[:, 0:W], AF.Sqrt, [rsb], [rsb], bias=EPS)
        self.S.op("dve", lambda e: e.reciprocal(rs[:, 0:W], rs[:, 0:W]), [rsb], [rsb])
        tmp = [self.f32(W) for _ in range(2)]; tmb = [Buf("lt0"), Buf("lt1")]
        for dc in range(NDC):
            t, tb = tmp[dc % 2], tmb[dc % 2]
            self.tt("dve", t[:, 0:W], cT[:, dc, HO:HO + W], mu[:, 0:W], ALU.subtract, [cb[dc], mub], [tb])
            self.tt("dve", t[:, 0:W], t[:, 0:W], rs[:, 0:W], ALU.mult, [tb, rsb], [tb])
            self.act(cT[:, dc, HO:HO + W], t[:, 0:W], AF.Silu, [tb, self.cbuf], [cb[dc]],
                     bias=self.vcol("a_ln_b", ja * NDC + dc), scale=self.vcol("a_ln_g", ja * NDC + dc))
        self.proj_residual("pw2", ja, cT, cb)

    def proj_residual(self, key, j, src, srcb, scale_name=None):
        c = self.cfg
        NDC = c["NDC"]
        own = self.tiles(0)
        for dc2 in range(NDC // 2):
            slab, sb = self.wget((key, j, dc2) if j is not None else (key, dc2))
            sv = slab.rearrange("p (a b) -> p a b", b=256)
            for k in range(2):
                dco = dc2 * 2 + k
                for (c0, w) in own:
                    ps, pb = self.ps()
                    for kdc in range(NDC):
                        self.mm(ps[:, 0:w], sv[:, kdc, k * 128:(k + 1) * 128], src[:, kdc, c0:c0 + w], kdc == 0, kdc == NDC - 1,
                                [sb, srcb[kdc]], [pb])
                    xs = self.xT[:, dco, c0 - HO:c0 - HO + w]
                    self.tt("dve", xs, xs, ps[:, 0:w], ALU.add, [pb, self.xb[dco]], [self.xb[dco]])

    def mixer_b(self):
        c = self.cfg
        NDC, P, PT, W, WH, NS, D = (c[k] for k in ["NDC", "P", "PT", "W", "WH", "NS", "D"])
        NG = NDC // 4
        SW = PH + TS
        LP = PH + P
        LB = LP + NS * SW
        stb_, nbs, nbp = self.d["st_b"].ap(), self.d["nb_s"].ap(), self.d["nb_p"].ap()
        tail = self.f32(NDC, PH); tlb = [Buf("tl%d" % i) for i in range(NDC)]
        for dc in range(NDC):
            self.stt("dve", tail[:, dc, :], self.xT[:, dc, P - PH:P], self.vcol("norm_mix", 1 * NDC + dc), self.rstd[:, P - PH:P],
                     ALU.mult, ALU.mult, [self.xb[dc], self.rstd_b, self.cbuf], [tlb[dc]])
        halo = self.f32(NDC, PH); hlb = [Buf("hl%d" % i) for i in range(NDC)]
        self.handoff(tail, tlb, PH, PH, halo, hlb, 0, F32)
        self.stage_staging()
        self.rows_out(lambda dc: (tail[:, dc, :], [tlb[dc]]), PH, lambda st, sb: self.dma("sp", nbp[0], st[0:PH, :], [sb], []))
        for s in range(NS):
            self.dma("sp", nbs[0][s, 0:PH - TS, :], stb_[0][s, TS:PH, :], [], [])
        hist = self.f32(NDC, NS * PH); hib = [Buf("hi%d" % i) for i in range(NDC)]
        for s0 in range(0, NS, 8):
            ns = min(8, NS - s0)

            def dst_fn(dc, psap, pb, s0=s0, ns=ns):
                self.copy("act", hist[:, dc, s0 * PH:(s0 + ns) * PH], psap, [pb], [hib[dc]])
            self.rows_in(stb_[0][s0:s0 + ns].rearrange("s r c -> (s r) c"), ns * PH, dst_fn)
        hs = self.f32(NDC, PT); hsb = [Buf("hs%d" % i) for i in range(NDC)]
        for dc in range(NDC):
            self.stt("dve", hs[:, dc, :], self.xT[:, dc, P:W], self.vcol("norm_mix", 1 * NDC + dc), self.rstd[:, P:W],
                     ALU.mult, ALU.mult, [self.xb[dc], self.rstd_b, self.cbuf], [hsb[dc]])
        def samp_out(st, sb):
            for s in range(NS):
                self.dma("sp", nbs[0][s, PH - TS:PH, :], st[s * TS:(s + 1) * TS, :], [sb], [])
        self.rows_out(lambda dc: (hs[:, dc, :], [hsb[dc]]), PT, samp_out)
        pooled, plb = self.hT, self.hb
        hf = self.f32(LB); hfb = Buf("hf")
        sA = self.f32(LB); sAb = Buf("sA")
        sB = self.f32(LB); sBb = Buf("sB")
        own = self.tiles(0)
        for g in range(4):
            wdw = (2, 4, 8, 16)[g]
            for dl in range(NG):
                dc = g * NG + dl
                gcol = self.vcol("norm_mix", 1 * NDC + dc)
                self.copy("act", hf[:, 0:PH], halo[:, dc, :], [hlb[dc]], [hfb])
                self.stt("dve", hf[:, PH:LP], self.xT[:, dc, 0:P], gcol, self.rstd[:, 0:P], ALU.mult, ALU.mult,
                         [self.xb[dc], self.rstd_b, self.cbuf], [hfb])
                sv = hf[:, LP:LB].rearrange("p (s t) -> p s t", t=SW)
                self.copy("act", sv[:, :, 0:PH], hist[:, dc, :].rearrange("p (s t) -> p s t", t=PH), [hib[dc]], [hfb])
                self.copy("act", sv[:, :, PH:SW], hs[:, dc, :].rearrange("p (s t) -> p s t", t=TS), [hsb[dc]], [hfb])
                cur, curb = hf, hfb
                sh = 1
                bufs = [(sA, sAb), (sB, sBb)]
                lvl = 0
                while sh < wdw:
                    nxt, nxtb = bufs[lvl % 2]
                    self.tt("dve", nxt[:, sh:LP], cur[:, sh:LP], cur[:, 0:LP - sh], ALU.add, [curb], [nxtb])
                    cs = cur[:, LP:LB].rearrange("p (s t) -> p s t", t=SW)
                    ns_ = nxt[:, LP:LB].rearrange("p (s t) -> p s t", t=SW)
                    self.tt("dve", ns_[:, :, sh:SW], cs[:, :, sh:SW], cs[:, :, 0:SW - sh], ALU.add, [curb], [nxtb])
                    cur, curb = nxt, nxtb
                    sh *= 2
                    lvl += 1
                self.stt("dve", pooled[:, dc, HO + 16:HO + P], cur[:, PH + 16:LP], 1.0 / wdw, hf[:, PH + 16:LP], ALU.mult, ALU.subtract,
                         [curb, hfb], [plb[dc]])
                rc = self.vecs[:, self.voff["rc"] + g * 16:self.voff["rc"] + (g + 1) * 16]
                self.tt("dve", sB[:, 0:16] if cur is not sB else sA[:, 0:16], cur[:, PH:PH + 16], rc, ALU.mult, [curb, self.cbuf],
                        [sBb if cur is not sB else sAb])
                o16 = sB if cur is not sB else sA
                o16b = sBb if cur is not sB else sAb
                self.tt("dve", pooled[:, dc, HO:HO + 16], o16[:, 0:16], hf[:, PH:PH + 16], ALU.subtract, [o16b, hfb], [plb[dc]])
                cs = cur[:, LP:LB].rearrange("p (s t) -> p s t", t=SW)
                hv = hf[:, LP:LB].rearrange("p (s t) -> p s t", t=SW)
                self.stt("dve", pooled[:, dc, HO + P:HO + W].rearrange("p (s t) -> p s t", t=TS), cs[:, :, PH:SW], 1.0 / wdw, hv[:, :, PH:SW],
                         ALU.mult, ALU.subtract, [curb, hfb], [plb[dc]])
            slab, sb = self.wget(("grp", g))
            sv = slab[:, 0:NG * NG * 128].rearrange("p (a b) -> p a b", b=NG * 128)
            for dl in range(NG):
                dco = g * NG + dl
                for (c0, w) in own:
                    ps, pb = self.ps()
                    for k in range(NG):
                        self.mm(ps[:, 0:w], sv[:, k, dl * 128:(dl + 1) * 128], pooled[:, g * NG + k, c0:c0 + w], k == 0, k == NG - 1,
                                [sb, plb[g * NG + k]], [pb])
                    xs = self.xT[:, dco, c0 - HO:c0 - HO + w]
                    self.stt("dve", xs, ps[:, 0:w], self.vcol("b_scale", dco), xs, ALU.mult, ALU.add, [pb, self.xb[dco], self.cbuf], [self.xb[dco]])

    def mixer_c(self):
        c = self.cfg
        NDC, P, PT, W, WH, NS, D, CH = (c[k] for k in ["NDC", "P", "PT", "W", "WH", "NS", "D", "CH"])
        nc = self.nc
        H = NDC
        stc, ncs, ncp = self.d["st_c"].ap(), self.d["nc_s"].ap(), self.d["nc_p"].ap()
        oN = self.bf16(NDC, WH); onb = [Buf("on%d" % i) for i in range(NDC)]
        self.spill_x()
        nf = lambda nm: (self.f32(W), Buf(nm))
        A2 = [nf("A0"), nf("A1")]; B, Bb = nf("B"); X1, X1b = nf("X1"); X2, X2b = nf("X2"); Fb, Fbb = nf("F")
        G3 = [(self.bf16(W), Buf("G%d" % i)) for i in range(3)]
        E2 = [nf("E0"), nf("E1")]
        qt = self.bf16(W); qtb = Buf("qt")
        kt = self.bf16(W); ktb = Buf("kt")
        qh2 = [(self.bf16(P), Buf("qh0")), (self.bf16(P), Buf("qh1"))]
        osq = self.bf16(W); osqb = Buf("osq")
        rs = self.f32(W); rsb = Buf("rs")
        lb = self.f32(NDC); lbb = Buf("lb")
        oml = self.f32(NDC)
        ex = self.f32(4 * NDC)
        Sin = self.f32(NS, 128); Sinb = Buf("Sin")
        Sout = self.f32(NS, 128); Soutb = Buf("Sout")
        NR = 6
        Sf = [self.f32(128) for _ in range(NR)]; Sfb = [Buf("Sf%d" % i) for i in range(NR)]
        Sb = [self.bf16(128) for _ in range(NR)]; Sbb = [Buf("Sb%d" % i) for i in range(NR)]
        k2c = [self.f32(CH) for _ in range(4)]; k2b = [Buf("k2c%d" % i) for i in range(4)]
        kiT = [self.bf16(256) for _ in range(4)]; kib = [Buf("kiT%d" % i) for i in range(4)]
        PTm = [self.bf16(CH) for _ in range(4)]; ptb = [Buf("PT%d" % i) for i in range(4)]
        Send2 = [(self.f32(128), Buf("Se0")), (self.f32(128), Buf("Se1"))]
        Srv2 = [(self.f32(128), Buf("Srv0")), (self.f32(128), Buf("Srv1"))]
        Srb = self.bf16(128); Srbb = Buf("Srb")
        Pc2 = [(self.f32(P // CH + 1), Buf("Pc0")), (self.f32(P // CH + 1), Buf("Pc1"))]
        Sfin = self.f32(128); Sfinb = Buf("Sfin")
        self.use_scr = False
        cl = self.vecs[:, self.voff["c_lb"]:self.voff["c_lb"] + 4 * NDC]
        self.act(ex[:, 0:4 * NDC], cl, AF.Exp, [self.cbuf], [lbb])
        exv = ex[:, 0:4 * NDC].rearrange("p (l c) -> p l c", c=NDC)
        self.tt("dve", lb[:, 0:NDC], exv[:, 1, :], exv[:, 2, :], ALU.add, [lbb], [lbb])
        self.tt("dve", oml[:, 0:NDC], exv[:, 0, :], exv[:, 3, :], ALU.add, [lbb], [lbb])
        self.tt("dve", ex[:, 0:NDC], lb[:, 0:NDC], oml[:, 0:NDC], ALU.add, [lbb], [lbb])
        self.S.op("dve", lambda e: e.reciprocal(ex[:, 0:NDC], ex[:, 0:NDC]), [lbb], [lbb])
        self.tt("dve", lb[:, 0:NDC], lb[:, 0:NDC], ex[:, 0:NDC], ALU.mult, [lbb], [lbb])
        self.tt("dve", oml[:, 0:NDC], oml[:, 0:NDC], ex[:, 0:NDC], ALU.mult, [lbb], [lbb])
        own = self.tiles(0)
        NCH = P // CH

        def finalize(hd):
            par = hd % 2
            E, Eb = E2[par]; G, Gb = G3[hd % 3]; qh, qhb = qh2[par]
            Srv, Srvb = Srv2[par]; Send, Sendb = Send2[par]; Pc, Pcb = Pc2[par]
            self.ts("dve", Srv, Srv, self.vcol("isodd", 0), None, ALU.mult, None, [Srvb, self.cbuf], [Srvb])
            self.copy("act", Srb, Srv, [Srvb], [Srbb])
            self.stt("dve", Sfin, Srv, Pc[:, NCH:NCH + 1], Send, ALU.mult, ALU.add, [Srvb, Pcb, Sendb], [Sfinb])
            self.dma("sp", ncp[0][hd], Sfin, [Sfinb], [])
            for (c0, w) in own:
                pp, sp = self.split(c0, w)
                if pp:
                    a = pp[0] - HO
                    ps, pb = self.ps()
                    self.mm(ps[:, 0:pp[1]], Srb, qh[:, a:a + pp[1]], True, True, [Srbb, qhb], [pb])
                    self.tt("dve", E[:, a:a + pp[1]], E[:, a:a + pp[1]], ps[:, 0:pp[1]], ALU.add, [pb, Eb], [Eb])
            self.act(osq[:, 0:W], E[:, 0:W], AF.Square, [Eb], [osqb])
            for (c0, w) in own:
                ps, pb = self.ps()
                self.mm(ps[:, 0:w], self.ones_bf, osq[:, c0 - HO:c0 - HO + w], True, True, [osqb, self.cbuf], [pb])
                self.act(rs[:, c0 - HO:c0 - HO + w], ps[:, 0:w], AF.Sqrt, [pb], [rsb], bias=EPS, scale=1.0 / 128)
            self.S.op("dve", lambda e: e.reciprocal(rs[:, 0:W], rs[:, 0:W]), [rsb], [rsb])
            self.tt("dve", E[:, 0:W], E[:, 0:W], rs[:, 0:W], ALU.mult, [Eb, rsb], [Eb])
            self.stt("dve", oN[:, hd, HO:HO + W], E[:, 0:W], self.vcol("c_g_norm", hd), G[:, 0:W], ALU.mult, ALU.mult,
                     [Eb, Gb, self.cbuf], [onb[hd]])

        ri = 0

        def make_proj_items(hd):
            par = hd % 2
            Ai, Aib = A2[par]; G, Gb = G3[hd % 3]
            s0, s0b = self.wget(("cin0", hd))
            s1, s1b = self.wget(("cin1", hd))
            v0 = s0.rearrange("p (a b) -> p a b", b=256)
            v1 = s1.rearrange("p (a b) -> p a b", b=256)
            items = []

            def item(sv, sbuf, k, fn, c0, w):
                def run():
                    ps, pb = self.ps()
                    for dc in range(NDC):
                        self.mm(ps[:, 0:w], sv[:, dc, k * 128:(k + 1) * 128], self.hT[:, dc, c0:c0 + w], dc == 0, dc == NDC - 1,
                                [sbuf, self.hb[dc]], [pb])
                    fn(ps[:, 0:w], pb, c0 - HO, w)
                return run
            specs = [
                (v0, s0b, 1, lambda ps, pb, a, w: self.act(X1[:, a:a + w], ps, AF.Sigmoid, [pb], [X1b])),
                (v0, s0b, 0, lambda ps, pb, a, w: self.act(X2[:, a:a + w], ps, AF.Silu, [pb], [X2b])),
                (v1, s1b, 0, lambda ps, pb, a, w: self.copy("act", Ai[:, a:a + w], ps, [pb], [Aib])),
                (v1, s1b, 1, lambda ps, pb, a, w: self.act(G[:, a:a + w], ps, AF.Silu, [pb], [Gb])),
            ]
            for (sv, sbuf, k, fn) in specs:
                for (c0, w) in own:
                    items.append(item(sv, sbuf, k, fn, c0, w))
            return items

        def prep(hd):
            par = hd % 2
            T, Tb = E2[par]
            Pc, Pcb = Pc2[par]
            self.ts("dve", X1[:, 0:W], X1[:, 0:W], oml[:, hd:hd + 1], lb[:, hd:hd + 1], ALU.mult, ALU.add, [X1b, lbb], [X1b])
            self.ts("dve", B[:, 0:W], X1[:, 0:W], -1.0, 1.0, ALU.mult, ALU.add, [X1b], [Bb])
            self.tt("dve", T[:, 0:W], X1[:, 0:W], self.m0, ALU.mult, [X1b, self.cbuf], [Tb])
            self.tt("dve", X1[:, 0:W], X1[:, 0:W], T[:, 0:W], ALU.subtract, [X1b, Tb], [X1b])
            self.S.op("dve", lambda e: e.tensor_tensor_scan(Fb[:, 0:W], T[:, 0:W], X1[:, 0:W], 0.0, ALU.mult, ALU.add), [X1b, Tb], [Fbb])
            self.ts("dve", T[:, 0:W], Fb[:, 0:W], 1e-36, None, ALU.max, None, [Fbb], [Tb])
            self.S.op("dve", lambda e: e.reciprocal(T[:, 0:W], T[:, 0:W]), [Tb], [Tb])
            self.tt("dve", B[:, 0:W], B[:, 0:W], T[:, 0:W], ALU.mult, [Tb, Bb], [Bb])
            self.copy("act", kt[:, 0:W], B[:, 0:W], [Bb], [ktb])
            self.tt("dve", qt[:, 0:W], X2[:, 0:W], Fb[:, 0:W], ALU.mult, [X2b, Fbb], [qtb])
            Fe = Fb[:, 0:P].rearrange("p (c t) -> p c t", t=CH)[:, :, CH - 1]
            self.S.op("dve", lambda e, Pc=Pc: e.memset(Pc[:, 0:1], 1.0), [], [Pcb])
            self.S.op("dve", lambda e, Fe=Fe, Pc=Pc: e.tensor_tensor_scan(Pc[:, 1:NCH + 1], Fe, self.zero[:, 0:NCH], 1.0, ALU.mult, ALU.add),
                      [Fbb, self.cbuf], [Pcb])

        for it in make_proj_items(0):
            it()
        prep(0)
        for hd in range(H):
            par = hd % 2
            E, Eb = E2[par]; G, Gb = G3[hd % 3]; qh, qhb = qh2[par]; A, Ab = A2[par]
            Srv, Srvb = Srv2[par]; Send, Sendb = Send2[par]; Pc, Pcb = Pc2[par]
            pending = make_proj_items(hd + 1) if hd + 1 < H else []
            self.dma("sp", Sin[:, 0:NS, :], stc[0][:, hd, :, :].rearrange("s k v -> k s v"), [], [Sinb])
            psteps = [("p", ci, ci * CH, CH) for ci in range(NCH)]
            ssteps = [("s", s, P + s * TS, TS) for s in range(NS)]
            steps = []
            for i_ in range(max(len(psteps), len(ssteps))):
                if i_ < len(psteps):
                    steps.append(psteps[i_])
                if i_ < len(ssteps):
                    steps.append(ssteps[i_])
            st = {"cur": None}
            ctx = []

            def S1(t):
                kind, ci, a, C = steps[t]
                d_ = {"i4": t % 4}
                i4 = d_["i4"]
                fend = Fb[:, a + C - 1:a + C]
                self.ts("dve", k2c[i4][:, 0:C], B[:, a:a + C], fend, None, ALU.mult, None, [Bb, Fbb], [k2b[i4]])
                pT, pTb = self.ps()
                self.tr(pT[0:C, 0:128], k2c[i4][:, 0:C], 128, [k2b[i4]], [pTb])
                self.tr(pT[0:C, 128:256], A[:, a:a + C], 128, [Ab], [pTb])
                pS, pSb = self.ps()
                self.mm(pS[0:C, 0:C], kt[:, a:a + C], qt[:, a:a + C], True, True, [ktb, qtb], [pSb])
                d_.update(pT=pT, pTb=pTb, pS=pS, pSb=pSb, fend=fend)
                ctx.append(d_)

            def S2(t):
                kind, ci, a, C = steps[t]
                d_ = ctx[t]
                i4 = d_["i4"]
                self.copy("act", kiT[i4][0:C, 0:256], d_["pT"][0:C, 0:256], [d_["pTb"]], [kib[i4]])
                self.tt("dve", PTm[i4][0:C, 0:C], d_["pS"][0:C, 0:C], self.tri[0:C, 0:C], ALU.mult, [d_["pSb"], self.cbuf], [ptb[i4]])
                if kind == "p":
                    self.ts("dve", qh[:, a:a + C], qt[:, a:a + C], Pc[:, ci:ci + 1], None, ALU.mult, None, [qtb, Pcb], [qhb])
                pD, pDb = self.ps()
                self.mm(pD[:, 0:128], kiT[i4][0:C, 0:128], kiT[i4][0:C, 128:256], True, True, [kib[i4]], [pDb])
                d_.update(pD=pD, pDb=pDb)

            def S3(t):
                nonlocal ri
                kind, ci, a, C = steps[t]
                d_ = ctx[t]
                i4 = d_["i4"]
                fend, pD, pDb = d_["fend"], d_["pD"], d_["pDb"]
                cur = st["cur"]
                if kind == "p":
                    have_S = cur is not None
                    if have_S:
                        sf_in, sfb_in, sb_in, sbb_in = cur
                    r = ri % NR
                    ri += 1
                    nf_, nfb_, nb_, nbb_ = Sf[r], Sfb[r], Sb[r], Sbb[r]
                    if have_S:
                        self.stt("dve", nf_, sf_in, fend, pD[:, 0:128], ALU.mult, ALU.add, [sfb_in, Fbb, pDb], [nfb_])
                    else:
                        self.copy("dve", nf_, pD[:, 0:128], [pDb], [nfb_])
                    self.copy("act", nb_, nf_, [nfb_], [nbb_])
                    st["cur"] = (nf_, nfb_, nb_, nbb_)
                else:
                    have_S = True
                    sf_in, sfb_in = Sin[:, ci, :], Sinb
                    r = ri % NR
                    ri += 1
                    sb_in, sbb_in = Sb[r], Sbb[r]
                    self.copy("act", sb_in, sf_in, [sfb_in], [sbb_in])
                    self.stt("dve", Sout[:, ci, :], sf_in, fend, pD[:, 0:128], ALU.mult, ALU.add, [sfb_in, Fbb, pDb], [Soutb])
                pO, pOb = self.ps()
                self.mm(pO[:, 0:C], kiT[i4][0:C, 128:256], PTm[i4][0:C, 0:C], True, not have_S, [kib[i4], ptb[i4]], [pOb])
                if have_S:
                    self.mm(pO[:, 0:C], sb_in, qt[:, a:a + C], False, True, [sbb_in, qtb], [pOb])
                self.copy("act", E[:, a:a + C], pO[:, 0:C], [pOb], [Eb])

            nst = len(steps)
            every = 10 ** 9
            for t in range(nst + 2):
                if t < nst:
                    S1(t)
                if 1 <= t <= nst:
                    S2(t - 1)
                if t >= 2:
                    S3(t - 2)
            cur = st["cur"]
            self.dma("sp", ncs[0][:, hd, :, :].rearrange("s k v -> k s v"), Sout[:, 0:NS, :], [Soutb], [])
            if hd >= 1:
                finalize(hd - 1)
            self.copy("dve", Send, cur[0], [cur[1]], [Sendb])
            k = self.ncc_tensors
            self.ncc_tensors += 1
            snd = nc.dram_tensor("ss%d" % k, [128, 128], F32)
            rcv = nc.dram_tensor("sr%d" % k, [256, 128], F32)
            sdb, rvb = Buf("ssnd"), Buf("srcv")
            self.dma("pool", snd.ap(), Send, [Sendb], [sdb])
            self.S.cc(lambda e, snd=snd, rcv=rcv: e.collective_compute("AllGather", ALU.bypass, replica_groups=PAIRS, ins=[snd.ap()], outs=[rcv.ap()]),
                      [sdb], [rvb])
            self.dma("sp", Srv, rcv.ap()[0:128, :], [rvb], [Srvb])
            if hd + 1 < H:
                half = len(pending) // 2
                for it in pending[:half]:
                    it()
                prep(hd + 1)
                for it in pending[half:]:
                    it()
        finalize(H - 1)
        self.restore_x()
        self.proj_residual("wo", None, oN, onb)

    def final_out(self):
        c = self.cfg
        NDC, P, PT, W, D = c["NDC"], c["P"], c["PT"], c["W"], c["D"]
        self.sumsq_to_rstd([(self.xT[:, dc, :], self.xb[dc]) for dc in range(NDC)], W, D, self.rstd, self.rstd_b)
        dsts = [(self.d["yp"].ap()[t * 128:(t + 1) * 128, :], 128, t * 128) for t in range(P // 128)]
        dsts.append((self.d["ys"].ap()[:, :], PT, P))
        stg = [self.f32(D) for _ in range(2)]
        sb = [Buf("ystg0"), Buf("ystg1")]
        tmp = [self.f32(128) for _ in range(8)]
        tmb = [Buf("yt%d" % i) for i in range(8)]
        ti = 0
        for i, (dst, n, col) in enumerate(dsts):
            s, b = stg[i % 2], sb[i % 2]
            for q in range(NDC // 4):
                ps, pb = self.ps()
                for k in range(4):
                    dc = q * 4 + k
                    t, tb = tmp[ti % 8], tmb[ti % 8]
                    ti += 1
                    self.stt("dve", t[:, 0:n], self.xT[:, dc, col:col + n], self.vcol("norm_final", dc), self.rstd[:, col:col + n],
                             ALU.mult, ALU.mult, [self.xb[dc], self.rstd_b, self.cbuf], [tb])
                    self.tr(ps[0:n, k * 128:(k + 1) * 128], t[:, 0:n], 128, [tb], [pb])
                self.copy("act", s[0:n, q * 512:(q + 1) * 512], ps[0:n, 0:512], [pb], [b])
            self.dma("sp", dst, s[0:n, :], [b], [])


_CACHE = {}


def run(cfg, inp):
    D, P, NS, NDC, DFF = cfg["D"], cfg["P"], cfg["NS"], cfg["NDC"], cfg["DFF"]
    inp = {k: np.asarray(v) for k, v in inp.items()}
    wslab, sidx = build_slabs(cfg, inp)
    consts = build_consts(cfg)
    key = tuple(sorted(cfg.items()))
    if key not in _CACHE:
        _CACHE[key] = Builder(cfg, sidx, wslab.shape[0]).build()
    nc = _CACHE[key]
    B = inp["x_prompt"].shape[0]
    assert 2 * B == NCORES and inp["x_prompt"].shape[1] == 2 * P and inp["x_sample"].shape[0] == NCORES * NS
    in_maps = []
    for c in range(NCORES):
        sq, hf = c // 2, c % 2
        sl = slice(c * NS, (c + 1) * NS)
        in_maps.append({
            "xp": np.ascontiguousarray(inp["x_prompt"][sq, hf * P:(hf + 1) * P]),
            "xs": np.ascontiguousarray(inp["x_sample"][sl]).reshape(NS * TS, D),
            "st_a": np.ascontiguousarray(inp["state_conv_a"][:, sl]),
            "st_b": np.ascontiguousarray(inp["state_pool"][:, sl]),
            "st_c": np.ascontiguousarray(inp["state_hgrn"][:, sl]),
            "st_f": np.ascontiguousarray(inp["state_ffn_conv"][:, sl]),
            "wslab": wslab, "vecs": build_vecs(cfg, inp, c), "consts": consts,
        })
    res = run_bass_kernel_spmd(nc, in_maps, core_ids=list(range(NCORES)))
    R = res.results
    f = np.float32
    y_p = np.zeros((B, 2 * P, D), f); y_s = np.zeros((NCORES * NS, TS, D), f)
    na_p = np.zeros((2, B, HO, D), f); na_s = np.zeros((2, NCORES * NS, HO, D), f)
    nb_p = np.zeros((1, B, PH, D), f); nb_s = np.zeros((1, NCORES * NS, PH, D), f)
    nc_p = np.zeros((1, B, NDC, 128, 128), f); nc_s = np.zeros((1, NCORES * NS, NDC, 128, 128), f)
    nf_p = np.zeros((4, B, 2, 2 * DFF), f); nf_s = np.zeros((4, NCORES * NS, 2, 2 * DFF), f)
    for c in range(NCORES):
        sq, hf = c // 2, c % 2
        sl = slice(c * NS, (c + 1) * NS)
        r = R[c]
        y_p[sq, hf * P:(hf + 1) * P] = r["yp"]
        y_s[sl] = r["ys"].reshape(NS, TS, D)
        na_s[:, sl] = r["na_s"]; nb_s[:, sl] = r["nb_s"]; nc_s[:, sl] = r["nc_s"]; nf_s[:, sl] = r["nf_s"]
        if hf == 1:
            na_p[:, sq] = r["na_p"]; nb_p[:, sq] = r["nb_p"]; nc_p[:, sq] = r["nc_p"]; nf_p[:, sq] = r["nf_p"]
    return (y_p, y_s, na_p, na_s, nb_p, nb_s, nc_p, nc_s, nf_p, nf_s)


def kernel(**inputs):
    return run(make_cfg(), inputs)
```

```python
import numpy as np
import ml_dtypes
import concourse.bass as bass
import concourse.mybir as mybir
from concourse.bass_utils import run_bass_kernel_spmd

F32 = mybir.dt.float32
BF16 = mybir.dt.bfloat16
AF = mybir.ActivationFunctionType
ALU = mybir.AluOpType

NCORES = 8
EPS = 1e-6
HO = 30
CW = 31
PH = 15
TS = 8
SLOT = 4096
NSLOT = 3
PAIRS = [[0, 1], [2, 3], [4, 5], [6, 7]]


def make_cfg(D=2048, DFF=5632, P=1024, NS=16, GP=4, CH=64):
    c = dict(D=D, DFF=DFF, P=P, NS=NS, GP=GP, CH=CH)
    c["NDC"] = D // 128
    c["NFP"] = DFF // 128
    c["PT"] = NS * TS
    c["W"] = P + NS * TS
    c["WH"] = HO + c["W"]
    assert c["NFP"] % GP == 0 and c["NDC"] % 4 == 0 and P % CH == 0 and c["PT"] <= 128
    assert (GP * 128) % 1 == 0
    return c


def slab_pair(Wm, c0, c1):
    K = Wm.shape[0]
    a = Wm[:, c0:c0 + 128].reshape(K // 128, 128, 128)
    b = Wm[:, c1:c1 + 128].reshape(K // 128, 128, 128)
    s = np.concatenate([a, b], axis=2)
    return np.ascontiguousarray(s.transpose(1, 0, 2)).reshape(128, -1)


def slab_rows(Wm, r0, nrc, c0, ncol):
    s = Wm[r0:r0 + nrc * 128, c0:c0 + ncol].reshape(nrc, 128, ncol)
    return np.ascontiguousarray(s.transpose(1, 0, 2)).reshape(128, -1)


def build_slabs(cfg, inp):
    D, DFF, NDC, NFP, GP = cfg["D"], cfg["DFF"], cfg["NDC"], cfg["NFP"], cfg["GP"]
    slabs = []
    idx = {}

    def add(key, arr):
        pad = np.zeros((128, SLOT), np.float32)
        pad[:, :arr.shape[1]] = arr
        idx[key] = len(slabs)
        slabs.append(pad)

    def ffn(l):
        wu, wd = inp["f_w_up"][l], inp["f_w_down"][l]
        ng = NFP // GP
        dcols = min(D, SLOT // GP)
        ndh = D // dcols
        for g in range(ng + 1):
            if g < ng:
                for p in range(GP):
                    j = g * GP + p
                    add(("up", l, j), slab_pair(wu, j * 128, DFF + j * 128))
            if g >= 1:
                for h in range(ndh):
                    add(("down", l, g - 1, h), slab_rows(wd, (g - 1) * GP * 128, GP, h * dcols, dcols))

    def mixA(j):
        for dc in range(NDC):
            add(("pw1", j, dc), slab_pair(inp["a_w_pw1"][j], dc * 128, D + dc * 128))
        for dc2 in range(NDC // 2):
            add(("pw2", j, dc2), slab_rows(inp["a_w_pw2"][j], 0, NDC, dc2 * 256, 256))

    def mixB():
        ng = NDC // 4
        for g in range(4):
            add(("grp", g), slab_rows(inp["b_w_grp"][0][g], 0, ng, 0, ng * 128))

    def mixC():
        w = inp["c_w_in"][0]
        for hd in range(NDC):
            add(("cin0", hd), slab_pair(w, hd * 128, D + hd * 128))
            add(("cin1", hd), slab_pair(w, 2 * D + hd * 128, 3 * D + hd * 128))
        for dc2 in range(NDC // 2):
            add(("wo", dc2), slab_rows(inp["c_w_o"][0], 0, NDC, dc2 * 256, 256))

    mixA(0); ffn(0); mixB(); ffn(1); mixC(); ffn(2); mixA(1); ffn(3)
    return np.stack(slabs, 0), idx


def vec_layout(cfg):
    NDC, NFP = cfg["NDC"], cfg["NFP"]
    off = {}
    n = 0
    for name, cnt in [("norm_mix", 4 * NDC), ("norm_ffn", 4 * NDC), ("norm_final", NDC), ("a_b_dw", 2 * NDC),
                      ("a_ln_g", 2 * NDC), ("a_ln_b", 2 * NDC), ("a_w_dw", 2 * CW * NDC), ("b_scale", NDC),
                      ("c_lb", 4 * NDC), ("c_g_norm", NDC), ("f_w_dw", 4 * 3 * 2 * NFP), ("f_b_dw", 4 * 2 * NFP),
                      ("isodd", 1), ("rc", 4 * 16)]:
        off[name] = n
        n += cnt
    return off, n


def colize(v):
    sh = v.shape
    C = sh[-1] // 128
    a = v.reshape(sh[:-1] + (C, 128))
    a = np.moveaxis(a, -1, 0)
    return np.ascontiguousarray(a).reshape(128, -1)


def build_vecs(cfg, inp, core):
    off, n = vec_layout(cfg)
    V = np.zeros((128, n), np.float32)

    def put(name, arr):
        V[:, off[name]:off[name] + arr.shape[1]] = arr
    for k in ["norm_mix", "norm_ffn", "a_b_dw", "a_ln_g", "a_ln_b", "a_w_dw", "b_scale", "c_lb", "c_g_norm",
              "f_w_dw", "f_b_dw"]:
        put(k, colize(np.asarray(inp[k])))
    put("norm_final", colize(np.asarray(inp["norm_final"])[None, :]))
    V[:, off["isodd"]] = float(core % 2)
    start = (core % 2) * cfg["P"]
    rc = np.zeros((4, 16), np.float32)
    for g, w in enumerate((2, 4, 8, 16)):
        for t in range(16):
            rc[g, t] = 1.0 / min(w, start + t + 1)
    V[:, off["rc"]:off["rc"] + 64] = rc.reshape(1, 64)
    return V


def build_consts(cfg):
    W, P, CH = cfg["W"], cfg["P"], cfg["CH"]
    ident = np.eye(128, dtype=np.float32)
    tri = np.triu(np.ones((128, 128), np.float32))
    m0 = np.ones((W,), np.float32)
    m0[0:P:CH] = 0.0
    m0[P::TS] = 0.0
    m0 = np.broadcast_to(m0[None, :], (128, W))
    return np.ascontiguousarray(np.concatenate([ident, tri, m0], axis=1))


class Buf:
    __slots__ = ("lw", "rd", "name")

    def __init__(self, name=""):
        self.lw = None
        self.rd = []
        self.name = name


class Op:
    __slots__ = ("eng", "fn", "deps", "signal", "tick", "idx", "dsem", "dval", "kind")

    def __init__(self, eng, fn, kind):
        self.eng, self.fn, self.kind = eng, fn, kind
        self.deps = []
        self.signal = False
        self.tick = 0
        self.idx = 0
        self.dsem = None
        self.dval = 0


ENGS = ["pe", "act", "dve", "pool", "sp"]
NDSEM = 8


class Sched:
    def __init__(self):
        self.streams = {e: [] for e in ENGS}
        self.ndma = {"pool": 0, "sp": 0}
        self.ncc = 0
        self.dma_ops = {"pool": [], "sp": []}

    def _deps(self, op, r, w):
        deps = op.deps
        for b in r:
            if b.lw is not None:
                deps.append((b.lw, "raw"))
        for b in w:
            if b.lw is not None:
                deps.append((b.lw, "waw"))
            for x in b.rd:
                deps.append((x, "war"))
        for b in r:
            if op.kind == "c":
                b.rd = [x for x in b.rd if not (x.kind == "c" and x.eng == op.eng)]
            b.rd.append(op)
        for b in w:
            b.lw = op
            b.rd = []

    def op(self, eng, fn, r=(), w=()):
        o = Op(eng, fn, "c")
        o.idx = len(self.streams[eng])
        self._deps(o, r, w)
        self.streams[eng].append(o)
        return o

    def dma(self, q, fn, r=(), w=()):
        o = Op(q, fn, "d")
        o.idx = len(self.streams[q])
        n = self.ndma[q]
        self.ndma[q] += 1
        o.dsem = (q, n % NDSEM)
        o.dval = 16 * (n // NDSEM + 1)
        if n >= NDSEM:
            o.deps.append((self.dma_ops[q][n - NDSEM], "raw"))
        self.dma_ops[q].append(o)
        self._deps(o, r, w)
        self.streams[q].append(o)
        return o

    def cc(self, fn, r=(), w=()):
        o = Op("pool", fn, "cc")
        o.idx = len(self.streams["pool"])
        o.dsem = ("cc", self.ncc)
        self.ncc += 1
        o.dval = 1
        self._deps(o, r, w)
        self.streams["pool"].append(o)
        return o

    def barrier(self, bufs):
        b = Buf("barrier")
        lasts = []
        for e in ENGS:
            if self.streams[e]:
                lasts.append(self.streams[e][-1])
        outstanding = []
        for q in ("pool", "sp"):
            outstanding += self.dma_ops[q][-NDSEM:]
        for e in ["pe", "act", "dve", "pool", "sp"]:
            o = Op(e, None, "nop")
            o.idx = len(self.streams[e])
            for x in lasts + outstanding:
                if x is not o:
                    o.deps.append((x, "raw"))
            self.streams[e].append(o)

    def finalize(self):
        for e in ENGS:
            for o in self.streams[e]:
                keep = []
                for (d, kind) in o.deps:
                    if d.kind == "c" or d.kind == "nop":
                        if d.eng == o.eng and o.kind in ("c", "nop"):
                            if d.kind == "nop" or e == "pe" or kind != "raw" or o.idx - d.idx > 3:
                                continue
                        if d.kind == "nop":
                            continue
                        d.signal = True
                    keep.append(d)
                o.deps = keep
        for e in ENGS:
            t = 0
            for o in self.streams[e]:
                if o.signal:
                    t += 1
                    o.tick = t

    def emit(self, nc, sems, dsems, ccsems):
        engobj = {"pe": "tensor", "act": "scalar", "dve": "vector", "pool": "gpsimd", "sp": "sync"}
        with nc.Block() as block:
            for e in ENGS:
                def body(eng, e=e):
                    known = {}
                    for o in self.streams[e]:
                        for d in o.deps:
                            if d.kind == "c":
                                key, val, sem = ("c", d.eng), d.tick, sems[d.eng]
                            elif d.kind == "d":
                                key, val, sem = d.dsem, d.dval, dsems[d.dsem]
                            else:
                                key, val, sem = d.dsem, d.dval, ccsems[d.dsem[1]]
                            if known.get(key, 0) >= val:
                                continue
                            known[key] = val
                            eng.wait_ge(sem, val)
                        if o.fn is None:
                            continue
                        ins = o.fn(eng)
                        if o.kind == "d":
                            ins.then_inc(dsems[o.dsem], 16)
                        elif o.kind == "cc":
                            ins.then_inc(ccsems[o.dsem[1]], 1)
                        elif o.signal:
                            ins.then_inc(sems[e], 1)
                getattr(block, engobj[e])(body)


class Builder:
    def __init__(self, cfg, slab_idx, nslab):
        self.cfg = cfg
        self.sidx = slab_idx
        self.nslab = nslab
        self.S = Sched()
        self.nc = bass.Bass("TRN2", target_bir_lowering=False)
        self.voff, self.nvec = vec_layout(cfg)
        self.next_slab = 0
        self.ncc_tensors = 0
        self.use_scr = False

    def alloc(self, n_f32):
        n_f32 = (n_f32 + 7) // 8 * 8
        if self.use_scr and self.xtop + n_f32 <= self.xlimit:
            a = self.xtop
            self.xtop += n_f32
            return self.arena[:, a:a + n_f32]
        a = self.top
        self.top += n_f32
        assert self.top <= self.arena_words, ("SBUF arena overflow", self.top, self.arena_words)
        return self.arena[:, a:a + n_f32]

    def spill_issue(self):
        c = self.cfg
        xsp = self.d["xspill"].ap().rearrange("p (a b) -> p a b", b=c["W"])
        self.spb = Buf("xspill")
        self.dma("sp", xsp, self.xT, list(self.xb), [self.spb])
        self.spill_pending = True

    def spill_x(self):
        if not getattr(self, "spill_pending", False):
            self.spill_issue()
        self.spill_pending = False
        self.S.barrier([])
        self.xtop = self.x_start
        self.use_scr = True

    def restore_x(self):
        c = self.cfg
        self.use_scr = False
        self.S.barrier([])
        xsp = self.d["xspill"].ap().rearrange("p (a b) -> p a b", b=c["W"])
        self.dma("sp", self.xT, xsp, [self.spb], list(self.xb))

    def f32(self, *shape):
        n = int(np.prod(shape))
        ap = self.alloc(n)[:, 0:n]
        if len(shape) == 2:
            return ap.rearrange("p (a b) -> p a b", b=shape[1])
        if len(shape) == 3:
            return ap.rearrange("p (a b c) -> p a b c", b=shape[1], c=shape[2])
        return ap

    def bf16(self, *shape):
        n = int(np.prod(shape))
        ap = self.alloc((n + 1) // 2).bitcast(BF16)[:, 0:n]
        if len(shape) == 2:
            return ap.rearrange("p (a b) -> p a b", b=shape[1])
        if len(shape) == 3:
            return ap.rearrange("p (a b c) -> p a b c", b=shape[1], c=shape[2])
        return ap

    def vcol(self, name, i):
        o = self.voff[name] + i
        return self.vecs[:, o:o + 1]

    def mm(self, out, lhsT, rhs, start, stop, r, w):
        return self.S.op("pe", lambda e: e.matmul(out, lhsT, rhs, start=start, stop=stop), r, w)

    def tr(self, out, in_, npart, r, w):
        idn = self.ident[0:npart, 0:npart]
        return self.S.op("pe", lambda e: e.transpose(out, in_, idn), list(r) + [self.cbuf], w)

    def act(self, out, in_, func, r, w, bias=0.0, scale=1.0):
        return self.S.op("act", lambda e: e.activation(out, in_, func, bias=bias, scale=scale), r, w)

    def ts(self, eng, out, in0, s1, s2, op0, op1, r, w):
        if s2 is None:
            return self.S.op(eng, lambda e: e.tensor_scalar(out, in0, s1, None, op0), r, w)
        return self.S.op(eng, lambda e: e.tensor_scalar(out, in0, s1, s2, op0, op1), r, w)

    def tt(self, eng, out, in0, in1, op, r, w):
        return self.S.op(eng, lambda e: e.tensor_tensor(out, in0, in1, op), r, w)

    def stt(self, eng, out, in0, sc, in1, op0, op1, r, w):
        return self.S.op(eng, lambda e: e.scalar_tensor_tensor(out, in0, sc, in1, op0, op1), r, w)

    def copy(self, eng, out, in_, r, w):
        if eng == "act":
            return self.S.op("act", lambda e: e.copy(out, in_), r, w)
        return self.S.op(eng, lambda e: e.tensor_copy(out, in_), r, w)

    def dma(self, q, out, in_, r, w):
        return self.S.dma(q, lambda e: e.dma_start(out=out, in_=in_), r, w)

    def ps(self):
        i = self.ps_next
        self.ps_next = (i + 1) % 8
        return self.psum[i], self.psb[i]

    def wget(self, key):
        sid = self.sidx[key]
        assert sid == self.next_slab, (key, sid, self.next_slab)
        self.next_slab += 1
        k = sid % NSLOT
        slot, sb = self.wslots[k], self.wsb[k]
        src = self.wslab[sid]
        self.dma("pool", slot, src, [], [sb])
        return slot, sb

    def tiles(self, halo):
        c = self.cfg
        out = []
        c0 = HO - halo
        while c0 < c["WH"]:
            w = min(512, c["WH"] - c0)
            out.append((c0, w))
            c0 += w
        return out

    def split(self, c0, w):
        c = self.cfg
        pe = HO + c["P"]
        pp = (c0, min(c0 + w, pe) - c0) if c0 < pe else None
        sp = None
        if c0 + w > pe:
            assert c0 <= pe and c0 + w == c["WH"]
            sp = (pe, c["PT"])
        return pp, sp

    def build(self):
        nc, c = self.nc, self.cfg
        D, NDC, P, NS, PT, W, WH = c["D"], c["NDC"], c["P"], c["NS"], c["PT"], c["W"], c["WH"]
        NFP = c["NFP"]
        dt = lambda n, s, t=F32, k="ExternalInput": nc.dram_tensor(n, s, t, kind=k)
        self.d = d = {}
        d["xp"] = dt("xp", [P, D]); d["xs"] = dt("xs", [PT, D])
        d["st_a"] = dt("st_a", [2, NS, HO, D]); d["st_b"] = dt("st_b", [1, NS, PH, D])
        d["st_c"] = dt("st_c", [1, NS, NDC, 128, 128]); d["st_f"] = dt("st_f", [4, NS, 2, 2 * c["DFF"]])
        d["wslab"] = dt("wslab", [self.nslab, 128, SLOT]); d["vecs"] = dt("vecs", [128, self.nvec])
        d["consts"] = dt("consts", [128, 256 + W])
        o = "ExternalOutput"
        d["yp"] = dt("yp", [P, D], F32, o); d["ys"] = dt("ys", [PT, D], F32, o)
        d["na_p"] = dt("na_p", [2, HO, D], F32, o); d["na_s"] = dt("na_s", [2, NS, HO, D], F32, o)
        d["nb_p"] = dt("nb_p", [1, PH, D], F32, o); d["nb_s"] = dt("nb_s", [1, NS, PH, D], F32, o)
        d["nc_p"] = dt("nc_p", [1, NDC, 128, 128], F32, o); d["nc_s"] = dt("nc_s", [1, NS, NDC, 128, 128], F32, o)
        d["nf_p"] = dt("nf_p", [4, 2, 2 * c["DFF"]], F32, o); d["nf_s"] = dt("nf_s", [4, NS, 2, 2 * c["DFF"]], F32, o)
        d["xspill"] = nc.dram_tensor("xspill", [128, NDC * W], F32)
        self.wslab = d["wslab"].ap()
        self.arena_words = (nc.sbuf_top - nc.sbuf_base) // 4 - 64
        nsem = {}
        import contextlib
        with contextlib.ExitStack() as es:
            self.arena = es.enter_context(nc.sbuf_tensor("arena", [128, self.arena_words], F32))
            self.psum = [es.enter_context(nc.psum_tensor("ps%d" % i, [128, 512], F32)) for i in range(8)]
            self.psb = [Buf("ps%d" % i) for i in range(8)]
            self.ps_next = 0
            sems = {e: es.enter_context(nc.semaphore("s_" + e)) for e in ENGS}
            dsems = {(q, i): es.enter_context(nc.semaphore("d_%s%d" % (q, i))) for q in ("pool", "sp") for i in range(NDSEM)}
            ccsems = [es.enter_context(nc.semaphore("cc%d" % i)) for i in range(32)]
            self.top = 0
            self.vecs = self.alloc(self.nvec)
            cst = self.alloc(256 + W)
            self.ident = cst[:, 0:128]
            self.tri = cst[:, 128:256]
            self.m0 = cst[:, 256:256 + W]
            self.cbuf = Buf("consts")
            self.ones_bf = self.bf16(128)
            self.zero = self.alloc(64)
            self.wslots = [self.bf16(SLOT) for _ in range(NSLOT)]
            self.wsb = [Buf("ws%d" % i) for i in range(NSLOT)]
            self.dma("sp", self.vecs[:, 0:self.nvec], d["vecs"].ap(), [], [self.cbuf])
            self.dma("sp", cst[:, 0:256 + W], d["consts"].ap(), [], [self.cbuf])
            self.S.op("dve", lambda e: e.memset(self.ones_bf, 1.0), [], [self.cbuf])
            self.S.op("dve", lambda e: e.memset(self.zero, 0.0), [], [self.cbuf])
            self.base_top = self.top
            self.program()
            assert self.next_slab == self.nslab, (self.next_slab, self.nslab)
            self.S.barrier([])
            self.S.finalize()
            assert self.S.ncc <= 32
            self.S.emit(nc, sems, dsems, ccsems)
        return nc

    def program(self):
        c = self.cfg
        NDC, P, PT, W, WH, D = c["NDC"], c["P"], c["PT"], c["W"], c["WH"], c["D"]
        self.x_start = self.top
        self.xT = self.f32(NDC, W)
        self.xlimit = self.top
        self.xb = [Buf("x%d" % i) for i in range(NDC)]
        self.hT = self.bf16(NDC, WH)
        self.hb = [Buf("h%d" % i) for i in range(NDC)]
        self.rstd = self.f32(W)
        self.rstd_b = Buf("rstd")
        self.stage_top = self.top
        self.load_x()
        for layer in range(4):
            kind = layer % 3
            self.top = self.stage_top
            if kind != 1:
                self.spill_issue()
            self.rmsnorm("norm_mix", layer, halo=(HO if kind == 0 else 0))
            if kind == 0:
                self.mixer_a(layer // 3)
            elif kind == 1:
                self.mixer_b()
            else:
                self.mixer_c()
            self.S.barrier([])
            self.top = self.stage_top
            self.rmsnorm("norm_ffn", layer, halo=2)
            self.ffn(layer)
            self.S.barrier([])
        self.top = self.stage_top
        self.final_out()

    def load_x(self):
        c = self.cfg
        NDC, P, PT, D = c["NDC"], c["P"], c["PT"], c["D"]
        stg = [self.f32(D) for _ in range(2)]
        sb = [Buf("xstg0"), Buf("xstg1")]
        srcs = [(self.d["xp"].ap()[t * 128:(t + 1) * 128, :], 128, t * 128) for t in range(P // 128)]
        srcs.append((self.d["xs"].ap()[:, :], PT, P))
        for i, (src, n, col) in enumerate(srcs):
            s, b = stg[i % 2], sb[i % 2]
            self.dma("sp", s[0:n, :], src, [], [b])
            for q in range(NDC // 4):
                ps, pb = self.ps()
                for k in range(4):
                    dc = q * 4 + k
                    self.tr(ps[:, k * 128:k * 128 + n], s[0:n, dc * 128:(dc + 1) * 128], n, [b], [pb])
                eng = "act" if q % 2 == 0 else "dve"
                for k in range(4):
                    dc = q * 4 + k
                    self.copy(eng, self.xT[:, dc, col:col + n], ps[:, k * 128:k * 128 + n], [pb], [self.xb[dc]])

    def sumsq_to_rstd(self, srcs, width, dim, dst, dstb):
        sq = [self.bf16(512) for _ in range(2)]
        sqb = [Buf("sq0"), Buf("sq1")]
        n = len(srcs)
        c0 = 0
        k = 0
        while c0 < width:
            w = min(512, width - c0)
            ps, pb = self.ps()
            for i, (ap, b) in enumerate(srcs):
                self.act(sq[k % 2][:, 0:w], ap[:, c0:c0 + w], AF.Square, [b], [sqb[k % 2]])
                self.mm(ps[:, 0:w], self.ones_bf, sq[k % 2][:, 0:w], i == 0, i == n - 1, [sqb[k % 2], self.cbuf], [pb])
                k += 1
            self.act(dst[:, c0:c0 + w], ps[:, 0:w], AF.Sqrt, [pb], [dstb], bias=EPS, scale=1.0 / dim)
            c0 += w
        self.S.op("dve", lambda e: e.reciprocal(dst[:, 0:width], dst[:, 0:width]), [dstb], [dstb])

    def rmsnorm(self, gname, layer, halo=0):
        c = self.cfg
        NDC, W, D, P = c["NDC"], c["W"], c["D"], c["P"]
        self.sumsq_to_rstd([(self.xT[:, dc, :], self.xb[dc]) for dc in range(NDC)], W, D, self.rstd, self.rstd_b)
        dst, dstb = self.hT, self.hb
        st = None
        if halo:
            tail = self.bf16(NDC, halo)
            tlb = [Buf("rt%d" % i) for i in range(NDC)]
            for dc in range(NDC):
                self.stt("dve", tail[:, dc, :], self.xT[:, dc, P - halo:P], self.vcol(gname, layer * NDC + dc), self.rstd[:, P - halo:P],
                         ALU.mult, ALU.mult, [self.xb[dc], self.rstd_b, self.cbuf], [tlb[dc]])
            st = self.handoff_start(tail, tlb, halo, halo, BF16)
        for dc in range(NDC):
            g = self.vcol(gname, layer * NDC + dc)
            self.stt("dve", dst[:, dc, HO:HO + W], self.xT[:, dc, :], g, self.rstd[:, 0:W], ALU.mult, ALU.mult,
                     [self.xb[dc], self.rstd_b, self.cbuf], [dstb[dc]])
        if st is not None:
            self.handoff_finish(st, self.hT, self.hb, HO - halo)

    def handoff_start(self, src, srcb, col_end, h, dtype):
        c = self.cfg
        NDC = c["NDC"]
        nc = self.nc
        k = self.ncc_tensors
        self.ncc_tensors += 1
        snd = nc.dram_tensor("hs%d" % k, [128, NDC * h], dtype)
        rcv = nc.dram_tensor("hr%d" % k, [256, NDC * h], dtype)
        sb, rb = Buf("snd"), Buf("rcv")
        self.dma("pool", snd.ap().rearrange("p (a b) -> p a b", b=h), src[:, :, col_end - h:col_end], list(srcb), [sb])
        self.S.cc(lambda e: e.collective_compute("AllGather", ALU.bypass, replica_groups=PAIRS, ins=[snd.ap()], outs=[rcv.ap()]),
                  [sb], [rb])
        tmp = self.f32(NDC, h) if dtype == F32 else self.bf16(NDC, h)
        tb = Buf("hrtmp")
        self.dma("sp", tmp, rcv.ap()[0:128, :].rearrange("p (a b) -> p a b", b=h), [rb], [tb])
        return (tmp, tb, h)

    def handoff_finish(self, st, dst, dstb, dst_col):
        tmp, tb, h = st
        self.ts("dve", dst[:, :, dst_col:dst_col + h], tmp, self.vcol("isodd", 0), None, ALU.mult, None,
                [tb, self.cbuf], list(dstb))

    def handoff(self, src, srcb, col_end, h, dst, dstb, dst_col, dtype):
        st = self.handoff_start(src, srcb, col_end, h, dtype)
        self.handoff_finish(st, dst, dstb, dst_col)

    def ffn(self, l):
        c = self.cfg
        NDC, P, PT, W, WH, NS, NFP, GP, DFF, D = (c[k] for k in ["NDC", "P", "PT", "W", "WH", "NS", "NFP", "GP", "DFF", "D"])
        NG = NFP // GP
        UW = 2 + P + NS * 10
        tiles = self.tiles(2)
        own_tiles = self.tiles(0)
        ub1 = self.f32(2, UW); ubb1 = Buf("ub0")
        ubuf = [ub1, ub1]
        ubb = [ubb1, ubb1]
        cb1 = self.f32(2, W); cbb1 = Buf("cb0")
        cbuf_ = [cb1, cb1]
        cbb = [cbb1, cbb1]
        gT = [self.bf16(GP, W) for _ in range(2)]
        gTb = [[Buf("g%d_%d" % (i, p)) for p in range(GP)] for i in range(2)]
        ust = [self.f32(2 * GP, 34) for _ in range(2)]
        ustb = [Buf("ust0"), Buf("ust1")]
        os1 = self.f32(2, GP * 128); osb1 = Buf("os0")
        ostg = [os1, os1]
        ostb = [osb1, osb1]
        hsb_sb = [self.f32(GP * 4 * NS) for _ in range(2)]
        hsb_b = [Buf("hsb0"), Buf("hsb1")]
        ss1 = self.f32(2, GP * 128); ssb1 = Buf("ss0")
        sstg = [ss1, ss1]
        sstb = [ssb1, ssb1]
        stf = self.d["st_f"].ap()
        nfs = self.d["nf_s"].ap()
        nfp = self.d["nf_p"].ap()
        dcols = min(D, SLOT // GP)
        ndh = D // dcols
        pend_down = None
        pi = 0
        for g in range(NG + 1):
            if g < NG:
                gi = g % 2
                ss, ssb = sstg[gi], sstb[gi]
                for k in range(2):
                    src = stf[l][:, :, k * DFF + g * GP * 128:k * DFF + (g + 1) * GP * 128].rearrange("s r c -> (s r) c")
                    self.dma("sp", ss[0:2 * NS, k, :], src, [], [ssb])
                hps, hpb = hsb_sb[gi], hsb_b[gi]

                def hist_block():
                    hps_, hpb_ = self.ps()
                    for p_ in range(GP):
                        for k_ in range(2):
                            q_ = p_ * 2 + k_
                            self.tr(hps_[:, q_ * 2 * NS:(q_ + 1) * 2 * NS], ss[0:2 * NS, k_, p_ * 128:(p_ + 1) * 128], 2 * NS, [ssb], [hpb_])
                    self.copy("act", hps[:, 0:GP * 4 * NS], hps_[:, 0:GP * 4 * NS], [hpb_], [hpb])
                for p in range(GP):
                    j = g * GP + p
                    ui = pi % 2
                    pi += 1
                    ub, ubf = ubuf[ui], ubb[ui]
                    slab, sb = self.wget(("up", l, j))
                    sv = slab.rearrange("p (a b) -> p a b", b=256)
                    for k in range(2):
                        for (c0, w) in tiles:
                            ps, pb = self.ps()
                            for dc in range(NDC):
                                self.mm(ps[:, 0:w], sv[:, dc, k * 128:(k + 1) * 128], self.hT[:, dc, c0:c0 + w],
                                        dc == 0, dc == NDC - 1, [sb, self.hb[dc]], [pb])
                            pp, sp = self.split(c0, w)
                            if pp:
                                self.copy("act", ub[:, k, pp[0] - (HO - 2):pp[0] - (HO - 2) + pp[1]], ps[:, 0:pp[1]], [pb], [ubf])
                            if sp:
                                o0 = sp[0] - c0
                                self.copy("act", ub[:, k, 2 + P:2 + P + NS * 10].rearrange("p (s t) -> p s t", t=10)[:, :, 2:10],
                                          ps[:, o0:o0 + PT].rearrange("p (s t) -> p s t", t=TS), [pb], [ubf])
                    if p == 0:
                        hist_block()
                    for k in range(2):
                        q = p * 2 + k
                        self.copy("act", ub[:, k, 2 + P:2 + P + NS * 10].rearrange("p (s t) -> p s t", t=10)[:, :, 0:2],
                                  hps[:, q * 2 * NS:(q + 1) * 2 * NS].rearrange("p (s r) -> p s r", r=2), [hpb], [ubf])
                    cb, cbf = cbuf_[ui], cbb[ui]
                    for k in range(2):
                        ch = k * NFP + j
                        wv = lambda t: self.vcol("f_w_dw", (l * 3 + t) * 2 * NFP + ch)
                        bv = self.vcol("f_b_dw", l * 2 * NFP + ch)
                        for seg in range(2):
                            if seg == 0:
                                src = lambda t: ub[:, k, t:t + P]
                                dst = cb[:, k, 0:P]
                            else:
                                sview = ub[:, k, 2 + P:2 + P + NS * 10].rearrange("p (s t) -> p s t", t=10)
                                src = lambda t: sview[:, :, t:t + TS]
                                dst = cb[:, k, P:W].rearrange("p (s t) -> p s t", t=TS)
                            self.ts("dve", dst, src(0), wv(0), bv, ALU.mult, ALU.add, [ubf, self.cbuf], [cbf])
                            self.stt("dve", dst, src(1), wv(1), dst, ALU.mult, ALU.add, [ubf, cbf, self.cbuf], [cbf])
                            self.stt("dve", dst, src(2), wv(2), dst, ALU.mult, ALU.add, [ubf, cbf, self.cbuf], [cbf])
                    self.act(cb[:, 0, :], cb[:, 0, :], AF.Silu, [cbf], [cbf])
                    self.tt("dve", gT[gi][:, p, :], cb[:, 0, :], cb[:, 1, :], ALU.mult, [cbf], [gTb[gi][p]])
                    us, usb = ust[gi], ustb[gi]
                    for k in range(2):
                        self.copy("act", us[:, p * 2 + k, 0:2], ub[:, k, P:P + 2], [ubf], [usb])
                        self.copy("act", us[:, p * 2 + k, 2:2 + 2 * NS].rearrange("p (s r) -> p s r", r=2),
                                  ub[:, k, 2 + P:2 + P + NS * 10].rearrange("p (s t) -> p s t", t=10)[:, :, 8:10], [ubf], [usb])
            if pend_down is not None:
                pg = pend_down
                pgi = pg % 2
                for h in range(ndh):
                    slab, sb = self.wget(("down", l, pg, h))
                    sv = slab.rearrange("p (a b) -> p a b", b=dcols)
                    for dl in range(dcols // 128):
                        dc = h * (dcols // 128) + dl
                        for (c0, w) in own_tiles:
                            ps, pb = self.ps()
                            for p in range(GP):
                                self.mm(ps[:, 0:w], sv[:, p, dl * 128:(dl + 1) * 128], gT[pgi][:, p, c0 - HO:c0 - HO + w],
                                        p == 0, p == GP - 1, [sb, gTb[pgi][p]], [pb])
                            self.tt("dve", self.xT[:, dc, c0 - HO:c0 - HO + w], self.xT[:, dc, c0 - HO:c0 - HO + w], ps[:, 0:w],
                                    ALU.add, [pb, self.xb[dc]], [self.xb[dc]])
            if g < NG:
                us, usb = ust[gi], ustb[gi]
                og, ogb = ostg[gi], ostb[gi]
                n34 = 2 + 2 * NS
                for k in range(2):
                    ps, pb = self.ps()
                    for p in range(GP):
                        self.tr(ps[0:n34, p * 128:(p + 1) * 128], us[:, p * 2 + k, 0:n34], 128, [usb], [pb])
                    self.copy("act", og[0:n34, k, :], ps[0:n34, 0:GP * 128], [pb], [ogb])
                    col = k * DFF + g * GP * 128
                    self.dma("sp", nfp[l][:, col:col + GP * 128], og[0:2, k, :], [ogb], [])
                    self.dma("sp", nfs[l][:, :, col:col + GP * 128].rearrange("s r c -> (s r) c"), og[2:n34, k, :], [ogb], [])
            pend_down = g if g < NG else None

    def stage_staging(self):
        self.rstg = self.f32(self.cfg["D"])
        self.rstgb = Buf("rowstage")

    def rows_out(self, srcf, n, dst_rows_fn):
        c = self.cfg
        NDC, D = c["NDC"], c["D"]
        st, stb = self.rstg, self.rstgb
        for q in range(NDC // 4):
            ps, pb = self.ps()
            for k in range(4):
                dc = q * 4 + k
                ap, bufs = srcf(dc)
                self.tr(ps[0:n, k * 128:(k + 1) * 128], ap, 128, bufs, [pb])
            self.copy("act", st[0:n, q * 512:(q + 1) * 512], ps[0:n, 0:512], [pb], [stb])
        dst_rows_fn(st, stb)

    def rows_in(self, src_rows, n, dst_fn):
        c = self.cfg
        NDC, D = c["NDC"], c["D"]
        st, stb = self.rstg, self.rstgb
        self.dma("sp", st[0:n, :], src_rows, [], [stb])
        for q in range(NDC // 4):
            ps, pb = self.ps()
            for k in range(4):
                dc = q * 4 + k
                self.tr(ps[:, k * 128:k * 128 + n], st[0:n, dc * 128:(dc + 1) * 128], n, [stb], [pb])
            for k in range(4):
                dst_fn(q * 4 + k, ps[:, k * 128:k * 128 + n], pb)

    def mixer_a(self, ja):
        c = self.cfg
        NDC, P, PT, W, WH, NS, D = (c[k] for k in ["NDC", "P", "PT", "W", "WH", "NS", "D"])
        SW = HO + TS
        VW = HO + P + NS * SW
        self.stage_staging()
        self.spill_x()
        vbuf = self.bf16(NDC, VW)
        vb = [Buf("v%d" % i) for i in range(NDC)]
        vf = self.f32(NDC, HO + PT)
        vfb = [Buf("vf%d" % i) for i in range(NDC)]
        sig = [self.f32(512) for _ in range(2)]
        sgb = [Buf("sig0"), Buf("sig1")]
        self.use_scr = False
        sta, nas, nap = self.d["st_a"].ap(), self.d["na_s"].ap(), self.d["na_p"].ap()
        for s0 in range(0, NS, 4):
            ns = min(4, NS - s0)

            def dst_fn(dc, psap, pb, s0=s0, ns=ns):
                self.copy("act", vbuf[:, dc, HO + P + s0 * SW:HO + P + (s0 + ns) * SW].rearrange("p (s t) -> p s t", t=SW)[:, :, 0:HO],
                          psap.rearrange("p (s t) -> p s t", t=HO), [pb], [vb[dc]])
            self.rows_in(sta[ja][s0:s0 + ns].rearrange("s r c -> (s r) c"), ns * HO, dst_fn)
        for s in range(NS):
            self.dma("sp", nas[ja][s, 0:HO - TS, :], sta[ja][s, TS:HO, :], [], [])
        tiles = self.tiles(HO)
        si = 0
        for dc in range(NDC):
            slab, sb = self.wget(("pw1", ja, dc))
            sv = slab.rearrange("p (a b) -> p a b", b=256)
            for (c0, w) in tiles:
                pa, pab = self.ps()
                pg, pgb = self.ps()
                for kdc in range(NDC):
                    self.mm(pa[:, 0:w], sv[:, kdc, 0:128], self.hT[:, kdc, c0:c0 + w], kdc == 0, kdc == NDC - 1, [sb, self.hb[kdc]], [pab])
                for kdc in range(NDC):
                    self.mm(pg[:, 0:w], sv[:, kdc, 128:256], self.hT[:, kdc, c0:c0 + w], kdc == 0, kdc == NDC - 1, [sb, self.hb[kdc]], [pgb])
                sg, sgbuf = sig[si % 2], sgb[si % 2]
                si += 1
                self.act(sg[:, 0:w], pg[:, 0:w], AF.Sigmoid, [pgb], [sgbuf])
                pp, sp = self.split(c0, w)
                if pp:
                    self.tt("dve", vbuf[:, dc, pp[0]:pp[0] + pp[1]], pa[:, 0:pp[1]], sg[:, 0:pp[1]], ALU.mult, [pab, sgbuf], [vb[dc]])
                    t0 = max(pp[0], P)
                    t1 = pp[0] + pp[1]
                    if t1 > t0:
                        self.tt("dve", vf[:, dc, t0 - P:t1 - P], pa[:, t0 - c0:t1 - c0], sg[:, t0 - c0:t1 - c0], ALU.mult, [pab, sgbuf], [vfb[dc]])
                if sp:
                    o0 = sp[0] - c0
                    self.tt("dve", vbuf[:, dc, HO + P:VW].rearrange("p (s t) -> p s t", t=SW)[:, :, HO:SW],
                            pa[:, o0:o0 + PT].rearrange("p (s t) -> p s t", t=TS), sg[:, o0:o0 + PT].rearrange("p (s t) -> p s t", t=TS),
                            ALU.mult, [pab, sgbuf], [vb[dc]])
                    self.tt("dve", vf[:, dc, HO:HO + PT], pa[:, o0:o0 + PT], sg[:, o0:o0 + PT], ALU.mult, [pab, sgbuf], [vfb[dc]])
        self.rows_out(lambda dc: (vf[:, dc, 0:HO], [vfb[dc]]), HO,
                      lambda st, stb: self.dma("sp", nap[ja], st[0:HO, :], [stb], []))
        def samp_out(st, stb):
            for s in range(NS):
                self.dma("sp", nas[ja][s, HO - TS:HO, :], st[s * TS:(s + 1) * TS, :], [stb], [])
        self.rows_out(lambda dc: (vf[:, dc, HO:HO + PT], [vfb[dc]]), PT, samp_out)
        top_conv = self.top
        diag = [self.bf16(CW, 128) for _ in range(2)]
        dgb = [Buf("dg0"), Buf("dg1")]
        acc = [self.f32(W) for _ in range(2)]
        accb = [Buf("acc0"), Buf("acc1")]
        own = self.tiles(0)
        cT, cb = self.hT, self.hb
        for dc in range(NDC):
            dg, dgbuf = diag[dc % 2], dgb[dc % 2]
            for j in range(CW):
                if j % 4 != 0:
                    self.act(dg[:, j, :], self.ident, AF.Copy, [self.cbuf], [dgbuf], scale=self.vcol("a_w_dw", (ja * CW + j) * NDC + dc))
            ac, acb_ = acc[dc % 2], accb[dc % 2]
            wcol = lambda j: self.vcol("a_w_dw", (ja * CW + j) * NDC + dc)
            sview = vbuf[:, dc, HO + P:VW].rearrange("p (s t) -> p s t", t=SW)
            acs = ac[:, P:W].rearrange("p (s t) -> p s t", t=TS)
            odd = list(range(0, CW, 4))
            for n_, j in enumerate(odd):
                if n_ == 0:
                    self.ts("dve", ac[:, 0:P], vbuf[:, dc, j:j + P], wcol(j), self.vcol("a_b_dw", ja * NDC + dc), ALU.mult, ALU.add,
                            [vb[dc], self.cbuf], [acb_])
                    self.ts("dve", acs, sview[:, :, j:j + TS], wcol(j), self.vcol("a_b_dw", ja * NDC + dc), ALU.mult, ALU.add,
                            [vb[dc], self.cbuf], [acb_])
                else:
                    self.stt("dve", ac[:, 0:P], vbuf[:, dc, j:j + P], wcol(j), ac[:, 0:P], ALU.mult, ALU.add, [vb[dc], acb_, self.cbuf], [acb_])
                    self.stt("dve", acs, sview[:, :, j:j + TS], wcol(j), acs, ALU.mult, ALU.add, [vb[dc], acb_, self.cbuf], [acb_])
            even = [j for j in range(CW) if j % 4 != 0]
            for (c0, w) in own:
                ps, pb = self.ps()
                pp, sp = self.split(c0, w)
                if pp:
                    for n_, j in enumerate(even):
                        a0 = pp[0] - HO + j
                        self.mm(ps[:, 0:pp[1]], dg[:, j, :], vbuf[:, dc, a0:a0 + pp[1]], n_ == 0, n_ == len(even) - 1, [dgbuf, vb[dc]], [pb])
                if sp:
                    o0 = sp[0] - c0
                    for n_, j in enumerate(even):
                        self.mm(ps[:, o0:o0 + PT].rearrange("p (s t) -> p s t", t=TS), dg[:, j, :], sview[:, :, j:j + TS],
                                n_ == 0, n_ == len(even) - 1, [dgbuf, vb[dc]], [pb])
                self.tt("dve", cT[:, dc, c0:c0 + w], ps[:, 0:w], ac[:, c0 - HO:c0 - HO + w], ALU.add, [pb, acb_], [cb[dc]])
        self.restore_x()
        self.top = top_conv
        mu = self.f32(W); mub = Buf("mu")
        rs = self.f32(W); rsb = Buf("rs")
        sq = [self.bf16(W) for _ in range(2)]; sqb = [Buf("lsq0"), Buf("lsq1")]
        tl = [(c0, w) + self.ps() + self.ps() for (c0, w) in own]
        for dc in range(NDC):
            self.act(sq[dc % 2], cT[:, dc, HO:HO + W], AF.Square, [cb[dc]], [sqb[dc % 2]])
            for (c0, w, p1, p1b, p2, p2b) in tl:
                self.mm(p1[:, 0:w], self.ones_bf, cT[:, dc, c0:c0 + w], dc == 0, dc == NDC - 1, [cb[dc], self.cbuf], [p1b])
                self.mm(p2[:, 0:w], self.ones_bf, sq[dc % 2][:, c0 - HO:c0 - HO + w], dc == 0, dc == NDC - 1, [sqb[dc % 2], self.cbuf], [p2b])
        for (c0, w, p1, p1b, p2, p2b) in tl:
            a, b_ = c0 - HO, c0 - HO + w
            self.act(mu[:, a:b_], p1[:, 0:w], AF.Copy, [p1b], [mub], scale=1.0 / D)
            self.tt("dve", rs[:, a:b_], mu[:, a:b_], mu[:, a:b_], ALU.mult, [mub], [rsb])
            self.stt("dve", rs[:, a:b_], p2[:, 0:w], 1.0 / D, rs[:, a:b_], ALU.mult, ALU.subtract, [p2b, rsb], [rsb])
        self.act(rs[:, 0:W], rs[:, 0:W], AF.Sqrt, [rsb], [rsb], bias=EPS)
        self.S.op("dve", lambda e: e.reciprocal(rs[:, 0:W], rs[:, 0:W]), [rsb], [rsb])
        tmp = [self.f32(W) for _ in range(2)]; tmb = [Buf("lt0"), Buf("lt1")]
        for dc in range(NDC):
            t, tb = tmp[dc % 2], tmb[dc % 2]
            self.tt("dve", t[:, 0:W], cT[:, dc, HO:HO + W], mu[:, 0:W], ALU.subtract, [cb[dc], mub], [tb])
            self.tt("dve", t[:, 0:W], t[:, 0:W], rs[:, 0:W], ALU.mult, [tb, rsb], [tb])
            self.act(cT[:, dc, HO:HO + W], t[:, 0:W], AF.Silu, [tb, self.cbuf], [cb[dc]],
                     bias=self.vcol("a_ln_b", ja * NDC + dc), scale=self.vcol("a_ln_g", ja * NDC + dc))
        self.proj_residual("pw2", ja, cT, cb)

    def proj_residual(self, key, j, src, srcb, scale_name=None):
        c = self.cfg
        NDC = c["NDC"]
        own = self.tiles(0)
        for dc2 in range(NDC // 2):
            slab, sb = self.wget((key, j, dc2) if j is not None else (key, dc2))
            sv = slab.rearrange("p (a b) -> p a b", b=256)
            for k in range(2):
                dco = dc2 * 2 + k
                for (c0, w) in own:
                    ps, pb = self.ps()
                    for kdc in range(NDC):
                        self.mm(ps[:, 0:w], sv[:, kdc, k * 128:(k + 1) * 128], src[:, kdc, c0:c0 + w], kdc == 0, kdc == NDC - 1,
                                [sb, srcb[kdc]], [pb])
                    xs = self.xT[:, dco, c0 - HO:c0 - HO + w]
                    self.tt("dve", xs, xs, ps[:, 0:w], ALU.add, [pb, self.xb[dco]], [self.xb[dco]])

    def mixer_b(self):
        c = self.cfg
        NDC, P, PT, W, WH, NS, D = (c[k] for k in ["NDC", "P", "PT", "W", "WH", "NS", "D"])
        NG = NDC // 4
        SW = PH + TS
        LP = PH + P
        LB = LP + NS * SW
        stb_, nbs, nbp = self.d["st_b"].ap(), self.d["nb_s"].ap(), self.d["nb_p"].ap()
        tail = self.f32(NDC, PH); tlb = [Buf("tl%d" % i) for i in range(NDC)]
        for dc in range(NDC):
            self.stt("dve", tail[:, dc, :], self.xT[:, dc, P - PH:P], self.vcol("norm_mix", 1 * NDC + dc), self.rstd[:, P - PH:P],
                     ALU.mult, ALU.mult, [self.xb[dc], self.rstd_b, self.cbuf], [tlb[dc]])
        halo = self.f32(NDC, PH); hlb = [Buf("hl%d" % i) for i in range(NDC)]
        self.handoff(tail, tlb, PH, PH, halo, hlb, 0, F32)
        self.stage_staging()
        self.rows_out(lambda dc: (tail[:, dc, :], [tlb[dc]]), PH, lambda st, sb: self.dma("sp", nbp[0], st[0:PH, :], [sb], []))
        for s in range(NS):
            self.dma("sp", nbs[0][s, 0:PH - TS, :], stb_[0][s, TS:PH, :], [], [])
        hist = self.f32(NDC, NS * PH); hib = [Buf("hi%d" % i) for i in range(NDC)]
        for s0 in range(0, NS, 8):
            ns = min(8, NS - s0)

            def dst_fn(dc, psap, pb, s0=s0, ns=ns):
                self.copy("act", hist[:, dc, s0 * PH:(s0 + ns) * PH], psap, [pb], [hib[dc]])
            self.rows_in(stb_[0][s0:s0 + ns].rearrange("s r c -> (s r) c"), ns * PH, dst_fn)
        hs = self.f32(NDC, PT); hsb = [Buf("hs%d" % i) for i in range(NDC)]
        for dc in range(NDC):
            self.stt("dve", hs[:, dc, :], self.xT[:, dc, P:W], self.vcol("norm_mix", 1 * NDC + dc), self.rstd[:, P:W],
                     ALU.mult, ALU.mult, [self.xb[dc], self.rstd_b, self.cbuf], [hsb[dc]])
        def samp_out(st, sb):
            for s in range(NS):
                self.dma("sp", nbs[0][s, PH - TS:PH, :], st[s * TS:(s + 1) * TS, :], [sb], [])
        self.rows_out(lambda dc: (hs[:, dc, :], [hsb[dc]]), PT, samp_out)
        pooled, plb = self.hT, self.hb
        hf = self.f32(LB); hfb = Buf("hf")
        sA = self.f32(LB); sAb = Buf("sA")
        sB = self.f32(LB); sBb = Buf("sB")
        own = self.tiles(0)
        for g in range(4):
            wdw = (2, 4, 8, 16)[g]
            for dl in range(NG):
                dc = g * NG + dl
                gcol = self.vcol("norm_mix", 1 * NDC + dc)
                self.copy("act", hf[:, 0:PH], halo[:, dc, :], [hlb[dc]], [hfb])
                self.stt("dve", hf[:, PH:LP], self.xT[:, dc, 0:P], gcol, self.rstd[:, 0:P], ALU.mult, ALU.mult,
                         [self.xb[dc], self.rstd_b, self.cbuf], [hfb])
                sv = hf[:, LP:LB].rearrange("p (s t) -> p s t", t=SW)
                self.copy("act", sv[:, :, 0:PH], hist[:, dc, :].rearrange("p (s t) -> p s t", t=PH), [hib[dc]], [hfb])
                self.copy("act", sv[:, :, PH:SW], hs[:, dc, :].rearrange("p (s t) -> p s t", t=TS), [hsb[dc]], [hfb])
                cur, curb = hf, hfb
                sh = 1
                bufs = [(sA, sAb), (sB, sBb)]
                lvl = 0
                while sh < wdw:
                    nxt, nxtb = bufs[lvl % 2]
                    self.tt("dve", nxt[:, sh:LP], cur[:, sh:LP], cur[:, 0:LP - sh], ALU.add, [curb], [nxtb])
                    cs = cur[:, LP:LB].rearrange("p (s t) -> p s t", t=SW)
                    ns_ = nxt[:, LP:LB].rearrange("p (s t) -> p s t", t=SW)
                    self.tt("dve", ns_[:, :, sh:SW], cs[:, :, sh:SW], cs[:, :, 0:SW - sh], ALU.add, [curb], [nxtb])
                    cur, curb = nxt, nxtb
                    sh *= 2
                    lvl += 1
                self.stt("dve", pooled[:, dc, HO + 16:HO + P], cur[:, PH + 16:LP], 1.0 / wdw, hf[:, PH + 16:LP], ALU.mult, ALU.subtract,
                         [curb, hfb], [plb[dc]])
                rc = self.vecs[:, self.voff["rc"] + g * 16:self.voff["rc"] + (g + 1) * 16]
                self.tt("dve", sB[:, 0:16] if cur is not sB else sA[:, 0:16], cur[:, PH:PH + 16], rc, ALU.mult, [curb, self.cbuf],
                        [sBb if cur is not sB else sAb])
                o16 = sB if cur is not sB else sA
                o16b = sBb if cur is not sB else sAb
                self.tt("dve", pooled[:, dc, HO:HO + 16], o16[:, 0:16], hf[:, PH:PH + 16], ALU.subtract, [o16b, hfb], [plb[dc]])
                cs = cur[:, LP:LB].rearrange("p (s t) -> p s t", t=SW)
                hv = hf[:, LP:LB].rearrange("p (s t) -> p s t", t=SW)
                self.stt("dve", pooled[:, dc, HO + P:HO + W].rearrange("p (s t) -> p s t", t=TS), cs[:, :, PH:SW], 1.0 / wdw, hv[:, :, PH:SW],
                         ALU.mult, ALU.subtract, [curb, hfb], [plb[dc]])
            slab, sb = self.wget(("grp", g))
            sv = slab[:, 0:NG * NG * 128].rearrange("p (a b) -> p a b", b=NG * 128)
            for dl in range(NG):
                dco = g * NG + dl
                for (c0, w) in own:
                    ps, pb = self.ps()
                    for k in range(NG):
                        self.mm(ps[:, 0:w], sv[:, k, dl * 128:(dl + 1) * 128], pooled[:, g * NG + k, c0:c0 + w], k == 0, k == NG - 1,
                                [sb, plb[g * NG + k]], [pb])
                    xs = self.xT[:, dco, c0 - HO:c0 - HO + w]
                    self.stt("dve", xs, ps[:, 0:w], self.vcol("b_scale", dco), xs, ALU.mult, ALU.add, [pb, self.xb[dco], self.cbuf], [self.xb[dco]])

    def mixer_c(self):
        c = self.cfg
        NDC, P, PT, W, WH, NS, D, CH = (c[k] for k in ["NDC", "P", "PT", "W", "WH", "NS", "D", "CH"])
        nc = self.nc
        H = NDC
        stc, ncs, ncp = self.d["st_c"].ap(), self.d["nc_s"].ap(), self.d["nc_p"].ap()
        oN = self.bf16(NDC, WH); onb = [Buf("on%d" % i) for i in range(NDC)]
        self.spill_x()
        nf = lambda nm: (self.f32(W), Buf(nm))
        A2 = [nf("A0"), nf("A1")]; B, Bb = nf("B"); X1, X1b = nf("X1"); X2, X2b = nf("X2"); Fb, Fbb = nf("F")
        G3 = [(self.bf16(W), Buf("G%d" % i)) for i in range(3)]
        E2 = [nf("E0"), nf("E1")]
        qt = self.bf16(W); qtb = Buf("qt")
        kt = self.bf16(W); ktb = Buf("kt")
        qh2 = [(self.bf16(P), Buf("qh0")), (self.bf16(P), Buf("qh1"))]
        osq = self.bf16(W); osqb = Buf("osq")
        rs = self.f32(W); rsb = Buf("rs")
        lb = self.f32(NDC); lbb = Buf("lb")
        oml = self.f32(NDC)
        ex = self.f32(4 * NDC)
        Sin = self.f32(NS, 128); Sinb = Buf("Sin")
        Sout = self.f32(NS, 128); Soutb = Buf("Sout")
        NR = 6
        Sf = [self.f32(128) for _ in range(NR)]; Sfb = [Buf("Sf%d" % i) for i in range(NR)]
        Sb = [self.bf16(128) for _ in range(NR)]; Sbb = [Buf("Sb%d" % i) for i in range(NR)]
        k2c = [self.f32(CH) for _ in range(4)]; k2b = [Buf("k2c%d" % i) for i in range(4)]
        kiT = [self.bf16(256) for _ in range(4)]; kib = [Buf("kiT%d" % i) for i in range(4)]
        PTm = [self.bf16(CH) for _ in range(4)]; ptb = [Buf("PT%d" % i) for i in range(4)]
        Send2 = [(self.f32(128), Buf("Se0")), (self.f32(128), Buf("Se1"))]
        Srv2 = [(self.f32(128), Buf("Srv0")), (self.f32(128), Buf("Srv1"))]
        Srb = self.bf16(128); Srbb = Buf("Srb")
        Pc2 = [(self.f32(P // CH + 1), Buf("Pc0")), (self.f32(P // CH + 1), Buf("Pc1"))]
        Sfin = self.f32(128); Sfinb = Buf("Sfin")
        self.use_scr = False
        cl = self.vecs[:, self.voff["c_lb"]:self.voff["c_lb"] + 4 * NDC]
        self.act(ex[:, 0:4 * NDC], cl, AF.Exp, [self.cbuf], [lbb])
        exv = ex[:, 0:4 * NDC].rearrange("p (l c) -> p l c", c=NDC)
        self.tt("dve", lb[:, 0:NDC], exv[:, 1, :], exv[:, 2, :], ALU.add, [lbb], [lbb])
        self.tt("dve", oml[:, 0:NDC], exv[:, 0, :], exv[:, 3, :], ALU.add, [lbb], [lbb])
        self.tt("dve", ex[:, 0:NDC], lb[:, 0:NDC], oml[:, 0:NDC], ALU.add, [lbb], [lbb])
        self.S.op("dve", lambda e: e.reciprocal(ex[:, 0:NDC], ex[:, 0:NDC]), [lbb], [lbb])
        self.tt("dve", lb[:, 0:NDC], lb[:, 0:NDC], ex[:, 0:NDC], ALU.mult, [lbb], [lbb])
        self.tt("dve", oml[:, 0:NDC], oml[:, 0:NDC], ex[:, 0:NDC], ALU.mult, [lbb], [lbb])
        own = self.tiles(0)
        NCH = P // CH

        def finalize(hd):
            par = hd % 2
            E, Eb = E2[par]; G, Gb = G3[hd % 3]; qh, qhb = qh2[par]
            Srv, Srvb = Srv2[par]; Send, Sendb = Send2[par]; Pc, Pcb = Pc2[par]
            self.ts("dve", Srv, Srv, self.vcol("isodd", 0), None, ALU.mult, None, [Srvb, self.cbuf], [Srvb])
            self.copy("act", Srb, Srv, [Srvb], [Srbb])
            self.stt("dve", Sfin, Srv, Pc[:, NCH:NCH + 1], Send, ALU.mult, ALU.add, [Srvb, Pcb, Sendb], [Sfinb])
            self.dma("sp", ncp[0][hd], Sfin, [Sfinb], [])
            for (c0, w) in own:
                pp, sp = self.split(c0, w)
                if pp:
                    a = pp[0] - HO
                    ps, pb = self.ps()
                    self.mm(ps[:, 0:pp[1]], Srb, qh[:, a:a + pp[1]], True, True, [Srbb, qhb], [pb])
                    self.tt("dve", E[:, a:a + pp[1]], E[:, a:a + pp[1]], ps[:, 0:pp[1]], ALU.add, [pb, Eb], [Eb])
            self.act(osq[:, 0:W], E[:, 0:W], AF.Square, [Eb], [osqb])
            for (c0, w) in own:
                ps, pb = self.ps()
                self.mm(ps[:, 0:w], self.ones_bf, osq[:, c0 - HO:c0 - HO + w], True, True, [osqb, self.cbuf], [pb])
                self.act(rs[:, c0 - HO:c0 - HO + w], ps[:, 0:w], AF.Sqrt, [pb], [rsb], bias=EPS, scale=1.0 / 128)
            self.S.op("dve", lambda e: e.reciprocal(rs[:, 0:W], rs[:, 0:W]), [rsb], [rsb])
            self.tt("dve", E[:, 0:W], E[:, 0:W], rs[:, 0:W], ALU.mult, [Eb, rsb], [Eb])
            self.stt("dve", oN[:, hd, HO:HO + W], E[:, 0:W], self.vcol("c_g_norm", hd), G[:, 0:W], ALU.mult, ALU.mult,
                     [Eb, Gb, self.cbuf], [onb[hd]])

        ri = 0

        def make_proj_items(hd):
            par = hd % 2
            Ai, Aib = A2[par]; G, Gb = G3[hd % 3]
            s0, s0b = self.wget(("cin0", hd))
            s1, s1b = self.wget(("cin1", hd))
            v0 = s0.rearrange("p (a b) -> p a b", b=256)
            v1 = s1.rearrange("p (a b) -> p a b", b=256)
            items = []

            def item(sv, sbuf, k, fn, c0, w):
                def run():
                    ps, pb = self.ps()
                    for dc in range(NDC):
                        self.mm(ps[:, 0:w], sv[:, dc, k * 128:(k + 1) * 128], self.hT[:, dc, c0:c0 + w], dc == 0, dc == NDC - 1,
                                [sbuf, self.hb[dc]], [pb])
                    fn(ps[:, 0:w], pb, c0 - HO, w)
                return run
            specs = [
                (v0, s0b, 1, lambda ps, pb, a, w: self.act(X1[:, a:a + w], ps, AF.Sigmoid, [pb], [X1b])),
                (v0, s0b, 0, lambda ps, pb, a, w: self.act(X2[:, a:a + w], ps, AF.Silu, [pb], [X2b])),
                (v1, s1b, 0, lambda ps, pb, a, w: self.copy("act", Ai[:, a:a + w], ps, [pb], [Aib])),
                (v1, s1b, 1, lambda ps, pb, a, w: self.act(G[:, a:a + w], ps, AF.Silu, [pb], [Gb])),
            ]
            for (sv, sbuf, k, fn) in specs:
                for (c0, w) in own:
                    items.append(item(sv, sbuf, k, fn, c0, w))
            return items

        def prep(hd):
            par = hd % 2
            T, Tb = E2[par]
            Pc, Pcb = Pc2[par]
            self.ts("dve", X1[:, 0:W], X1[:, 0:W], oml[:, hd:hd + 1], lb[:, hd:hd + 1], ALU.mult, ALU.add, [X1b, lbb], [X1b])
            self.ts("dve", B[:, 0:W], X1[:, 0:W], -1.0, 1.0, ALU.mult, ALU.add, [X1b], [Bb])
            self.tt("dve", T[:, 0:W], X1[:, 0:W], self.m0, ALU.mult, [X1b, self.cbuf], [Tb])
            self.tt("dve", X1[:, 0:W], X1[:, 0:W], T[:, 0:W], ALU.subtract, [X1b, Tb], [X1b])
            self.S.op("dve", lambda e: e.tensor_tensor_scan(Fb[:, 0:W], T[:, 0:W], X1[:, 0:W], 0.0, ALU.mult, ALU.add), [X1b, Tb], [Fbb])
            self.ts("dve", T[:, 0:W], Fb[:, 0:W], 1e-36, None, ALU.max, None, [Fbb], [Tb])
            self.S.op("dve", lambda e: e.reciprocal(T[:, 0:W], T[:, 0:W]), [Tb], [Tb])
            self.tt("dve", B[:, 0:W], B[:, 0:W], T[:, 0:W], ALU.mult, [Tb, Bb], [Bb])
            self.copy("act", kt[:, 0:W], B[:, 0:W], [Bb], [ktb])
            self.tt("dve", qt[:, 0:W], X2[:, 0:W], Fb[:, 0:W], ALU.mult, [X2b, Fbb], [qtb])
            Fe = Fb[:, 0:P].rearrange("p (c t) -> p c t", t=CH)[:, :, CH - 1]
            self.S.op("dve", lambda e, Pc=Pc: e.memset(Pc[:, 0:1], 1.0), [], [Pcb])
            self.S.op("dve", lambda e, Fe=Fe, Pc=Pc: e.tensor_tensor_scan(Pc[:, 1:NCH + 1], Fe, self.zero[:, 0:NCH], 1.0, ALU.mult, ALU.add),
                      [Fbb, self.cbuf], [Pcb])

        for it in make_proj_items(0):
            it()
        prep(0)
        for hd in range(H):
            par = hd % 2
            E, Eb = E2[par]; G, Gb = G3[hd % 3]; qh, qhb = qh2[par]; A, Ab = A2[par]
            Srv, Srvb = Srv2[par]; Send, Sendb = Send2[par]; Pc, Pcb = Pc2[par]
            pending = make_proj_items(hd + 1) if hd + 1 < H else []
            self.dma("sp", Sin[:, 0:NS, :], stc[0][:, hd, :, :].rearrange("s k v -> k s v"), [], [Sinb])
            psteps = [("p", ci, ci * CH, CH) for ci in range(NCH)]
            ssteps = [("s", s, P + s * TS, TS) for s in range(NS)]
            steps = []
            for i_ in range(max(len(psteps), len(ssteps))):
                if i_ < len(psteps):
                    steps.append(psteps[i_])
                if i_ < len(ssteps):
                    steps.append(ssteps[i_])
            st = {"cur": None}
            ctx = []

            def S1(t):
                kind, ci, a, C = steps[t]
                d_ = {"i4": t % 4}
                i4 = d_["i4"]
                fend = Fb[:, a + C - 1:a + C]
                self.ts("dve", k2c[i4][:, 0:C], B[:, a:a + C], fend, None, ALU.mult, None, [Bb, Fbb], [k2b[i4]])
                pT, pTb = self.ps()
                self.tr(pT[0:C, 0:128], k2c[i4][:, 0:C], 128, [k2b[i4]], [pTb])
                self.tr(pT[0:C, 128:256], A[:, a:a + C], 128, [Ab], [pTb])
                pS, pSb = self.ps()
                self.mm(pS[0:C, 0:C], kt[:, a:a + C], qt[:, a:a + C], True, True, [ktb, qtb], [pSb])
                d_.update(pT=pT, pTb=pTb, pS=pS, pSb=pSb, fend=fend)
                ctx.append(d_)

            def S2(t):
                kind, ci, a, C = steps[t]
                d_ = ctx[t]
                i4 = d_["i4"]
                self.copy("act", kiT[i4][0:C, 0:256], d_["pT"][0:C, 0:256], [d_["pTb"]], [kib[i4]])
                self.tt("dve", PTm[i4][0:C, 0:C], d_["pS"][0:C, 0:C], self.tri[0:C, 0:C], ALU.mult, [d_["pSb"], self.cbuf], [ptb[i4]])
                if kind == "p":
                    self.ts("dve", qh[:, a:a + C], qt[:, a:a + C], Pc[:, ci:ci + 1], None, ALU.mult, None, [qtb, Pcb], [qhb])
                pD, pDb = self.ps()
                self.mm(pD[:, 0:128], kiT[i4][0:C, 0:128], kiT[i4][0:C, 128:256], True, True, [kib[i4]], [pDb])
                d_.update(pD=pD, pDb=pDb)

            def S3(t):
                nonlocal ri
                kind, ci, a, C = steps[t]
                d_ = ctx[t]
                i4 = d_["i4"]
                fend, pD, pDb = d_["fend"], d_["pD"], d_["pDb"]
                cur = st["cur"]
                if kind == "p":
                    have_S = cur is not None
                    if have_S:
                        sf_in, sfb_in, sb_in, sbb_in = cur
                    r = ri % NR
                    ri += 1
                    nf_, nfb_, nb_, nbb_ = Sf[r], Sfb[r], Sb[r], Sbb[r]
                    if have_S:
                        self.stt("dve", nf_, sf_in, fend, pD[:, 0:128], ALU.mult, ALU.add, [sfb_in, Fbb, pDb], [nfb_])
                    else:
                        self.copy("dve", nf_, pD[:, 0:128], [pDb], [nfb_])
                    self.copy("act", nb_, nf_, [nfb_], [nbb_])
                    st["cur"] = (nf_, nfb_, nb_, nbb_)
                else:
                    have_S = True
                    sf_in, sfb_in = Sin[:, ci, :], Sinb
                    r = ri % NR
                    ri += 1
                    sb_in, sbb_in = Sb[r], Sbb[r]
                    self.copy("act", sb_in, sf_in, [sfb_in], [sbb_in])
                    self.stt("dve", Sout[:, ci, :], sf_in, fend, pD[:, 0:128], ALU.mult, ALU.add, [sfb_in, Fbb, pDb], [Soutb])
                pO, pOb = self.ps()
                self.mm(pO[:, 0:C], kiT[i4][0:C, 128:256], PTm[i4][0:C, 0:C], True, not have_S, [kib[i4], ptb[i4]], [pOb])
                if have_S:
                    self.mm(pO[:, 0:C], sb_in, qt[:, a:a + C], False, True, [sbb_in, qtb], [pOb])
                self.copy("act", E[:, a:a + C], pO[:, 0:C], [pOb], [Eb])

            nst = len(steps)
            every = 10 ** 9
            for t in range(nst + 2):
                if t < nst:
                    S1(t)
                if 1 <= t <= nst:
                    S2(t - 1)
                if t >= 2:
                    S3(t - 2)
            cur = st["cur"]
            self.dma("sp", ncs[0][:, hd, :, :].rearrange("s k v -> k s v"), Sout[:, 0:NS, :], [Soutb], [])
            if hd >= 1:
                finalize(hd - 1)
            self.copy("dve", Send, cur[0], [cur[1]], [Sendb])
            k = self.ncc_tensors
            self.ncc_tensors += 1
            snd = nc.dram_tensor("ss%d" % k, [128, 128], F32)
            rcv = nc.dram_tensor("sr%d" % k, [256, 128], F32)
            sdb, rvb = Buf("ssnd"), Buf("srcv")
            self.dma("pool", snd.ap(), Send, [Sendb], [sdb])
            self.S.cc(lambda e, snd=snd, rcv=rcv: e.collective_compute("AllGather", ALU.bypass, replica_groups=PAIRS, ins=[snd.ap()], outs=[rcv.ap()]),
                      [sdb], [rvb])
            self.dma("sp", Srv, rcv.ap()[0:128, :], [rvb], [Srvb])
            if hd + 1 < H:
                half = len(pending) // 2
                for it in pending[:half]:
                    it()
                prep(hd + 1)
                for it in pending[half:]:
                    it()
        finalize(H - 1)
        self.restore_x()
        self.proj_residual("wo", None, oN, onb)

    def final_out(self):
        c = self.cfg
        NDC, P, PT, W, D = c["NDC"], c["P"], c["PT"], c["W"], c["D"]
        self.sumsq_to_rstd([(self.xT[:, dc, :], self.xb[dc]) for dc in range(NDC)], W, D, self.rstd, self.rstd_b)
        dsts = [(self.d["yp"].ap()[t * 128:(t + 1) * 128, :], 128, t * 128) for t in range(P // 128)]
        dsts.append((self.d["ys"].ap()[:, :], PT, P))
        stg = [self.f32(D) for _ in range(2)]
        sb = [Buf("ystg0"), Buf("ystg1")]
        tmp = [self.f32(128) for _ in range(8)]
        tmb = [Buf("yt%d" % i) for i in range(8)]
        ti = 0
        for i, (dst, n, col) in enumerate(dsts):
            s, b = stg[i % 2], sb[i % 2]
            for q in range(NDC // 4):
                ps, pb = self.ps()
                for k in range(4):
                    dc = q * 4 + k
                    t, tb = tmp[ti % 8], tmb[ti % 8]
                    ti += 1
                    self.stt("dve", t[:, 0:n], self.xT[:, dc, col:col + n], self.vcol("norm_final", dc), self.rstd[:, col:col + n],
                             ALU.mult, ALU.mult, [self.xb[dc], self.rstd_b, self.cbuf], [tb])
                    self.tr(ps[0:n, k * 128:(k + 1) * 128], t[:, 0:n], 128, [tb], [pb])
                self.copy("act", s[0:n, q * 512:(q + 1) * 512], ps[0:n, 0:512], [pb], [b])
            self.dma("sp", dst, s[0:n, :], [b], [])


_CACHE = {}


def run(cfg, inp):
    D, P, NS, NDC, DFF = cfg["D"], cfg["P"], cfg["NS"], cfg["NDC"], cfg["DFF"]
    inp = {k: np.asarray(v) for k, v in inp.items()}
    wslab, sidx = build_slabs(cfg, inp)
    consts = build_consts(cfg)
    key = tuple(sorted(cfg.items()))
    if key not in _CACHE:
        _CACHE[key] = Builder(cfg, sidx, wslab.shape[0]).build()
    nc = _CACHE[key]
    B = inp["x_prompt"].shape[0]
    assert 2 * B == NCORES and inp["x_prompt"].shape[1] == 2 * P and inp["x_sample"].shape[0] == NCORES * NS
    in_maps = []
    for c in range(NCORES):
        sq, hf = c // 2, c % 2
        sl = slice(c * NS, (c + 1) * NS)
        in_maps.append({
            "xp": np.ascontiguousarray(inp["x_prompt"][sq, hf * P:(hf + 1) * P]),
            "xs": np.ascontiguousarray(inp["x_sample"][sl]).reshape(NS * TS, D),
            "st_a": np.ascontiguousarray(inp["state_conv_a"][:, sl]),
            "st_b": np.ascontiguousarray(inp["state_pool"][:, sl]),
            "st_c": np.ascontiguousarray(inp["state_hgrn"][:, sl]),
            "st_f": np.ascontiguousarray(inp["state_ffn_conv"][:, sl]),
            "wslab": wslab, "vecs": build_vecs(cfg, inp, c), "consts": consts,
        })
    res = run_bass_kernel_spmd(nc, in_maps, core_ids=list(range(NCORES)))
    R = res.results
    f = np.float32
    y_p = np.zeros((B, 2 * P, D), f); y_s = np.zeros((NCORES * NS, TS, D), f)
    na_p = np.zeros((2, B, HO, D), f); na_s = np.zeros((2, NCORES * NS, HO, D), f)
    nb_p = np.zeros((1, B, PH, D), f); nb_s = np.zeros((1, NCORES * NS, PH, D), f)
    nc_p = np.zeros((1, B, NDC, 128, 128), f); nc_s = np.zeros((1, NCORES * NS, NDC, 128, 128), f)
    nf_p = np.zeros((4, B, 2, 2 * DFF), f); nf_s = np.zeros((4, NCORES * NS, 2, 2 * DFF), f)
    for c in range(NCORES):
        sq, hf = c // 2, c % 2
        sl = slice(c * NS, (c + 1) * NS)
        r = R[c]
        y_p[sq, hf * P:(hf + 1) * P] = r["yp"]
        y_s[sl] = r["ys"].reshape(NS, TS, D)
        na_s[:, sl] = r["na_s"]; nb_s[:, sl] = r["nb_s"]; nc_s[:, sl] = r["nc_s"]; nf_s[:, sl] = r["nf_s"]
        if hf == 1:
            na_p[:, sq] = r["na_p"]; nb_p[:, sq] = r["nb_p"]; nc_p[:, sq] = r["nc_p"]; nf_p[:, sq] = r["nf_p"]
    return (y_p, y_s, na_p, na_s, nb_p, nb_s, nc_p, nc_s, nf_p, nf_s)


def kernel(**inputs):
    return run(make_cfg(), inputs)
```
